# Optimizing a Trainium2 kernel written in Bass

```python
import jax, jax.numpy as jnp
from jax import lax
import numpy as np

D_MODEL = 1024
BATCH = 32
SEQ = 2048
DEPTH = 2
DEC_BATCH = 8
DEC_SEQ = 8192
PAST_LEN = 128

GRID_W = 64
N_GROUPS = 4
GROUP_W = D_MODEL // N_GROUPS
GLA_HEADS = 4
GLA_DK = GROUP_W // 2 // GLA_HEADS
GLA_DV = GROUP_W // GLA_HEADS
GLA_GATE_RANK = 16
GLA_GATE_NORM = 16.0
GLA_CHUNK = 64
POOL_WINDOWS = (2, 4, 8, 16)
POOL_GW = GROUP_W // len(POOL_WINDOWS)
NA_HEADS = 4
NA_DH = GROUP_W // NA_HEADS
NA_WIN_R = 8
NA_WIN_C = 16
NA_COL_BLOCK = 16
NA_KEY_COLS = 32
MLA_HEADS = 4
MLA_NOPE = 64
MLA_ROPE = 32
MLA_V = GROUP_W // MLA_HEADS
MLA_Q_RANK = 256
MLA_KV_RANK = 128
MLA_SCALE = (MLA_NOPE + MLA_ROPE) ** -0.5
ROPE_THETA = 10000.0
Q_BLOCK = 128
D_FF = 2816
N_MOD = 9
EPS = 1e-6
NEG_INF = -1e30
F32 = jnp.float32

PROJ_SIZES = (
    GLA_HEADS * GLA_DK, GLA_HEADS * GLA_DK, GLA_HEADS * GLA_DV, GLA_HEADS * GLA_DV,
    GLA_GATE_RANK, GLA_GATE_RANK,
    GROUP_W,
    NA_HEADS * NA_DH, NA_HEADS * NA_DH, NA_HEADS * NA_DH,
    MLA_Q_RANK, MLA_KV_RANK, MLA_ROPE,
)
PROJ_W = sum(PROJ_SIZES)

kernel_name = "hybrid_bidir_encoder_hymba4"


def rmsnorm(x, w):
    xf = x.astype(F32)
    y = xf * lax.rsqrt(jnp.mean(xf * xf, axis=-1, keepdims=True) + EPS)
    return (y * w.astype(F32)).astype(x.dtype)


def swiglu(h, wg, wu, wd):
    return (jax.nn.silu(h @ wg) * (h @ wu)) @ wd


def gla_chunked(q, k, v, g, inclusive):
    B, H, S, DK = q.shape
    DV = v.shape[-1]
    C = GLA_CHUNK
    N = S // C
    q = q.astype(F32).reshape(B, H, N, C, DK)
    k = k.astype(F32).reshape(B, H, N, C, DK)
    v = v.astype(F32).reshape(B, H, N, C, DV)
    b = jnp.cumsum(g.astype(F32).reshape(B, H, N, C, DK), axis=3)
    q_dec = q * jnp.exp(b)
    attn = jnp.einsum('bhnid,bhnjd->bhnij', q_dec, k * jnp.exp(-b))
    mask = np.tril(np.ones((C, C), dtype=bool), 0 if inclusive else -1)
    attn = jnp.where(mask, attn, 0.0)
    o = jnp.einsum('bhnij,bhnjv->bhniv', attn, v)
    b_last = b[:, :, :, -1:, :]
    kv_chunk = jnp.einsum('bhncd,bhncv->nbhdv', k * jnp.exp(b_last - b), v)
    dec_chunk = jnp.moveaxis(jnp.exp(b_last[:, :, :, 0, :]), 2, 0)

    def step(state, inp):
        kv_n, dec_n = inp
        return dec_n[..., None] * state + kv_n, state

    _, prev = lax.scan(step, jnp.zeros((B, H, DK, DV), F32), (kv_chunk, dec_chunk))
    o = o + jnp.einsum('bhncd,nbhdv->bhncv', q_dec, prev)
    return o.reshape(B, H, S, DV)


def gla_mixer(q, k, v, g_out, lr_f, lr_b, w_f, b_f, w_b, b_b, w_norm):
    B, S, _ = q.shape

    def heads(t, d):
        return t.reshape(B, S, GLA_HEADS, d).transpose(0, 2, 1, 3)

    def log_decay(lr, w, bias):
        z = (lr @ w + bias).astype(F32)
        return heads(jax.nn.log_sigmoid(z) / GLA_GATE_NORM, GLA_DK)

    qh = heads(q, GLA_DK) * (GLA_DK ** -0.5)
    kh = heads(k, GLA_DK)
    vh = heads(v, GLA_DV)
    o_fwd = gla_chunked(qh, kh, vh, log_decay(lr_f, w_f, b_f), True)
    flip = lambda t: jnp.flip(t, axis=2)
    o_bwd = flip(gla_chunked(flip(qh), flip(kh), flip(vh), flip(log_decay(lr_b, w_b, b_b)), False))
    o = rmsnorm((o_fwd + o_bwd).transpose(0, 2, 1, 3), w_norm).reshape(B, S, GLA_HEADS * GLA_DV)
    return (o * jax.nn.silu(g_out.astype(F32))).astype(q.dtype)


def pool_mixer(u, w_pool, scale):
    B, S, _ = u.shape
    uf = u.astype(F32)
    cs = jnp.pad(jnp.cumsum(uf, axis=1), ((0, 0), (1, 0), (0, 0)))
    t = jnp.arange(S)
    outs = []
    for gi, w in enumerate(POOL_WINDOWS):
        lo = jnp.clip(t - w // 2, 0, S)
        hi = jnp.clip(t + w // 2, 0, S)
        sl = slice(gi * POOL_GW, (gi + 1) * POOL_GW)
        csg = cs[:, :, sl]
        mean = (jnp.take(csg, hi, axis=1) - jnp.take(csg, lo, axis=1)) / (hi - lo).astype(F32)[None, :, None]
        outs.append(jnp.einsum('bsc,cd->bsd', mean - uf[:, :, sl], w_pool[gi].astype(F32)))
    return (jnp.concatenate(outs, axis=-1) * scale.astype(F32)).astype(u.dtype)


def na_mixer(q, k, v, rpb):
    B, S, _ = q.shape
    rows = S // GRID_W
    win_r = min(NA_WIN_R, rows)
    n_cb = GRID_W // NA_COL_BLOCK
    shp = (B, rows, GRID_W, NA_HEADS, NA_DH)
    qg = q.reshape(shp) * (NA_DH ** -0.5)
    kg = k.reshape(shp)
    vg = v.reshape(shp)
    qcol = np.arange(GRID_W).reshape(n_cb, NA_COL_BLOCK)
    kstart = np.clip(np.arange(n_cb) * NA_COL_BLOCK - NA_WIN_C // 2, 0, GRID_W - NA_KEY_COLS)
    kcol = kstart[:, None] + np.arange(NA_KEY_COLS)[None, :]
    qstart = np.clip(qcol - NA_WIN_C // 2, 0, GRID_W - NA_WIN_C)
    col_ok = (kcol[:, None, :] >= qstart[:, :, None]) & (kcol[:, None, :] < qstart[:, :, None] + NA_WIN_C)
    dcol_idx = np.clip(kcol[:, None, :] - qcol[:, :, None] + NA_WIN_C - 1, 0, 2 * NA_WIN_C - 2)
    col_ok = jnp.asarray(col_ok)[:, :, None, :]
    dcol_idx = jnp.asarray(dcol_idx)[:, :, None, :]
    kb_all = kg[:, :, kcol]
    vb_all = vg[:, :, kcol]

    def row_fn(r):
        r0 = jnp.clip(r - win_r // 2, 0, rows - win_r)
        kb = lax.dynamic_slice_in_dim(kb_all, r0, win_r, axis=1)
        vb = lax.dynamic_slice_in_dim(vb_all, r0, win_r, axis=1)
        qr = lax.dynamic_index_in_dim(qg, r, axis=1, keepdims=False)
        qr = qr.reshape(B, n_cb, NA_COL_BLOCK, NA_HEADS, NA_DH)
        s = jnp.einsum('bnqhd,brnkhd->bhnqrk', qr, kb).astype(F32)
        drow_idx = (r0 + jnp.arange(win_r) - r + NA_WIN_R - 1)[None, None, :, None]
        bias = rpb[:, drow_idx, dcol_idx].astype(F32)
        s = jnp.where(col_ok, s + bias, NEG_INF)
        p = jax.nn.softmax(s, axis=(-2, -1)).astype(vb.dtype)
        o = jnp.einsum('bhnqrk,brnkhd->bnqhd', p, vb)
        return o.reshape(B, GRID_W, NA_HEADS * NA_DH)

    out = lax.map(row_fn, jnp.arange(rows))
    return jnp.moveaxis(out, 0, 1).reshape(B, S, NA_HEADS * NA_DH)


def rope(x, pos):
    half = MLA_ROPE // 2
    inv = ROPE_THETA ** (-jnp.arange(half, dtype=F32) / half)
    ang = pos[:, None] * inv[None, :]
    cos = jnp.cos(ang)[None, :, None, :]
    sin = jnp.sin(ang)[None, :, None, :]
    xf = x.astype(F32)
    x1, x2 = xf[..., :half], xf[..., half:]
    return jnp.concatenate([x1 * cos - x2 * sin, x1 * sin + x2 * cos], axis=-1).astype(x.dtype)


def mla_mixer(c_q, c_kv, k_pe, w_qnorm, w_uq, w_kvnorm, w_ukv):
    B, S, _ = c_q.shape
    H = MLA_HEADS
    pos = jnp.arange(S, dtype=F32)
    q = (rmsnorm(c_q, w_qnorm) @ w_uq).reshape(B, S, H, MLA_NOPE + MLA_ROPE)
    kv = (rmsnorm(c_kv, w_kvnorm) @ w_ukv).reshape(B, S, H, MLA_NOPE + MLA_V)
    q = jnp.concatenate([q[..., :MLA_NOPE], rope(q[..., MLA_NOPE:], pos)], axis=-1) * MLA_SCALE
    k_rot = jnp.broadcast_to(rope(k_pe[:, :, None, :], pos), (B, S, H, MLA_ROPE))
    k = jnp.concatenate([kv[..., :MLA_NOPE], k_rot], axis=-1)
    v = kv[..., MLA_NOPE:]
    nb = S // Q_BLOCK
    qb = jnp.moveaxis(q.reshape(B, nb, Q_BLOCK, H, MLA_NOPE + MLA_ROPE), 1, 0)

    def block(qi):
        s = jnp.einsum('bqhd,bkhd->bhqk', qi, k).astype(F32)
        p = jax.nn.softmax(s, axis=-1).astype(v.dtype)
        return jnp.einsum('bhqk,bkhd->bqhd', p, v)

    o = lax.map(block, qb)
    return jnp.moveaxis(o, 0, 1).reshape(B, S, H * MLA_V)


def token_mixing(h, w_in, w_out, gla_wgk_f, gla_bgk_f, gla_wgk_b, gla_bgk_b, gla_norm,
                 pool_w, pool_scale, na_rpb, mla_qnorm, mla_wuq, mla_kvnorm, mla_wukv):
    proj = h @ w_in
    cuts = []
    acc = 0
    for size in PROJ_SIZES[:-1]:
        acc += size
        cuts.append(acc)
    (a_q, a_k, a_v, a_g, a_lrf, a_lrb, b_u, c_q, c_k, c_v, d_cq, d_ckv, d_kpe) = jnp.split(proj, cuts, axis=-1)
    y_a = gla_mixer(a_q, a_k, a_v, a_g, a_lrf, a_lrb, gla_wgk_f, gla_bgk_f, gla_wgk_b, gla_bgk_b, gla_norm)
    y_b = pool_mixer(b_u, pool_w, pool_scale)
    y_c = na_mixer(c_q, c_k, c_v, na_rpb)
    y_d = mla_mixer(d_cq, d_ckv, d_kpe, mla_qnorm, mla_wuq, mla_kvnorm, mla_wukv)
    y = jnp.concatenate([y_a.astype(h.dtype), y_b.astype(h.dtype), y_c.astype(h.dtype), y_d.astype(h.dtype)], axis=-1)
    return y @ w_out


def encoder_layer(x, c, ada_w, ada_b, norm_ffn1, ffn1_wg, ffn1_wu, ffn1_wd, norm_mix, w_in,
                  gla_wgk_f, gla_bgk_f, gla_wgk_b, gla_bgk_b, gla_norm, pool_w, pool_scale, na_rpb,
                  mla_qnorm, mla_wuq, mla_kvnorm, mla_wukv, w_out, norm_ffn2, ffn2_wg, ffn2_wu, ffn2_wd):
    B = x.shape[0]
    mod = (jax.nn.silu(c) @ ada_w + ada_b).reshape(B, 1, N_MOD, D_MODEL)
    m = [mod[:, :, i, :] for i in range(N_MOD)]

    def modnorm(t, w, shift, scale):
        return rmsnorm(t, w) * (1 + scale) + shift

    x = x + 0.5 * (1 + m[2]) * swiglu(modnorm(x, norm_ffn1, m[0], m[1]), ffn1_wg, ffn1_wu, ffn1_wd)
    h = modnorm(x, norm_mix, m[3], m[4])
    x = x + (1 + m[5]) * token_mixing(h, w_in, w_out, gla_wgk_f, gla_bgk_f, gla_wgk_b, gla_bgk_b, gla_norm,
                                      pool_w, pool_scale, na_rpb, mla_qnorm, mla_wuq, mla_kvnorm, mla_wukv)
    x = x + 0.5 * (1 + m[8]) * swiglu(modnorm(x, norm_ffn2, m[6], m[7]), ffn2_wg, ffn2_wu, ffn2_wd)
    return x


def setup_inputs(seed: int = 0) -> dict:
    key = jax.random.key(seed)
    ks = iter(jax.random.split(key, 40))
    L, D, F = DEPTH, D_MODEL, D_FF

    def nrm(shape, s):
        return jax.random.normal(next(ks), shape, F32) * s

    def gain(shape):
        return 1.0 + nrm(shape, 0.05)

    return {
        "x_prompt": nrm((BATCH, SEQ, D), 1.0),
        "x_sample": nrm((DEC_BATCH, DEC_SEQ, D), 1.0),
        "c_prompt": nrm((BATCH, D), 1.0),
        "c_sample": nrm((DEC_BATCH, D), 1.0),
        "ada_w": nrm((L, D, N_MOD * D), 0.2 * D ** -0.5),
        "ada_b": nrm((L, N_MOD * D), 0.01),
        "norm_ffn1": gain((L, D)),
        "ffn1_wg": nrm((L, D, F), D ** -0.5),
        "ffn1_wu": nrm((L, D, F), D ** -0.5),
        "ffn1_wd": nrm((L, F, D), F ** -0.5),
        "norm_mix": gain((L, D)),
        "w_in": nrm((L, D, PROJ_W), D ** -0.5),
        "gla_wgk_f": nrm((L, GLA_GATE_RANK, GLA_HEADS * GLA_DK), GLA_GATE_RANK ** -0.5),
        "gla_bgk_f": nrm((L, GLA_HEADS * GLA_DK), 0.1),
        "gla_wgk_b": nrm((L, GLA_GATE_RANK, GLA_HEADS * GLA_DK), GLA_GATE_RANK ** -0.5),
        "gla_bgk_b": nrm((L, GLA_HEADS * GLA_DK), 0.1),
        "gla_norm": gain((L, GLA_DV)),
        "pool_w": nrm((L, len(POOL_WINDOWS), POOL_GW, POOL_GW), POOL_GW ** -0.5),
        "pool_scale": gain((L, GROUP_W)),
        "na_rpb": nrm((L, NA_HEADS, 2 * NA_WIN_R - 1, 2 * NA_WIN_C - 1), 0.1),
        "mla_qnorm": gain((L, MLA_Q_RANK)),
        "mla_wuq": nrm((L, MLA_Q_RANK, MLA_HEADS * (MLA_NOPE + MLA_ROPE)), MLA_Q_RANK ** -0.5),
        "mla_kvnorm": gain((L, MLA_KV_RANK)),
        "mla_wukv": nrm((L, MLA_KV_RANK, MLA_HEADS * (MLA_NOPE + MLA_V)), MLA_KV_RANK ** -0.5),
        "w_out": nrm((L, D, D), D ** -0.5),
        "norm_ffn2": gain((L, D)),
        "ffn2_wg": nrm((L, D, F), D ** -0.5),
        "ffn2_wu": nrm((L, D, F), D ** -0.5),
        "ffn2_wd": nrm((L, F, D), F ** -0.5),
        "final_norm": gain((D,)),
    }


def reference(x_prompt, x_sample, c_prompt, c_sample, ada_w, ada_b, norm_ffn1, ffn1_wg, ffn1_wu, ffn1_wd,
              norm_mix, w_in, gla_wgk_f, gla_bgk_f, gla_wgk_b, gla_bgk_b, gla_norm, pool_w, pool_scale,
              na_rpb, mla_qnorm, mla_wuq, mla_kvnorm, mla_wukv, w_out, norm_ffn2, ffn2_wg, ffn2_wu,
              ffn2_wd, final_norm):
    def run(x, c):
        for l in range(DEPTH):
            x = encoder_layer(x, c, ada_w[l], ada_b[l], norm_ffn1[l], ffn1_wg[l], ffn1_wu[l], ffn1_wd[l],
                              norm_mix[l], w_in[l], gla_wgk_f[l], gla_bgk_f[l], gla_wgk_b[l], gla_bgk_b[l],
                              gla_norm[l], pool_w[l], pool_scale[l], na_rpb[l], mla_qnorm[l], mla_wuq[l],
                              mla_kvnorm[l], mla_wukv[l], w_out[l], norm_ffn2[l], ffn2_wg[l], ffn2_wu[l],
                              ffn2_wd[l])
        return rmsnorm(x, final_norm)

    y_prompt = run(x_prompt, c_prompt)
    y_sample = run(x_sample, c_sample)
    return (y_prompt, y_sample)
```

```python
import numpy as np
from contextlib import ExitStack
import concourse.bass as bass
import concourse.mybir as mybir
from concourse.bass_utils import run_bass_kernel_spmd

F32 = mybir.dt.float32
BF16 = mybir.dt.bfloat16
AF = mybir.ActivationFunctionType
ALU = mybir.AluOpType

D = 1024
FF = 2816
NL = 2
PW = 2240
EPS = 1e-6
MLA_SCALE = 96.0 ** -0.5
NEG = -30000.0
ENGS = ("pe", "act", "dve", "pool", "sp")
DMAQ = ("sp", "pool", "act")


class Op:
    __slots__ = ("eng", "fn", "deps", "is_dma", "sig", "sem", "val", "idx")

    def __init__(self, eng, fn, is_dma):
        self.eng = eng
        self.fn = fn
        self.is_dma = is_dma
        self.deps = []
        self.sig = False
        self.sem = None
        self.val = 0
        self.idx = 0


class Sched:
    NPOOL = 24

    def __init__(self, nc):
        self.nc = nc
        self.ops = []
        self.lastw = {}
        self.readers = {}
        self.finals = []
        self.barrier = None
        self.bar_seen = set()
        self.last_eng = {}

    def op(self, eng, fn, R=(), W=(), dma=False, final=False):
        o = Op(eng, fn, dma)
        o.idx = len(self.ops)
        deps = {}
        for k in R:
            w = self.lastw.get(k)
            if w is not None:
                deps[w.idx] = w
        for k in W:
            w = self.lastw.get(k)
            if w is not None:
                deps[w.idx] = w
            for r in self.readers.get(k, ()):
                deps[r.idx] = r
        if self.barrier is not None and eng not in self.bar_seen:
            self.bar_seen.add(eng)
            deps[self.barrier.idx] = self.barrier
        for d in deps.values():
            if d.eng == "pe" and eng == "pe" and not d.is_dma and not dma:
                continue
            o.deps.append(d)
            d.sig = True
        for k in W:
            self.lastw[k] = o
            self.readers[k] = []
        for k in R:
            self.readers.setdefault(k, []).append(o)
        self.ops.append(o)
        self.last_eng[eng] = o
        if final:
            o.sig = True
            self.finals.append(o)
        return o

    def dma(self, eng, out, in_, R=(), W=(), final=False):
        return self.op(eng, lambda e: e.dma_start(out=out, in_=in_), R=R, W=W, dma=True, final=final)

    def phase_barrier(self, scratch_dst, scratch_src):
        o = Op("sp", lambda e: e.dma_start(out=scratch_dst, in_=scratch_src), True)
        o.idx = len(self.ops)
        seen = {}
        for p in self.ops:
            if p.is_dma:
                seen[p.idx] = p
        for e, p in self.last_eng.items():
            seen[p.idx] = p
        start = self.barrier.idx if self.barrier is not None else -1
        for p in seen.values():
            if p.idx > start:
                o.deps.append(p)
                p.sig = True
        o.sig = True
        self.ops.append(o)
        self.last_eng["sp"] = o
        self.barrier = o
        self.bar_seen = {"sp"}
        self.lastw = {}
        self.readers = {}

    def emit(self):
        nc = self.nc
        with ExitStack() as st:
            esem = {e: st.enter_context(nc.semaphore("s_" + e)) for e in ENGS}
            dsem = {e: [st.enter_context(nc.semaphore("d_%s_%d" % (e, i))) for i in range(self.NPOOL)]
                    for e in DMAQ}
            ecount = {e: 0 for e in ENGS}
            dcount = {e: [0] * self.NPOOL for e in dsem}
            dnext = {e: 0 for e in dsem}
            dprev = {e: [None] * self.NPOOL for e in dsem}
            per_eng = {e: [] for e in ENGS}
            for o in self.ops:
                per_eng[o.eng].append(o)
                if o.is_dma:
                    i = dnext[o.eng]
                    dnext[o.eng] = (i + 1) % self.NPOOL
                    prev = dprev[o.eng][i]
                    if prev is not None:
                        o.deps.append(prev)
                    dprev[o.eng][i] = o
                    dcount[o.eng][i] += 16
                    o.sem = dsem[o.eng][i]
                    o.val = dcount[o.eng][i]
                    o.sig = True
                elif o.sig:
                    ecount[o.eng] += 1
                    o.sem = esem[o.eng]
                    o.val = ecount[o.eng]
            self.nwaits = 0
            blk = st.enter_context(nc.Block())

            def run(engname, e):
                waited = {}
                for o in per_eng[engname]:
                    need = {}
                    for d in o.deps:
                        key = id(d.sem)
                        if waited.get(key, 0) >= d.val:
                            continue
                        if key not in need or need[key][1] < d.val:
                            need[key] = (d.sem, d.val)
                    for key, (sem, val) in need.items():
                        e.wait_ge(sem, val)
                        waited[key] = val
                        self.nwaits += 1
                    ins = o.fn(e)
                    if o.sig:
                        ins.then_inc(o.sem, 16 if o.is_dma else 1)
                if engname == "sp":
                    for o in self.finals:
                        key = id(o.sem)
                        if waited.get(key, 0) < o.val:
                            e.wait_ge(o.sem, o.val)
                            waited[key] = o.val

            @blk.tensor
            def _(e):
                run("pe", e)

            @blk.scalar
            def _(e):
                run("act", e)

            @blk.vector
            def _(e):
                run("dve", e)

            @blk.gpsimd
            def _(e):
                run("pool", e)

            @blk.sync
            def _(e):
                run("sp", e)


class Rot:
    def __init__(self, name, aps, keys=None):
        self.name = name
        self.aps = aps
        self.keys = keys if keys is not None else [(name, i) for i in range(len(aps))]
        self.i = -1

    def next(self):
        self.i = (self.i + 1) % len(self.aps)
        return self.aps[self.i], self.keys[self.i]

    def cur(self):
        return self.aps[self.i], self.keys[self.i]


SB_LO = 16512
SB_HI = 229312


class Builder:
    def __init__(self, nc, SEQS, dbg=False):
        self.nc = nc
        self.S = Sched(nc)
        self.SEQS = list(SEQS)
        self.NT = sum(SEQS)
        self.NS = len(SEQS)
        self.off = [int(v) for v in np.cumsum([0] + list(SEQS))]
        self.dbg = dbg
        self.sbp = SB_LO
        self.uid = 0
        self.dram = {}
        self.outs = []

    def sb(self, shape, dt, name=None):
        nbytes = int(np.prod(shape[1:])) * (4 if dt == F32 else 2)
        nbytes = (nbytes + 63) // 64 * 64
        self.uid += 1
        t = self.nc.alloc_sbuf_tensor_at("%s_%d" % (name or "t", self.uid), list(shape), dt, offset=self.sbp)
        self.sbp += nbytes
        assert self.sbp <= SB_HI, ("SBUF overflow", name, self.sbp)
        return t

    def rot(self, name, n, shape, dt):
        self.uid += 1
        return Rot("%s%d" % (name, self.uid), [self.sb(shape, dt, name) for _ in range(n)])

    def pbrot(self, idxs):
        return Rot("pb", [self.pb[i] for i in idxs], keys=["pb%d" % i for i in idxs])

    def din(self, name, shape, dt=F32):
        t = self.nc.dram_tensor(name, list(shape), dt, kind="ExternalInput").ap()
        self.dram[name] = t
        return t

    def dscr(self, name, shape, dt):
        kind = "ExternalOutput" if self.dbg else "Internal"
        t = self.nc.dram_tensor(name, list(shape), dt, kind=kind).ap()
        self.dram[name] = t
        return t

    def seq_of(self, tok):
        for s in range(self.NS):
            if self.off[s] <= tok < self.off[s + 1]:
                return s
        raise ValueError

    def mm(self, out, lhsT, rhs, start, stop, R, W, skip=False):
        if skip:
            self.S.op("pe", lambda e: e.matmul(out, lhsT, rhs, start=start, stop=stop, skip_group_check=True),
                      R=R, W=W)
        else:
            self.S.op("pe", lambda e: e.matmul(out, lhsT, rhs, start=start, stop=stop), R=R, W=W)

    def act(self, out, in_, func, R, W, bias=None, scale=None):
        kw = {}
        if bias is not None:
            kw["bias"] = bias
        if scale is not None:
            kw["scale"] = scale
        self.S.op("act", lambda e: e.activation(out=out, in_=in_, func=func, **kw), R=R, W=W)

    def tt(self, out, in0, in1, op, R, W, eng="dve"):
        self.S.op(eng, lambda e: e.tensor_tensor(out=out, in0=in0, in1=in1, op=op), R=R, W=W)

    def ts(self, out, in0, s1, op0, R, W, s2=None, op1=None, eng="dve"):
        if op1 is None:
            self.S.op(eng, lambda e: e.tensor_scalar(out=out, in0=in0, scalar1=s1, scalar2=None, op0=op0), R=R, W=W)
        else:
            self.S.op(eng, lambda e: e.tensor_scalar(out=out, in0=in0, scalar1=s1, scalar2=s2, op0=op0, op1=op1),
                      R=R, W=W)

    def stt(self, out, in0, scalar, in1, op0, op1, R, W, eng="dve"):
        self.S.op(eng, lambda e: e.scalar_tensor_tensor(out=out, in0=in0, scalar=scalar, in1=in1, op0=op0, op1=op1),
                  R=R, W=W)

    def cp(self, out, in_, R, W, eng="dve"):
        self.S.op(eng, lambda e: e.tensor_copy(out=out, in_=in_), R=R, W=W)

    def recip(self, out, in_, R, W):
        self.S.op("dve", lambda e: e.reciprocal(out=out, in_=in_), R=R, W=W)

    def memset(self, ap, val, W, eng="dve"):
        self.S.op(eng, lambda e: e.memset(ap, val), W=W)

    def dma(self, out, in_, R=(), W=(), q="sp", final=False):
        self.S.dma(q, out, in_, R=R, W=W, final=final)

    def declare(self):
        NT, NS = self.NT, self.NS
        d = self.din
        self.xT = d("xT", [D, NT])
        self.cT = d("cT", [128, 8, NS])
        self.ada_w = d("ada_w", [NL, D, 9 * D])
        self.adabT = d("adabT", [NL, 128, 72])
        self.nrm = d("nrm", [128, (NL * 3 + 1) * 8])
        self.wg = [d("ffn1_wg", [NL, D, FF]), d("ffn2_wg", [NL, D, FF])]
        self.wu = [d("ffn1_wu", [NL, D, FF]), d("ffn2_wu", [NL, D, FF])]
        self.wd = [d("ffn1_wd", [NL, FF, D]), d("ffn2_wd", [NL, FF, D])]
        self.w_in = d("w_in", [NL, D, PW])
        self.w_out = d("w_out", [NL, D, D])
        self.cst = d("cst", [128, CST_W])
        self.vec = d("vec", [128, NL * VEC_W])
        self.wgk = d("wgk", [NL, 33, 256])
        self.w_uq = d("mla_wuq", [NL, 256, 384])
        self.w_ukv = d("mla_wukv", [NL, 128, 512])
        self.rope = d("rope", [32, 2, max(self.SEQS)])
        self.poolw = d("poolw", [NL, 2, 128, 128])
        self.natab = d("natab", [NL, 4, 21, 128, 128])
        self.icnt = {}
        for Sq in sorted(set(self.SEQS)):
            self.icnt[Sq] = d("icnt%d" % Sq, [256, Sq])
        sc = self.dscr
        self.qTa = sc("qTa", [128, NT], BF16)
        self.kTa = sc("kTa", [128, NT], BF16)
        self.ktok = sc("ktok", [NT, 128], BF16)
        self.vtok = sc("vtok", [NT, 256], BF16)
        self.sptok = sc("sptok", [NT, 256], F32)
        self.goT = sc("goT", [4, 64, NT], BF16)
        self.uT = sc("uT", [256, NT], F32)
        self.qTc = sc("qTc", [4, 64, NT], BF16)
        self.kTc = sc("kTc", [4, 64, NT], BF16)
        self.vtokc = sc("vtokc", [NT, 260], BF16)
        self.qTd = sc("qTd", [4, 96, NT], BF16)
        self.kTd = sc("kTd", [4, 96, NT], BF16)
        self.vtokd = sc("vtokd", [NT, 260], BF16)
        self.yT = self.nc.dram_tensor("yT", [D, NT], F32, kind="ExternalOutput").ap()
        self.xs = self.dscr("xs", [D, NT], F32)
        self.ymix = self.dscr("ymix", [16, 64, NT], BF16)
        self.bar = self.nc.dram_tensor("bar", [2, 64], F32, kind="Internal").ap()
        self.pb = [self.nc.alloc_psum_tensor("pb%d" % i, [128, 512], F32) for i in range(8)]

    def prologue(self):
        S = self.S
        NS = self.NS
        self.cst_sb = self.sb([128, CST_W], F32, "cst")
        self.dma(self.cst_sb[:], self.cst, W=["cst"])
        self.nrm_sb = self.sb([128, (NL * 3 + 1) * 8], F32, "nrm")
        self.dma(self.nrm_sb[:], self.nrm, W=["nrm"])
        self.vec_sb = self.sb([128, NL * VEC_W], F32, "vec")
        self.dma(self.vec_sb[:], self.vec, W=["vec"])
        self.ones_f = self.sb([128, 64], F32, "onesf")
        self.memset(self.ones_f[:], 1.0, W=["onesf"])
        self.ident_bf = self.sb([128, 128], BF16, "ident")
        self.cp(self.ident_bf[:], self.cst_sb[:, C_ID:C_ID + 128], R=["cst"], W=["ident"])
        self.ones_bf = self.sb([128, 128], BF16, "ones")
        self.memset(self.ones_bf[:], 1.0, W=["ones"])
        self.eps_c = self.cst_sb[:, C_EPS:C_EPS + 1]
        self.one_c = self.cst_sb[:, C_ONE:C_ONE + 1]
        self.mod = [self.sb([128, 72, NS], F32, "mod%d" % l) for l in range(NL)]
        self.Gn = [self.sb([128, 3, 8, NS], F32, "Gn%d" % l) for l in range(NL)]
        self.gate = [self.sb([128, 3, 8, NS], F32, "gate%d" % l) for l in range(NL)]
        persist_end = self.sbp
        csil = self.sb([128, 8, NS], F32, "csil")
        adab = self.sb([128, NL, 72], F32, "adab")
        self.dma(csil[:], self.cT, W=["csil"])
        self.dma(adab[:], self.adabT.rearrange("l p j -> p l j"), W=["adab"])
        self.act(csil[:], csil[:], AF.Silu, R=["csil"], W=["csil"])
        wrot = self.rot("adaw", 2, [128, 8, 512], F32)
        ps = self.pb[0]
        for l in range(NL):
            psv = ps[:, 0:72 * NS].rearrange("p (j s) -> p j s", s=NS)
            for pc in range(18):
                wt, wk = wrot.next()
                src = self.ada_w[l].rearrange("(kc p) c -> p kc c", p=128)[:, :, pc * 512:(pc + 1) * 512]
                self.dma(wt[:], src, W=[wk], q=("sp" if pc % 2 == 0 else "act"))
                for jj in range(4):
                    j = pc * 4 + jj
                    for kc in range(8):
                        self.mm(psv[:, j, :], wt[:, kc, jj * 128:(jj + 1) * 128], csil[:, kc, :],
                                kc == 0, kc == 7, R=[wk, "csil"], W=["pb0"])
            self.tt(self.mod[l][:], psv, adab[:, l, :].unsqueeze(2).to_broadcast([128, 72, NS]), ALU.add,
                    R=["pb0", "adab"], W=["mod%d" % l])
            m = self.mod[l][:].rearrange("p (i oc) s -> p i oc s", i=9)
            for k, (isc, igate, gmul) in enumerate(((1, 2, 0.5), (4, 5, 1.0), (7, 8, 0.5))):
                nv = self.nrm_sb[:, (l * 3 + k) * 8:(l * 3 + k + 1) * 8]
                self.ts(self.Gn[l][:, k], m[:, isc], 1.0, ALU.add, R=["mod%d" % l], W=["Gn%d" % l])
                self.tt(self.Gn[l][:, k], self.Gn[l][:, k], nv.unsqueeze(2).to_broadcast([128, 8, NS]), ALU.mult,
                        R=["Gn%d" % l, "nrm"], W=["Gn%d" % l])
                self.ts(self.gate[l][:, k], m[:, igate], 1.0, ALU.add, R=["mod%d" % l], W=["gate%d" % l],
                        s2=gmul, op1=ALU.mult)
        self.sbp = persist_end
        self.persist_end = persist_end
        S.phase_barrier(self.bar[0:1, 0:8], self.cst[0:1, 0:8])

    def shift(self, l, k, kc, s):
        i = (0, 3, 6)[k]
        return self.mod[l][:, i * 8 + kc, s:s + 1]

    def norm_tile(self, x, xk, T, l, k, s, sq, sqk, t, tk, h, hk, ssb, rs, rsk):
        if isinstance(sq, Rot):
            for kc in range(8):
                sq1, sq1k = sq.next()
                self.act(sq1[:], x[:, kc, :], AF.Square, R=[xk], W=[sq1k])
                self.mm(ssb[:, 0:T], self.ones_bf[:], sq1[:], kc == 0, kc == 7, R=[sq1k, "ones"], W=["pb6"])
        else:
            self.act(sq[:], x[:], AF.Square, R=[xk], W=[sqk])
            for kc in range(8):
                self.mm(ssb[:, 0:T], self.ones_bf[:], sq[:, kc, :], kc == 0, kc == 7, R=[sqk, "ones"], W=["pb6"])
        self.act(rs[:], ssb[:, 0:T], AF.Sqrt, R=["pb6", "cst"], W=[rsk], bias=self.eps_c, scale=1.0 / D)
        self.recip(rs[:], rs[:], R=[rsk], W=[rsk])
        for kc in range(8):
            tt_, ttk = t.next()
            self.tt(tt_[:], x[:, kc, :], rs[:], ALU.mult, R=[xk, rsk], W=[ttk])
            self.act(h[:, kc, :], tt_[:], AF.Identity, R=[ttk, "Gn%d" % l, "mod%d" % l], W=[hk],
                     bias=self.shift(l, k, kc, s), scale=self.Gn[l][:, k, kc, s:s + 1])

    def ffn_sweep(self, l, which, src, dst, pre_wout=False, final_norm=False, final=False):
        S = self.S
        T = 256
        NT = self.NT
        k = 0 if which == 0 else 2
        self.sbp = self.persist_end
        wg = self.sb([128, 8, FF], BF16, "wg")
        wu = self.sb([128, 8, FF], BF16, "wu")
        wd = self.sb([128, 22, D], BF16, "wd")
        self.dma(wg[:], self.wg[which][l].rearrange("(kc p) f -> p kc f", p=128), W=["wg"], q="pool")
        self.dma(wu[:], self.wu[which][l].rearrange("(kc p) f -> p kc f", p=128), W=["wu"], q="pool")
        self.dma(wd[:], self.wd[which][l].rearrange("(fc p) d -> p fc d", p=128), W=["wd"], q="pool")
        if pre_wout:
            wo = self.sb([128, 8, D], BF16, "wo")
            self.dma(wo[:], self.w_out[l].rearrange("(kc p) d -> p kc d", p=128), W=["wo"], q="pool")
            yrot = self.rot("yt", 2, [128, 8, T], BF16)
        xrot = self.rot("x", 3, [128, 8, T], F32)
        sq = self.rot("sq", 3, [128, T], BF16)
        t = self.rot("t", 2, [128, T], F32)
        hrot = self.rot("h", 2, [128, 8, T], BF16)
        rsrot = self.rot("rs", 2, [128, T], F32)
        sgrot = self.rot("sg", 2, [128, T], BF16)
        arot = self.rot("a", 3, [128, T], BF16)
        acc = [self.pb[i] for i in range(4)]
        gurot = self.pbrot([4, 5])
        ssb = self.pb[6]

        def accv(oc):
            return acc[oc // 2][:, (oc % 2) * 256:(oc % 2) * 256 + T], "pb%d" % (oc // 2)

        ntiles = NT // T
        xsrc = src.rearrange("(kc p) t -> p kc t", p=128)
        xdst = dst.rearrange("(kc p) t -> p kc t", p=128)
        ysrc = self.ymix.rearrange("(kc two) f t -> (two f) kc t", two=2)
        state = {}

        def stage_load(i):
            x, xk = xrot.next()
            self.dma(x[:], xsrc[:, :, i * T:(i + 1) * T], W=[xk], q="sp")
            state[i] = dict(x=x, xk=xk)
            if pre_wout:
                y, yk = yrot.next()
                self.dma(y[:], ysrc[:, :, i * T:(i + 1) * T], W=[yk], q="sp")
                state[i].update(y=y, yk=yk)

        def norm_steps(i):
            st = state[i]
            s = self.seq_of(i * T)
            x, xk = st["x"], st["xk"]
            steps = []
            if pre_wout:
                y, yk = st["y"], st["yk"]
                slots = [(self.pb[6][:, 256:256 + T], "pb6"), (self.pb[7][:, 0:T], "pb7")]

                def pre(oc):
                    av, ak = slots[oc % 2]
                    for kc in range(8):
                        self.mm(av, wo[:, kc, oc * 128:(oc + 1) * 128], y[:, kc, :], kc == 0, kc == 7,
                                R=[yk, "wo"], W=[ak])
                    self.stt(x[:, oc, :], av, self.gate[l][:, 1, oc, s:s + 1], x[:, oc, :], ALU.mult, ALU.add,
                             R=[ak, xk, "gate%d" % l], W=[xk])
                for o2 in range(4):
                    steps.append(lambda o2=o2: (pre(2 * o2), pre(2 * o2 + 1)))
            h, hk = hrot.next()
            rs, rsk = rsrot.next()
            st.update(h=h, hk=hk)

            def n2():
                for kc in range(8):
                    sq1, sq1k = sq.next()
                    self.tt(sq1[:], x[:, kc, :], x[:, kc, :], ALU.mult, R=[xk], W=[sq1k], eng="pool")
                    self.mm(ssb[:, 0:T], self.ones_bf[:], sq1[:], kc == 0, kc == 7, R=[sq1k, "ones"], W=["pb6"])

            def n3():
                self.act(rs[:], ssb[:, 0:T], AF.Sqrt, R=["pb6", "cst"], W=[rsk], bias=self.eps_c, scale=1.0 / D)
                self.recip(rs[:], rs[:], R=[rsk], W=[rsk])

            def n4(kc):
                tt_, ttk = t.next()
                self.tt(tt_[:], x[:, kc, :], rs[:], ALU.mult, R=[xk, rsk], W=[ttk])
                self.ts(h[:, kc, :], tt_[:], self.Gn[l][:, k, kc, s:s + 1], ALU.mult,
                        R=[ttk, "Gn%d" % l, "mod%d" % l], W=[hk], s2=self.shift(l, k, kc, s), op1=ALU.add)
            steps.append(n2)
            steps.append(None)
            steps.append(n3)
            steps.append(None)
            for kc in range(8):
                steps.append(lambda kc=kc: n4(kc))
            return steps

        def stage_norm(i):
            for f in norm_steps(i):
                if f is not None:
                    f()

        def stage_body(i):
            st = state[i]
            s = self.seq_of(i * T)
            x, xk, h, hk = st["x"], st["xk"], st["h"], st["hk"]

            def gu(fc):
                gub, gk = gurot.next()
                uk = gk
                g = gub[:, 0:T]
                u = gub[:, 256:256 + T]
                for kc in range(8):
                    self.mm(g, wg[:, kc, fc * 128:(fc + 1) * 128], h[:, kc, :], kc == 0, kc == 7, R=[hk, "wg"], W=[gk])
                for kc in range(8):
                    self.mm(u, wu[:, kc, fc * 128:(fc + 1) * 128], h[:, kc, :], kc == 0, kc == 7, R=[hk, "wu"], W=[uk])
                return g, gk, u, uk

            nsteps = norm_steps(i + 1) if i + 1 < ntiles else []
            pend = gu(0)
            for fc in range(22):
                g, gk, u, uk = pend
                if fc + 1 < 22:
                    pend = gu(fc + 1)
                if fc >= 1 and nsteps:
                    f = nsteps.pop(0)
                    if f is not None:
                        f()
                sg, sgk = sgrot.next()
                a, ak = arot.next()
                self.act(sg[:], g, AF.Silu, R=[gk], W=[sgk])
                self.tt(a[:], u, sg[:], ALU.mult, R=[uk, sgk], W=[ak])
                for oc in range(8):
                    av, avk = accv(oc)
                    self.mm(av, wd[:, fc, oc * 128:(oc + 1) * 128], a[:], fc == 0 and oc % 2 == 0, fc == 21,
                            R=[ak, "wd"], W=[avk], skip=True)
            while nsteps:
                f = nsteps.pop(0)
                if f is not None:
                    f()
            for oc in range(8):
                av, avk = accv(oc)
                self.stt(x[:, oc, :], av, self.gate[l][:, k, oc, s:s + 1], x[:, oc, :], ALU.mult, ALU.add,
                         R=[avk, xk, "gate%d" % l], W=[xk])
            if final_norm:
                rs, rsk = rsrot.next()
                for kc in range(8):
                    sq1, sq1k = sq.next()
                    self.act(sq1[:], x[:, kc, :], AF.Square, R=[xk], W=[sq1k])
                    self.mm(ssb[:, 0:T], self.ones_bf[:], sq1[:], kc == 0, kc == 7, R=[sq1k, "ones"], W=["pb6"])
                self.act(rs[:], ssb[:, 0:T], AF.Sqrt, R=["pb6", "cst"], W=[rsk], bias=self.eps_c, scale=1.0 / D)
                self.recip(rs[:], rs[:], R=[rsk], W=[rsk])
                fn = self.nrm_sb[:, NL * 3 * 8:NL * 3 * 8 + 8]
                for kc in range(8):
                    self.stt(x[:, kc, :], x[:, kc, :], fn[:, kc:kc + 1], rs[:], ALU.mult, ALU.mult,
                             R=[xk, rsk, "nrm"], W=[xk])
            self.dma(xdst[:, :, i * T:(i + 1) * T], x[:], R=[xk], q="act", final=final)
            del state[i]

        stage_load(0)
        if ntiles > 1:
            stage_load(1)
        stage_norm(0)
        for i in range(ntiles):
            if i + 2 < ntiles:
                stage_load(i + 2)
            stage_body(i)
        S.phase_barrier(self.bar[0:1, 0:8], self.cst[0:1, 0:8])


    def proj_sweep(self, l):
        S = self.S
        T = 512
        NT = self.NT
        self.sbp = self.persist_end
        cst = self.cst_sb
        vec = self.vec_sb[:, l * VEC_W:(l + 1) * VEC_W]
        win = self.sb([128, 8, PW], BF16, "win")
        self.dma(win[:], self.w_in[l].rearrange("(kc p) c -> p kc c", p=128), W=["win"], q="pool")
        wkpe = self.sb([128, 8, 96], BF16, "wkpe")
        wkrh = self.sb([128, 8, 96], BF16, "wkrh")
        self.memset(wkpe[:], 0.0, W=["wkpe"])
        self.memset(wkrh[:], 0.0, W=["wkrh"])
        wsrc = self.w_in[l].rearrange("(kc p) c -> p kc c", p=128)
        self.dma(wkpe[:, :, 64:96], wsrc[:, :, 2208:2240], W=["wkpe"], q="pool")
        self.dma(wkrh[:, :, 64:80], wsrc[:, :, 2224:2240], W=["wkrh"], q="pool")
        self.dma(wkrh[:, :, 80:96], wsrc[:, :, 2208:2224], W=["wkrh"], q="pool")
        self.ts(wkrh[:, :, 64:80], wkrh[:, :, 64:80], -1.0, ALU.mult, R=["wkrh"], W=["wkrh"])
        wuq_f = self.sb([128, 2, 384], F32, "wuqf")
        self.dma(wuq_f[:], self.w_uq[l].rearrange("(kc p) c -> p kc c", p=128), W=["wuqf"])
        wuq = self.sb([128, 2, 384], BF16, "wuq")
        wuqr = self.sb([128, 2, 384], BF16, "wuqr")
        for kc in range(2):
            self.ts(wuq[:, kc, :], wuq_f[:, kc, :], vec[:, kc:kc + 1], ALU.mult, R=["wuqf", "vec"], W=["wuq"])
            self.cp(wuqr[:, kc, :], wuq[:, kc, :], R=["wuq"], W=["wuqr"])
            v4 = wuq[:, kc, :].rearrange("p (h c) -> p h c", c=96)
            r4 = wuqr[:, kc, :].rearrange("p (h c) -> p h c", c=96)
            self.ts(r4[:, :, 64:80], v4[:, :, 80:96], -1.0, ALU.mult, R=["wuq", "wuqr"], W=["wuqr"])
            self.cp(r4[:, :, 80:96], v4[:, :, 64:80], R=["wuq", "wuqr"], W=["wuqr"])
        wukv_f = self.sb([128, 512], F32, "wukvf")
        self.dma(wukv_f[:], self.w_ukv[l], W=["wukvf"])
        wukv = self.sb([128, 512], BF16, "wukv")
        self.ts(wukv[:], wukv_f[:], vec[:, 2:3], ALU.mult, R=["wukvf", "vec"], W=["wukv"])
        wukvv = self.sb([128, 4, 64], BF16, "wukvv")
        self.cp(wukvv[:], wukv[:].rearrange("p (h c) -> p h c", c=128)[:, :, 64:128], R=["wukv"], W=["wukvv"])
        wgk = self.sb([33, 256], BF16, "wgk")
        self.dma(wgk[:], self.wgk[l], W=["wgk"], q="pool")
        xrot = self.rot("x", 2, [128, 8, T], F32)
        csrot = self.rot("cs", 2, [96, 2, T], F32)
        sq = self.sb([128, 8, T], BF16, "sq")
        trot = self.rot("t", 2, [128, T], F32)
        hrot = self.rot("h", 2, [128, 8, T], BF16)
        rsrot = self.rot("rs", 2, [128, T], F32)
        lrT = self.sb([33, T], BF16, "lrT")
        self.memset(lrT[:], 1.0, W=["lrT"])
        cq_sb = self.sb([128, 2, T], BF16, "cq")
        sqq = self.sb([128, 2, T], BF16, "sqq")
        ckv_sb = self.sb([128, T], BF16, "ckv")
        sqkv = self.sb([128, T], BF16, "sqkv")
        rq = self.sb([128, T], F32, "rq")
        rkv = self.sb([128, T], F32, "rkv")
        rkvc = self.sb([128, 4], F32, "rkvc")
        t1 = self.sb([96, T], F32, "t1")
        t2 = self.sb([96, T], F32, "t2")
        etmp = self.sb([128, 256], F32, "etmp")
        st = {}
        for nm, shp, dt in (("q", [128, T], BF16), ("k", [128, T], BF16), ("go", [128, 2, T], BF16),
                            ("u", [128, 2, T], F32), ("qc", [128, 2, T], BF16), ("kc", [128, 2, T], BF16),
                            ("krot", [96, T], BF16), ("qd", [96, 4, T], BF16), ("kn", [64, 4, T], BF16),
                            ("kv", [128, 4, 384], BF16), ("vc", [128, 4, 4, 65], BF16), ("sp", [128, 4, 256], F32),
                            ("vd", [128, 4, 4, 65], BF16)):
            st[nm] = self.sb(shp, dt, "st_" + nm)
        self.memset(st["vc"][:], 1.0, W=["st_vc"])
        self.memset(st["vd"][:], 1.0, W=["st_vd"])
        prot = self.pbrot(range(6))
        ssb = self.pb[6]
        xsrc = self.xs.rearrange("(kc p) t -> p kc t", p=128)
        ntiles = NT // T
        state = {}

        def stage_load(i):
            x, xk = xrot.next()
            self.dma(x[:], xsrc[:, :, i * T:(i + 1) * T], W=[xk], q="sp")
            cs, csk = csrot.next()
            s = self.seq_of(i * T)
            p0 = i * T - self.off[s]
            self.dma(cs[64:96, :, :], self.rope[:, :, p0:p0 + T], W=[csk], q="sp")
            state[i] = dict(x=x, xk=xk, cs=cs, csk=csk)

        pending = []

        def norm_steps(i):
            sti = state[i]
            s = self.seq_of(i * T)
            x, xk = sti["x"], sti["xk"]
            h, hk = hrot.next()
            rs, rsk = rsrot.next()
            sti.update(h=h, hk=hk)
            steps = []

            def n2():
                self.act(sq[:], x[:], AF.Square, R=[xk], W=["sq"])
                for kc in range(8):
                    self.mm(ssb[:, 0:T], self.ones_bf[:], sq[:, kc, :], kc == 0, kc == 7, R=["sq", "ones"], W=["pb6"])

            def n3():
                self.act(rs[:], ssb[:, 0:T], AF.Sqrt, R=["pb6", "cst"], W=[rsk], bias=self.eps_c, scale=1.0 / D)
                self.recip(rs[:], rs[:], R=[rsk], W=[rsk])

            def n4(kc):
                tt_, ttk = trot.next()
                self.tt(tt_[:], x[:, kc, :], rs[:], ALU.mult, R=[xk, rsk], W=[ttk])
                self.ts(h[:, kc, :], tt_[:], self.Gn[l][:, 1, kc, s:s + 1], ALU.mult,
                        R=[ttk, "Gn%d" % l, "mod%d" % l], W=[hk], s2=self.shift(l, 1, kc, s), op1=ALU.add)
            steps.append(n2)
            steps.append(None)
            steps.append(n3)
            steps.append(None)
            for kc in range(8):
                steps.append(lambda kc=kc: n4(kc))
            return steps

        def stage_norm(i):
            for f in norm_steps(i):
                if f is not None:
                    f()

        def poll():
            if pending:
                f = pending.pop(0)
                if f is not None:
                    f()

        def stage_body(i):
            sti = state[i]
            h, hk, cs, csk = sti["h"], sti["hk"], sti["cs"], sti["csk"]
            t0 = i * T
            cosv = cs[64:96, 0, :]
            sinv = cs[64:96, 1, :]

            def fm(w, wk, c0, M):
                ps, pk = prot.next()
                for kc in range(8):
                    self.mm(ps[0:M, :], w[:, kc, c0:c0 + M], h[:, kc, :], kc == 0, kc == 7, R=[hk, wk], W=[pk])
                poll()
                return ps, pk

            ps, pk = fm(win, "win", 0, 128)
            self.act(st["q"][:], ps[:], AF.Copy, R=[pk], W=["st_q"])
            self.dma(self.qTa[:, t0:t0 + T], st["q"][:], R=["st_q"], q="pool")
            ps, pk = fm(win, "win", 128, 128)
            self.cp(st["k"][:], ps[:], R=[pk], W=["st_k"])
            self.dma(self.kTa[:, t0:t0 + T], st["k"][:], R=["st_k"], q="pool")
            for hp in range(2):
                ps, pk = fm(win, "win", 512 + hp * 128, 128)
                self.act(st["go"][:, hp, :], ps[:], AF.Silu, R=[pk], W=["st_go"])
            self.dma(self.goT[:, :, t0:t0 + T].rearrange("(hp two) p t -> (two p) hp t", two=2), st["go"][:],
                     R=["st_go"], q="pool")
            ps, pk = fm(win, "win", 768, 32)
            self.cp(lrT[0:32, :], ps[0:32, :], R=[pk], W=["lrT"])
            for c in range(2):
                ps, pk = fm(win, "win", 800 + c * 128, 128)
                self.act(st["u"][:, c, :], ps[:], AF.Copy, R=[pk], W=["st_u"])
            self.dma(self.uT.rearrange("(c p) t -> p c t", p=128)[:, :, t0:t0 + T], st["u"][:], R=["st_u"], q="pool")
            for hp in range(2):
                ps, pk = fm(win, "win", 1056 + hp * 128, 128)
                self.ts(st["qc"][:, hp, :], ps[:], 0.125, ALU.mult, R=[pk], W=["st_qc"])
            self.dma(self.qTc[:, :, t0:t0 + T].rearrange("(hp two) p t -> (two p) hp t", two=2), st["qc"][:],
                     R=["st_qc"], q="pool")
            for hp in range(2):
                ps, pk = fm(win, "win", 1312 + hp * 128, 128)
                self.act(st["kc"][:, hp, :], ps[:], AF.Copy, R=[pk], W=["st_kc"])
            self.dma(self.kTc[:, :, t0:t0 + T].rearrange("(hp two) p t -> (two p) hp t", two=2), st["kc"][:],
                     R=["st_kc"], q="pool")
            for c in range(2):
                ps, pk = fm(win, "win", 1824 + c * 128, 128)
                self.act(cq_sb[:, c, :], ps[:], AF.Copy, R=[pk], W=["cq"])
                self.act(sqq[:, c, :], ps[:], AF.Square, R=[pk], W=["sqq"])
            for c in range(2):
                self.mm(ssb[:, 0:T], self.ones_bf[:], sqq[:, c, :], c == 0, c == 1, R=["sqq", "ones"], W=["pb6"])
            self.act(rq[:], ssb[:, 0:T], AF.Ln, R=["pb6", "cst"], W=["rq"], bias=self.eps_c, scale=1.0 / 256)
            self.act(rq[:], rq[:], AF.Exp, R=["rq"], W=["rq"], scale=-0.5)
            ps, pk = fm(win, "win", 2080, 128)
            self.act(ckv_sb[:], ps[:], AF.Copy, R=[pk], W=["ckv"])
            self.act(sqkv[:], ps[:], AF.Square, R=[pk], W=["sqkv"])
            self.mm(ssb[:, 0:T], self.ones_bf[:], sqkv[:], True, True, R=["sqkv", "ones"], W=["pb6"])
            self.act(rkv[:], ssb[:, 0:T], AF.Ln, R=["pb6", "cst"], W=["rkv"], bias=self.eps_c, scale=1.0 / 128)
            self.act(rkv[:], rkv[:], AF.Exp, R=["rkv"], W=["rkv"], scale=-0.5)
            for j in range(4):
                self.mm(ssb[:, j:j + 1], sqkv[:, j * 128:(j + 1) * 128], self.ones_bf[:, 0:1], True, True,
                        R=["sqkv", "ones"], W=["pb6"])
            self.act(rkvc[:], ssb[:, 0:4], AF.Ln, R=["pb6", "cst"], W=["rkvc"], bias=self.eps_c, scale=1.0 / 128)
            self.act(rkvc[:], rkvc[:], AF.Exp, R=["rkvc"], W=["rkvc"], scale=-0.5)
            psA, pkA = fm(wkpe, "wkpe", 0, 96)
            psB, pkB = fm(wkrh, "wkrh", 0, 96)
            self.tt(t1[64:96, :], psA[64:96, :], cosv, ALU.mult, R=[pkA, csk], W=["t1"])
            self.tt(t2[64:96, :], psB[64:96, :], sinv, ALU.mult, R=[pkB, csk], W=["t2"])
            self.tt(st["krot"][64:96, :], t1[64:96, :], t2[64:96, :], ALU.add, R=["t1", "t2"], W=["st_krot"])
            for hh in range(4):
                self.dma(self.kTd[hh, 64:96, t0:t0 + T], st["krot"][64:96, :], R=["st_krot"], q="pool")
            for hh in range(4):
                psQ, pkQ = prot.next()
                for kc in range(2):
                    self.mm(psQ[0:96, :], wuq[:, kc, hh * 96:(hh + 1) * 96], cq_sb[:, kc, :], kc == 0, kc == 1,
                            R=["cq", "wuq"], W=[pkQ])
                psR, pkR = prot.next()
                for kc in range(2):
                    self.mm(psR[0:96, :], wuqr[:, kc, hh * 96:(hh + 1) * 96], cq_sb[:, kc, :], kc == 0, kc == 1,
                            R=["cq", "wuqr"], W=[pkR])
                self.stt(st["qd"][0:64, hh, :], psQ[0:64, :], MLA_SCALE, rq[0:64, :], ALU.mult, ALU.mult,
                         R=[pkQ, "rq"], W=["st_qd"])
                self.tt(t1[64:96, :], psQ[64:96, :], cosv, ALU.mult, R=[pkQ, csk], W=["t1"])
                self.tt(t2[64:96, :], psR[64:96, :], sinv, ALU.mult, R=[pkR, csk], W=["t2"])
                self.tt(t1[64:96, :], t1[64:96, :], t2[64:96, :], ALU.add, R=["t1", "t2"], W=["t1"])
                self.stt(st["qd"][64:96, hh, :], t1[64:96, :], MLA_SCALE, rq[64:96, :], ALU.mult, ALU.mult,
                         R=["t1", "rq"], W=["st_qd"])
            self.dma(self.qTd[:, :, t0:t0 + T].rearrange("h p t -> p h t"), st["qd"][:], R=["st_qd"], q="pool")
            for hh in range(4):
                ps, pk = prot.next()
                self.mm(ps[0:64, :], wukv[:, hh * 128:hh * 128 + 64], ckv_sb[:], True, True, R=["ckv", "wukv"], W=[pk])
                self.tt(st["kn"][:, hh, :], ps[0:64, :], rkv[0:64, :], ALU.mult, R=[pk, "rkv"], W=["st_kn"])
            self.dma(self.kTd[:, 0:64, t0:t0 + T].rearrange("h p t -> p h t"), st["kn"][:], R=["st_kn"], q="pool")
            for j in range(4):
                tok = slice(j * 128, (j + 1) * 128)
                ps, pk = prot.next()
                for kc in range(8):
                    self.mm(ps[:, 0:384], h[:, kc, tok], win[:, kc, 128:512], kc == 0, kc == 7, R=[hk, "win"], W=[pk])
                self.act(st["kv"][:, j, :], ps[:, 0:384], AF.Copy, R=[pk], W=["st_kv"])
                ps, pk = prot.next()
                for kc in range(8):
                    self.mm(ps[:, 0:256], h[:, kc, tok], win[:, kc, 1568:1824], kc == 0, kc == 7, R=[hk, "win"], W=[pk])
                self.cp(st["vc"][:, j, :, 0:64], ps[:, 0:256].rearrange("p (h c) -> p h c", c=64), R=[pk], W=["st_vc"])
                ps, pk = prot.next()
                self.mm(ps[:, 0:256], lrT[0:33, tok], wgk[0:33, :], True, True, R=["lrT", "wgk"], W=[pk])
                self.act(etmp[:], ps[:, 0:256], AF.Exp, R=[pk], W=["etmp"], scale=-1.0)
                self.act(st["sp"][:, j, :], etmp[:], AF.Ln, R=["etmp", "cst"], W=["st_sp"], bias=self.one_c)
                ps, pk = prot.next()
                self.mm(ps[:, 0:256], ckv_sb[:, tok], wukvv[:].rearrange("p h c -> p (h c)"), True, True,
                        R=["ckv", "wukvv"], W=[pk])
                self.ts(st["vd"][:, j, :, 0:64], ps[:, 0:256].rearrange("p (h c) -> p h c", c=64), rkvc[:, j:j + 1],
                        ALU.mult, R=[pk, "rkvc"], W=["st_vd"])
            tmv = lambda d: d[t0:t0 + T, :].rearrange("(j p) c -> p j c", p=128)
            self.dma(tmv(self.ktok), st["kv"][:, :, 0:128], R=["st_kv"], q="pool")
            self.dma(tmv(self.vtok), st["kv"][:, :, 128:384], R=["st_kv"], q="pool")
            self.dma(tmv(self.vtokc), st["vc"][:].rearrange("p j h c -> p j (h c)"), R=["st_vc"], q="pool")
            self.dma(tmv(self.sptok), st["sp"][:], R=["st_sp"], q="pool")
            self.dma(tmv(self.vtokd), st["vd"][:].rearrange("p j h c -> p j (h c)"), R=["st_vd"], q="pool")
            del state[i]

        stage_load(0)
        stage_norm(0)
        for i in range(ntiles):
            if i + 1 < ntiles:
                stage_load(i + 1)
                pending.extend(norm_steps(i + 1))
            stage_body(i)
            while pending:
                poll()
        S.phase_barrier(self.bar[0:1, 0:8], self.cst[0:1, 0:8])


    def pool_mixer(self, l, s):
        Sq = self.SEQS[s]
        o = self.off[s]
        TP = 1024 if Sq >= 1024 else Sq
        L = TP + 16
        self.sbp = self.persist_end
        vec = self.vec_sb[:, l * VEC_W:(l + 1) * VEC_W]
        pw_f = self.sb([128, 2, 128], F32, "pwf")
        pw = self.sb([128, 2, 128], BF16, "pw")
        self.dma(pw_f[:], self.poolw[l].rearrange("c p q -> p c q"), W=["pwf"])
        self.cp(pw[:], pw_f[:], R=["pwf"], W=["pw"])
        urot = self.rot("u", 2, [128, 2, L], F32)
        icrot = self.rot("ic", 2, [128, 2, TP], F32)
        P1 = self.sb([128, 2, L], F32, "P1")
        Q = self.sb([128, 2, L], F32, "Q")
        Rr = self.sb([128, L], F32, "R")
        Tt = self.sb([128, L], F32, "Tt")
        dd = self.sb([128, 2, TP], F32, "dd")
        db = self.sb([128, 2, TP], BF16, "db")
        yo = self.rot("yo", 2, [128, 2, TP], BF16)
        prot = self.pbrot(range(4))
        usrc = self.uT.rearrange("(c p) t -> p c t", p=128)
        icsrc = self.icnt[Sq].rearrange("(c p) t -> p c t", p=128)
        ydst = self.ymix[4:8].rearrange("(c two) f t -> (two f) c t", two=2)
        for it in range(Sq // TP):
            p0 = it * TP
            u, uk = urot.next()
            ic, ick = icrot.next()
            lo = max(p0 - 8, 0)
            hi = min(p0 + TP + 8, Sq)
            if lo > p0 - 8:
                self.memset(u[:, :, 0:8], 0.0, W=[uk])
            if hi < p0 + TP + 8:
                self.memset(u[:, :, L - 8:L], 0.0, W=[uk])
            self.dma(u[:, :, lo - (p0 - 8):hi - (p0 - 8)], usrc[:, :, o + lo:o + hi], W=[uk])
            self.dma(ic[:], icsrc[:, :, p0:p0 + TP], W=[ick])
            self.tt(P1[:, :, 1:L], u[:, :, 0:L - 1], u[:, :, 1:L], ALU.add, R=[uk], W=["P1"])
            self.tt(Q[:, :, 2:L - 2], P1[:, :, 1:L - 3], P1[:, :, 3:L - 1], ALU.add, R=["P1"], W=["Q"])
            self.tt(Rr[:, 4:L - 4], Q[:, 1, 2:L - 6], Q[:, 1, 6:L - 2], ALU.add, R=["Q"], W=["R"])
            self.tt(Tt[64:128, 8:L - 8], Rr[64:128, 4:L - 12], Rr[64:128, 12:L - 4], ALU.add, R=["R"], W=["Tt"])
            c8 = slice(8, 8 + TP)
            self.tt(dd[0:64, 0, :], P1[0:64, 0, c8], ic[0:64, 0, :], ALU.mult, R=["P1", ick], W=["dd"])
            self.tt(dd[64:128, 0, :], Q[64:128, 0, c8], ic[64:128, 0, :], ALU.mult, R=["Q", ick], W=["dd"])
            self.tt(dd[0:64, 1, :], Rr[0:64, c8], ic[0:64, 1, :], ALU.mult, R=["R", ick], W=["dd"])
            self.tt(dd[64:128, 1, :], Tt[64:128, c8], ic[64:128, 1, :], ALU.mult, R=["Tt", ick], W=["dd"])
            self.tt(db[:], dd[:], u[:, :, c8], ALU.subtract, R=["dd", uk], W=["db"])
            y, yk = yo.next()
            for c in range(2):
                for q in range(TP // 512):
                    ps, pk = prot.next()
                    self.mm(ps[:], pw[:, c, :], db[:, c, q * 512:(q + 1) * 512], True, True, R=["db", "pw"], W=[pk])
                    self.act(y[:, c, q * 512:(q + 1) * 512], ps[:], AF.Copy, R=[pk, "vec"], W=[yk],
                             scale=vec[:, 3 + c:4 + c])
            self.dma(ydst[:, :, o + p0:o + p0 + TP], y[:], R=[yk], q="pool")
        self.S.phase_barrier(self.bar[0:1, 0:8], self.cst[0:1, 0:8])

    def mla_mixer(self, l, s):
        Sq = self.SEQS[s]
        o = self.off[s]
        NK = Sq // 128
        self.sbp = self.persist_end
        KT = self.sb([96, 4, Sq], BF16, "KT")
        self.dma(KT[:], self.kTd[:, :, o:o + Sq].rearrange("h p t -> p h t"), W=["KT"])
        Vp = self.sb([128, NK, 4, 65], BF16, "Vp")
        self.dma(Vp[:].rearrange("p n h c -> p n (h c)"), self.vtokd[o:o + Sq, :].rearrange("(n p) c -> p n c", p=128),
                 W=["Vp"])
        qrot = self.rot("QT", 2, [96, 4, 512], BF16)
        prot_p = self.rot("pT", 3, [128, 512], BF16)
        rec = self.sb([65, 512], F32, "rec")
        bcs = self.sb([64, 512], F32, "bcs")
        yrot = self.rot("y", 2, [64, 4, 512], BF16)
        srot = self.pbrot([0, 1, 2])
        orot = self.pbrot([3, 4])
        bcp = self.pb[5]
        NQ = Sq // 512
        qts = {}

        def load_q(qt):
            QT, qk = qrot.next()
            self.dma(QT[:], self.qTd[:, :, o + qt * 512:o + (qt + 1) * 512].rearrange("h p t -> p h t"), W=[qk])
            qts[qt] = (QT, qk)

        items = [(qt, h, kc) for qt in range(NQ) for h in range(4) for kc in range(NK)]
        load_q(0)
        pend = []

        def qk_mm(it):
            qt, h, kc = it
            QT, qk = qts[qt]
            sp_, sk = srot.next()
            self.mm(sp_[:], KT[:, h, kc * 128:(kc + 1) * 128], QT[:, h, :], True, True, R=["KT", qk], W=[sk])
            pend.append((sp_, sk))

        y = yk = None
        cur_o = None
        for n, (qt, h, kc) in enumerate(items):
            if h == 0 and kc == 0:
                y, yk = yrot.next()
                if qt + 1 < NQ:
                    load_q(qt + 1)
            if n == 0:
                qk_mm(items[0])
                if len(items) > 1:
                    qk_mm(items[1])
            if kc == 0:
                cur_o = orot.next()
            ops_, ok = cur_o
            if n + 2 < len(items):
                qk_mm(items[n + 2])
            sp_, sk = pend.pop(0)
            pT, pk = prot_p.next()
            self.act(pT[:], sp_[:], AF.Exp, R=[sk], W=[pk])
            self.mm(ops_[0:65, :], Vp[:, kc, h, :], pT[:], kc == 0, kc == NK - 1, R=["Vp", pk], W=[ok])
            if kc == NK - 1:
                self.act(rec[64:65, :], ops_[64:65, :], AF.Ln, R=[ok], W=["rec"])
                self.act(rec[64:65, :], rec[64:65, :], AF.Exp, R=["rec"], W=["rec"], scale=-1.0)
                self.mm(bcp[0:64, :], self.ones_f[64:65, 0:64], rec[64:65, :], True, True, R=["rec", "onesf"], W=["pb5"])
                self.cp(bcs[:], bcp[0:64, :], R=["pb5"], W=["bcs"])
                self.tt(y[:, h, :], ops_[0:64, :], bcs[:], ALU.mult, R=[ok, "bcs"], W=[yk])
                if h == 3:
                    self.dma(self.ymix[12:16, :, o + qt * 512:o + (qt + 1) * 512].rearrange("h p t -> p h t"), y[:],
                             R=[yk], q="pool")
        self.S.phase_barrier(self.bar[0:1, 0:8], self.cst[0:1, 0:8])

    def na_mixer(self, l, s):
        Sq = self.SEQS[s]
        o = self.off[s]
        P = Sq // 128
        self.sbp = self.persist_end
        KT = self.sb([64, 4, Sq], BF16, "KT")
        self.dma(KT[:], self.kTc[:, :, o:o + Sq].rearrange("h p t -> p h t"), W=["KT"])
        Vp = self.sb([128, P, 4, 65], BF16, "Vp")
        self.dma(Vp[:].rearrange("p n h c -> p n (h c)"), self.vtokc[o:o + Sq, :].rearrange("(n p) c -> p n c", p=128),
                 W=["Vp"])
        tab = self.sb([128, 4, 21, 128], BF16, "tab")
        for h in range(4):
            self.dma(tab[:, h], self.natab[l, h].rearrange("v k q -> k v q"), W=["tab"], q="pool")
        sT_r = self.rot("sT", 4, [128, 640], F32)
        qrot = self.rot("QT", 2, [64, 4, 512], BF16)
        prot_p = self.rot("pT", 3, [128, 640], BF16)
        rec = self.sb([65, 512], F32, "rec")
        bcs = self.sb([64, 512], F32, "bcs")
        yrot = self.rot("y", 2, [64, 4, 512], BF16)
        arot = self.pbrot([0, 1, 2])
        brot = self.pbrot([3, 7])
        orot = self.pbrot([4, 5])
        bcp = self.pb[6]

        def chunks(i):
            if P >= 5 and 2 <= i <= P - 3:
                return [(i - 2 + d, d) for d in range(5)]
            if i == 0:
                return [(j, 5 + j) for j in range(4)]
            if i == 1:
                return [(j, 9 + j) for j in range(4)]
            if i == P - 2:
                return [(P - 4 + j, 13 + j) for j in range(4)]
            assert i == P - 1
            return [(P - 4 + j, 17 + j) for j in range(4)]

        NG = Sq // 512
        qts = {}

        def load_q(g):
            QT, qk = qrot.next()
            self.dma(QT[:], self.qTc[:, :, o + g * 512:o + (g + 1) * 512].rearrange("h p t -> p h t"), W=[qk])
            qts[g] = (QT, qk)

        items = [(g, ip, h) for g in range(NG) for ip in range(4) for h in range(4)]
        pend = []

        def stage1(it):
            g, ip, h = it
            QT, qk = qts[g]
            ch = chunks(g * 4 + ip)
            pa, pak = arot.next()
            pb_, pbk = (None, None)
            if len(ch) > 4:
                pb_, pbk = brot.next()
            for ci, (j, tix) in enumerate(ch):
                dst = pa[:, ci * 128:(ci + 1) * 128] if ci < 4 else pb_[:, 0:128]
                dk = pak if ci < 4 else pbk
                self.mm(dst, KT[:, h, j * 128:(j + 1) * 128], QT[:, h, ip * 128:(ip + 1) * 128], True, True,
                        R=["KT", qk], W=[dk])
            sT, stk = sT_r.next()
            t0x = ch[0][1]
            n4 = min(len(ch), 4)
            self.tt(sT[:, 0:n4 * 128].rearrange("p (c q) -> p c q", c=n4), pa[:, 0:n4 * 128].rearrange("p (c q) -> p c q", c=n4),
                    tab[:, h, t0x:t0x + n4, :], ALU.add, R=[pak, "tab"], W=[stk])
            if len(ch) > 4:
                self.tt(sT[:, 512:640], pb_[:, 0:128], tab[:, h, t0x + 4, :], ALU.add, R=[pbk, "tab"], W=[stk])
            pend.append((ch, sT, stk))

        y = yk = None
        cur_o = None
        for n, (g, ip, h) in enumerate(items):
            if ip == 0 and h == 0:
                y, yk = yrot.next()
                if n == 0:
                    load_q(0)
                    stage1(items[0])
                    if len(items) > 1:
                        stage1(items[1])
                if g + 1 < NG:
                    load_q(g + 1)
            if h == 0:
                cur_o = orot.next()
            ops_, ok = cur_o
            if n + 2 < len(items):
                stage1(items[n + 2])
            ch, sT, stk = pend.pop(0)
            pT, pk = prot_p.next()
            self.act(pT[:, 0:len(ch) * 128], sT[:, 0:len(ch) * 128], AF.Exp, R=[stk], W=[pk])
            for ci, (j, tix) in enumerate(ch):
                self.mm(ops_[0:65, h * 128:(h + 1) * 128], Vp[:, j, h, :], pT[:, ci * 128:(ci + 1) * 128],
                        ci == 0, ci == len(ch) - 1, R=["Vp", pk], W=[ok])
            if h == 3:
                self.act(rec[64:65, :], ops_[64:65, :], AF.Ln, R=[ok], W=["rec"])
                self.act(rec[64:65, :], rec[64:65, :], AF.Exp, R=["rec"], W=["rec"], scale=-1.0)
                self.mm(bcp[0:64, :], self.ones_f[64:65, 0:64], rec[64:65, :], True, True, R=["rec", "onesf"], W=["pb6"])
                self.cp(bcs[:], bcp[0:64, :], R=["pb6"], W=["bcs"])
                self.tt(y[:, :, ip * 128:(ip + 1) * 128], ops_[0:64, :].rearrange("p (h t) -> p h t", h=4),
                        bcs[:].rearrange("p (h t) -> p h t", h=4), ALU.mult, R=[ok, "bcs"], W=[yk])
                if ip == 3:
                    self.dma(self.ymix[8:12, :, o + g * 512:o + (g + 1) * 512].rearrange("h p t -> p h t"), y[:],
                             R=[yk], q="pool")
        self.S.phase_barrier(self.bar[0:1, 0:8], self.cst[0:1, 0:8])


    def gla_mixer(self, l, s):
        Sq = self.SEQS[s]
        o = self.off[s]
        NCH = Sq // 128
        NSC = Sq // 512
        self.sbp = self.persist_end
        cst = self.cst_sb
        vec = self.vec_sb[:, l * VEC_W:(l + 1) * VEC_W]
        triu = cst[:, C_TRIU:C_TRIU + 128]
        tril = cst[:, C_TRIL:C_TRIL + 128]
        nsix = cst[:, C_NSIX:C_NSIX + 1]
        bmask = cst[:, C_BM:C_BM + 256]
        maskF = cst[:, C_MF:C_MF + 128]
        maskB = cst[:, C_MB:C_MB + 128]
        Sf_all = self.sb([128, NCH, 256], BF16, "Sfall")
        Sf = self.sb([128, 256], F32, "Sf")
        Sb = self.sb([128, 256], F32, "Sb")
        Sb_bf = self.sb([128, 256], BF16, "Sbbf")
        tmp = self.sb([128, 256], F32, "tmp")
        ebl = self.sb([128, 1], F32, "ebl")
        einv_tok = self.sb([128, 128], F32, "einvtok")
        ktil = self.sb([128, 128], BF16, "ktil")
        ktok_r = self.rot("ktok", 3, [128, 4, 128], BF16)
        vtok_r = self.rot("vtok", 3, [128, 4, 256], BF16)
        sp_r = self.rot("sp", 3, [128, 4, 256], F32)
        q_r = self.rot("q4", 3, [128, 512], BF16)
        k_r = self.rot("k4", 3, [128, 512], BF16)
        go_r = self.rot("go4", 3, [64, 4, 512], BF16)
        y_r = self.rot("y4", 2, [64, 4, 512], BF16)
        E = self.sb([128, 256], F32, "E")
        EI = self.sb([128, 384], F32, "EI")
        ecf = self.sb([128, 1], F32, "ecf")
        Qf_r = self.rot("Qf", 2, [128, 4, 128], BF16)
        Qb_r = self.rot("Qb", 2, [128, 4, 128], BF16)
        Kf = self.sb([128, 128], BF16, "Kf")
        Kb = self.sb([128, 128], BF16, "Kb")
        ktilb = self.sb([128, 128], BF16, "ktilb")
        Amf = self.sb([128, 4, 128], BF16, "Amf")
        Amb = self.sb([128, 4, 128], BF16, "Amb")
        osq = self.sb([64, 512], BF16, "osq")
        rr = self.sb([64, 512], F32, "rr")
        tt_ = self.sb([64, 512], F32, "tt")
        ktsrc = self.ktok[o:o + Sq, :].rearrange("(n p) c -> p n c", p=128)
        vtsrc = self.vtok[o:o + Sq, :].rearrange("(n p) c -> p n c", p=128)
        spsrc = self.sptok[o:o + Sq, :].rearrange("(n p) c -> p n c", p=128)

        def load_tok(sc, want_qk):
            d = {}
            kt, d["ktk"] = ktok_r.next()
            vt, d["vtk"] = vtok_r.next()
            sp, d["spk"] = sp_r.next()
            self.dma(kt[:], ktsrc[:, sc * 4:(sc + 1) * 4, :], W=[d["ktk"]])
            self.dma(vt[:], vtsrc[:, sc * 4:(sc + 1) * 4, :], W=[d["vtk"]])
            self.dma(sp[:], spsrc[:, sc * 4:(sc + 1) * 4, :], W=[d["spk"]])
            d.update(kt=kt, vt=vt, sp=sp)
            if want_qk:
                q4, d["qk"] = q_r.next()
                k4, d["kk"] = k_r.next()
                go, d["gok"] = go_r.next()
                tsl = slice(o + sc * 512, o + (sc + 1) * 512)
                self.dma(q4[:], self.qTa[:, tsl], W=[d["qk"]])
                self.dma(k4[:], self.kTa[:, tsl], W=[d["kk"]])
                self.dma(go[:], self.goT[:, :, tsl].rearrange("h p t -> p h t"), W=[d["gok"]])
                d.update(q4=q4, k4=k4, go=go)
            return d

        self.memset(Sf[:], 0.0, W=["Sf"])
        self.memset(Sf_all[:, 0, :], 0.0, W=["Sfall"])
        yb = self.pbrot([0, 1])
        kvb = self.pbrot([2, 3])
        ebl_r = self.rot("ebl1", 2, [128, 1], F32)
        einv_r = self.rot("einv1", 2, [128, 128], F32)
        ktil_r = self.rot("ktil1", 2, [128, 128], BF16)
        toks1 = {}

        def get_tok1(sc):
            if sc not in toks1:
                toks1[sc] = load_tok(sc, False)
            return toks1[sc]

        def prep1(n):
            sc, c = n // 4, n % 4
            cur = get_tok1(sc)
            if c == 1 and sc + 1 < NSC:
                get_tok1(sc + 1)
            spf = cur["sp"][:, c, 0:128]
            Y, yk = yb.next()
            self.mm(Y[:, 0:128], triu, spf, True, True, R=["cst", cur["spk"]], W=[yk])
            self.mm(Y[:, 128:129], spf, nsix, True, True, R=["cst", cur["spk"]], W=[yk])
            einv, eik = einv_r.next()
            ebl1, eblk = ebl_r.next()
            ktil1, ktk = ktil_r.next()
            self.act(einv[:], Y[:, 0:128], AF.Exp, R=[yk], W=[eik], scale=-1.0)
            self.act(ebl1[:], Y[:, 128:129], AF.Exp, R=[yk], W=[eblk])
            self.tt(ktil1[:], cur["kt"][:, c, :], einv[:], ALU.mult, R=[cur["ktk"], eik], W=[ktk], eng="pool")
            KV, kvk = kvb.next()
            self.mm(KV[:, 0:256], ktil1[:], cur["vt"][:, c, :], True, True, R=[ktk, cur["vtk"]], W=[kvk])
            return KV, kvk, ebl1, eblk

        if NCH > 1:
            pn = prep1(0)
            for n in range(NCH - 1):
                pnext = prep1(n + 1) if n + 1 < NCH - 1 else None
                KV, kvk, ebl1, eblk = pn
                self.tt(tmp[:], KV[:, 0:256], Sf[:], ALU.add, R=[kvk, "Sf"], W=["tmp"])
                self.stt(Sf[:], tmp[:], ebl1[:, 0:1], bmask, ALU.mult, ALU.mult, R=["tmp", eblk, "cst"], W=["Sf"])
                self.cp(Sf_all[:, n + 1, :], Sf[:], R=["Sf"], W=["Sfall"], eng="pool")
                pn = pnext
        self.memset(Sb[:], 0.0, W=["Sb"])
        self.memset(Sb_bf[:], 0.0, W=["Sbbf"])
        xb = self.pbrot([0, 1])
        ob = self.pbrot([2, 3])
        afp, abp, kvp, ssp = self.pb[4], self.pb[5], self.pb[6], self.pb[7]
        E_r = self.rot("E2", 2, [128, 256], F32)
        EI_r = self.rot("EI2", 2, [128, 384], F32)
        ecf_r = self.rot("ecf2", 2, [128, 1], F32)
        Kf_r = self.rot("Kf2", 2, [128, 128], BF16)
        Kb_r = self.rot("Kb2", 2, [128, 128], BF16)
        ktb_r = self.rot("ktb2", 2, [128, 128], BF16)
        Amf_r = self.rot("Amf2", 2, [128, 4, 128], BF16)
        Amb_r = self.rot("Amb2", 2, [128, 4, 128], BF16)
        toks = {}
        ys = {}

        def get_tok(sc):
            if sc not in toks:
                toks[sc] = load_tok(sc, True)
            return toks[sc]

        def stage_P(n):
            sc, c = n // 4, n % 4
            cur = get_tok(sc)
            csl = slice(c * 128, (c + 1) * 128)
            spf = cur["sp"][:, c, 0:128]
            spb = cur["sp"][:, c, 128:256]
            X, xk = xb.next()
            R0 = ["cst", cur["spk"]]
            self.mm(X[:, 0:128], spf, triu, True, True, R=R0, W=[xk])
            self.mm(X[:, 128:256], spb, tril, True, True, R=R0, W=[xk])
            self.mm(X[:, 256:384], tril, spb, True, True, R=R0, W=[xk])
            self.mm(X[:, 384:385], spb, nsix, True, True, R=R0, W=[xk])
            E, ek = E_r.next()
            EI, eik = EI_r.next()
            ecf, ecfk = ecf_r.next()
            self.act(E[:], X[:, 0:256], AF.Exp, R=[xk], W=[ek])
            self.act(EI[:], X[:, 0:384], AF.Exp, R=[xk], W=[eik], scale=-1.0)
            self.act(ecf[:], X[:, 384:385], AF.Exp, R=[xk], W=[ecfk])
            Qf, qfk = Qf_r.next()
            Qb, qbk = Qb_r.next()
            for h in range(4):
                hm = cst[:, C_HM + h:C_HM + h + 1]
                self.stt(Qf[:, h, :], cur["q4"][:, csl], hm, E[:, 0:128], ALU.mult, ALU.mult,
                         R=[cur["qk"], ek, "cst"], W=[qfk])
                self.stt(Qb[:, h, :], cur["q4"][:, csl], hm, E[:, 128:256], ALU.mult, ALU.mult,
                         R=[cur["qk"], ek, "cst"], W=[qbk])
            Kf, kfk = Kf_r.next()
            Kb, kbk = Kb_r.next()
            ktilb, ktbk = ktb_r.next()
            self.tt(Kf[:], cur["k4"][:, csl], EI[:, 0:128], ALU.mult, R=[cur["kk"], eik], W=[kfk], eng="pool")
            self.tt(Kb[:], cur["k4"][:, csl], EI[:, 128:256], ALU.mult, R=[cur["kk"], eik], W=[kbk], eng="pool")
            self.tt(ktilb[:], cur["kt"][:, c, :], EI[:, 256:384], ALU.mult, R=[cur["ktk"], eik], W=[ktbk], eng="pool")
            for h in range(4):
                self.mm(afp[:, h * 128:(h + 1) * 128], Kf[:], Qf[:, h, :], True, True, R=[kfk, qfk], W=["pb4"])
            for h in range(4):
                self.mm(abp[:, h * 128:(h + 1) * 128], Kb[:], Qb[:, h, :], True, True, R=[kbk, qbk], W=["pb5"])
            Amf, amfk = Amf_r.next()
            Amb, ambk = Amb_r.next()
            self.tt(Amf[:], afp[:].rearrange("p (h t) -> p h t", h=4), maskF.unsqueeze(1).to_broadcast([128, 4, 128]),
                    ALU.mult, R=["pb4", "cst"], W=[amfk])
            self.tt(Amb[:], abp[:].rearrange("p (h t) -> p h t", h=4), maskB.unsqueeze(1).to_broadcast([128, 4, 128]),
                    ALU.mult, R=["pb5", "cst"], W=[ambk])
            return dict(cur=cur, c=c, sc=sc, csl=csl, ecf=ecf, ecfk=ecfk, Qf=Qf, qfk=qfk, Qb=Qb, qbk=qbk,
                        ktilb=ktilb, ktbk=ktbk, Amf=Amf, amfk=amfk, Amb=Amb, ambk=ambk)

        def stage_Q(n, P):
            cur, c, sc, csl = P["cur"], P["c"], P["sc"], P["csl"]
            if sc not in ys:
                ys[sc] = y_r.next()
            y4, y4k = ys[sc]
            O, ok = ob.next()
            for h in range(4):
                od = O[0:64, h * 128:(h + 1) * 128]
                vh = cur["vt"][:, c, h * 64:(h + 1) * 64]
                self.mm(od, vh, P["Amf"][:, h, :], True, False, R=[cur["vtk"], P["amfk"]], W=[ok])
                self.mm(od, vh, P["Amb"][:, h, :], False, False, R=[cur["vtk"], P["ambk"]], W=[ok])
                self.mm(od, Sf_all[:, n, h * 64:(h + 1) * 64], P["Qf"][:, h, :], False, False, R=["Sfall", P["qfk"]], W=[ok])
                self.mm(od, Sb_bf[:, h * 64:(h + 1) * 64], P["Qb"][:, h, :], False, True, R=["Sbbf", P["qbk"]], W=[ok])
            self.mm(kvp[:, 0:256], P["ktilb"][:], cur["vt"][:, c, :], True, True, R=[P["ktbk"], cur["vtk"]], W=["pb6"])
            self.tt(tmp[:], kvp[:, 0:256], Sb[:], ALU.add, R=["pb6", "Sb"], W=["tmp"])
            self.stt(Sb[:], tmp[:], P["ecf"][:, 0:1], bmask, ALU.mult, ALU.mult, R=["tmp", P["ecfk"], "cst"], W=["Sb"])
            self.cp(Sb_bf[:], Sb[:], R=["Sb"], W=["Sbbf"], eng="pool")
            self.act(osq[:], O[0:64, :], AF.Square, R=[ok], W=["osq"])
            self.mm(ssp[0:64, :], self.ones_bf[0:64, 0:64], osq[:], True, True, R=["osq", "ones"], W=["pb7"])
            self.act(rr[:], ssp[0:64, :], AF.Ln, R=["pb7", "cst"], W=["rr"], bias=self.eps_c[0:64, :], scale=1.0 / 64)
            self.act(rr[:], rr[:], AF.Exp, R=["rr"], W=["rr"], scale=-0.5)
            self.tt(tt_[:], O[0:64, :], rr[:], ALU.mult, R=[ok, "rr"], W=["tt"])
            self.stt(y4[:, :, csl], tt_[:].rearrange("p (h t) -> p h t", h=4), vec[0:64, 5:6], cur["go"][:, :, csl],
                     ALU.mult, ALU.mult, R=["tt", "vec", cur["gok"]], W=[y4k])
            if c == 0:
                self.dma(self.ymix[0:4, :, o + sc * 512:o + (sc + 1) * 512].rearrange("h p t -> p h t"), y4[:], R=[y4k],
                         q="pool")
                del toks[sc]

        Pn = stage_P(NCH - 1)
        for n in range(NCH - 1, -1, -1):
            Pnext = None
            if n - 1 >= 0:
                if (n - 1) % 4 == 1 and (n - 1) // 4 - 1 >= 0:
                    get_tok((n - 1) // 4 - 1)
                Pnext = stage_P(n - 1)
            stage_Q(n, Pn)
            Pn = Pnext
        self.S.phase_barrier(self.bar[0:1, 0:8], self.cst[0:1, 0:8])


C_EPS = 0
C_NSIX = 1
C_ONE = 2
C_HM = 4
C_BM = 8
C_TRIU = 264
C_TRIL = 392
C_MF = 520
C_MB = 648
C_ID = 776
CST_W = 904
VEC_W = 8
POOL_WINDOWS = (2, 4, 8, 16)


def host_consts():
    c = np.zeros((128, CST_W), np.float32)
    c[:, C_EPS] = EPS
    c[:, C_NSIX] = -1.0 / 16.0
    c[:, C_ONE] = 1.0
    p = np.arange(128)
    for h in range(4):
        c[:, C_HM + h] = (p // 32 == h) * (32.0 ** -0.5)
    f = np.arange(256)
    c[:, C_BM:C_BM + 256] = (f[None, :] // 64 == p[:, None] // 32)
    j = p[:, None]
    i = p[None, :]
    c[:, C_TRIU:C_TRIU + 128] = (j <= i) * (-1.0 / 16.0)
    c[:, C_TRIL:C_TRIL + 128] = (j >= i) * (-1.0 / 16.0)
    c[:, C_MF:C_MF + 128] = (j <= i)
    c[:, C_MB:C_MB + 128] = (j > i)
    c[:, C_ID:C_ID + 128] = (j == i)
    return c


def host_rope(smax):
    half = 16
    inv = (10000.0 ** (-np.arange(half, dtype=np.float32) / half)).astype(np.float32)
    pos = np.arange(smax, dtype=np.float32)
    ang = (pos[None, :] * np.concatenate([inv, inv])[:, None]).astype(np.float32)
    return np.ascontiguousarray(np.stack([np.cos(ang), np.sin(ang)], axis=1).astype(np.float32))


def host_icnt(S):
    t = np.arange(S)
    out = np.zeros((256, S), np.float32)
    for gi, w in enumerate(POOL_WINDOWS):
        lo = np.clip(t - w // 2, 0, S)
        hi = np.clip(t + w // 2, 0, S)
        out[gi * 64:(gi + 1) * 64, :] = (1.0 / (hi - lo).astype(np.float32))[None, :]
    return out


def build(nc, SEQS, dbg=False, stages="ABCD", nlayers=NL, mixers="pdca"):
    b = Builder(nc, SEQS, dbg)
    b.declare()
    b.prologue()
    for l in range(nlayers):
        last = (l == nlayers - 1)
        if "A" in stages:
            b.ffn_sweep(l, 0, b.xT if l == 0 else b.xs, b.xs if "D" in stages or True else b.yT)
        if "B" in stages:
            b.proj_sweep(l)
        if "C" in stages:
            for sq in range(b.NS):
                if "p" in mixers:
                    b.pool_mixer(l, sq)
                if "d" in mixers:
                    b.mla_mixer(l, sq)
                if "c" in mixers:
                    b.na_mixer(l, sq)
                if "a" in mixers:
                    b.gla_mixer(l, sq)
        if "D" in stages:
            b.ffn_sweep(l, 1, b.xs, b.yT if last else b.xs, pre_wout=("C" in stages), final_norm=last, final=last)
    b.S.emit()
    return b


def host_shared(w, seqs):
    m = {}
    m["ada_w"] = np.ascontiguousarray(w["ada_w"], np.float32)
    m["adabT"] = np.ascontiguousarray(w["ada_b"].reshape(NL, 72, 128).transpose(0, 2, 1), np.float32)
    nrm = np.zeros((128, (NL * 3 + 1) * 8), np.float32)
    for l in range(NL):
        for k, name in enumerate(("norm_ffn1", "norm_mix", "norm_ffn2")):
            nrm[:, (l * 3 + k) * 8:(l * 3 + k + 1) * 8] = w[name][l].reshape(8, 128).T
    nrm[:, NL * 3 * 8:] = w["final_norm"].reshape(8, 128).T
    m["nrm"] = nrm
    for k in ("ffn1_wg", "ffn1_wu", "ffn1_wd", "ffn2_wg", "ffn2_wu", "ffn2_wd", "w_in", "w_out"):
        m[k] = np.ascontiguousarray(w[k], np.float32)
    m["cst"] = host_consts()
    vec = np.zeros((128, NL * VEC_W), np.float32)
    wgk = np.zeros((NL, 33, 256), np.float32)
    poolw = np.zeros((NL, 2, 128, 128), np.float32)
    for l in range(NL):
        vec[:, l * VEC_W + 0:l * VEC_W + 2] = w["mla_qnorm"][l].reshape(2, 128).T
        vec[:, l * VEC_W + 2] = w["mla_kvnorm"][l]
        vec[:, l * VEC_W + 3:l * VEC_W + 5] = w["pool_scale"][l].reshape(2, 128).T
        vec[0:64, l * VEC_W + 5] = w["gla_norm"][l]
        vec[64:128, l * VEC_W + 5] = w["gla_norm"][l]
        wgk[l, 0:16, 0:128] = w["gla_wgk_f"][l]
        wgk[l, 16:32, 128:256] = w["gla_wgk_b"][l]
        wgk[l, 32, 0:128] = w["gla_bgk_f"][l]
        wgk[l, 32, 128:256] = w["gla_bgk_b"][l]
        for c in range(2):
            poolw[l, c, 0:64, 0:64] = w["pool_w"][l, 2 * c]
            poolw[l, c, 64:128, 64:128] = w["pool_w"][l, 2 * c + 1]
    m["vec"] = vec
    m["wgk"] = wgk
    m["poolw"] = poolw
    m["mla_wuq"] = np.ascontiguousarray(w["mla_wuq"], np.float32)
    m["mla_wukv"] = np.ascontiguousarray(w["mla_wukv"], np.float32)
    m["natab"] = host_natab(w["na_rpb"], seqs)
    m["rope"] = host_rope(max(seqs))
    for Sq in sorted(set(seqs)):
        m["icnt%d" % Sq] = host_icnt(Sq)
    return m


def host_natab(rpb, seqs):
    Rr = 16
    P = Rr // 2
    variants = [(3, [1, 2, 3, 4, 5]), (0, [0, 1, 2, 3]), (1, [0, 1, 2, 3]), (P - 2, [P - 4, P - 3, P - 2, P - 1]),
                (P - 1, [P - 4, P - 3, P - 2, P - 1])]
    out = np.full((NL, 4, 21, 128, 128), NEG, np.float32)
    cq = np.arange(64)
    ck = np.arange(64)
    qstart = np.clip(cq - 8, 0, 48)
    col_ok = (ck[None, :] >= qstart[:, None]) & (ck[None, :] < qstart[:, None] + 16)
    dcol = ck[None, :] - cq[:, None] + 15
    t = 0
    for (i, js) in variants:
        for j in js:
            for rq in range(2):
                qrow = 2 * i + rq
                r0 = min(max(qrow - 4, 0), Rr - 8)
                for rk in range(2):
                    krow = 2 * j + rk
                    if not (r0 <= krow < r0 + 8):
                        continue
                    drow = krow - qrow + 7
                    blk = np.where(col_ok[None, None], rpb[:, :, drow, :][:, :, np.clip(dcol, 0, 30)], NEG)
                    out[:, :, t, rk * 64:(rk + 1) * 64, rq * 64:(rq + 1) * 64] = np.swapaxes(blk, -1, -2)
            t += 1
    assert t == 21
    return out


def host_core(xs, cs):
    xT = np.ascontiguousarray(np.concatenate(xs, axis=0).T, np.float32)
    NS = len(xs)
    cT = np.ascontiguousarray(np.asarray(cs, np.float32).T.reshape(8, 128, NS).transpose(1, 0, 2))
    return {"xT": xT, "cT": cT}


N_CORES = 8
SEQS_FULL = [2048, 2048, 2048, 2048, 8192]
_WNAMES = ("ada_w", "ada_b", "norm_ffn1", "ffn1_wg", "ffn1_wu", "ffn1_wd", "norm_mix", "w_in", "gla_wgk_f",
           "gla_bgk_f", "gla_wgk_b", "gla_bgk_b", "gla_norm", "pool_w", "pool_scale", "na_rpb", "mla_qnorm",
           "mla_wuq", "mla_kvnorm", "mla_wukv", "w_out", "norm_ffn2", "ffn2_wg", "ffn2_wu", "ffn2_wd", "final_norm")


def kernel(x_prompt, x_sample, c_prompt, c_sample, **weights):
    w = {k: np.asarray(weights[k], np.float32) for k in _WNAMES}
    x_prompt = np.asarray(x_prompt, np.float32)
    x_sample = np.asarray(x_sample, np.float32)
    c_prompt = np.asarray(c_prompt, np.float32)
    c_sample = np.asarray(c_sample, np.float32)
    nc = bass.Bass("TRN2", target_bir_lowering=False)
    b = build(nc, SEQS_FULL)
    shared = host_shared(w, SEQS_FULL)
    shared = {k: v for k, v in shared.items() if k in b.dram}
    in_maps = []
    for c in range(N_CORES):
        xs = [x_prompt[4 * c + i] for i in range(4)] + [x_sample[c]]
        cs = np.concatenate([c_prompt[4 * c:4 * c + 4], c_sample[c:c + 1]], axis=0)
        m = dict(shared)
        m.update(host_core(xs, cs))
        in_maps.append(m)
    res = run_bass_kernel_spmd(nc, in_maps, core_ids=list(range(N_CORES)))
    y_prompt = np.empty((32, 2048, D), np.float32)
    y_sample = np.empty((8, 8192, D), np.float32)
    for c in range(N_CORES):
        yT = np.asarray(res.results[c]["yT"], np.float32)
        for i in range(4):
            y_prompt[4 * c + i] = yT[:, i * 2048:(i + 1) * 2048].T
        y_sample[c] = yT[:, 8192:].T
    return (y_prompt, y_sample)
```

```python
import numpy as np
from contextlib import ExitStack
import concourse.bass as bass
import concourse.mybir as mybir
from concourse.bass_utils import run_bass_kernel_spmd

F32 = mybir.dt.float32
BF16 = mybir.dt.bfloat16
AF = mybir.ActivationFunctionType
ALU = mybir.AluOpType

D = 1024
FF = 2816
NL = 2
PW = 2240
EPS = 1e-6
MLA_SCALE = 96.0 ** -0.5
NEG = -30000.0
ENGS = ("pe", "act", "dve", "pool", "sp")
DMAQ = ("sp", "pool", "act")


class Op:
    __slots__ = ("eng", "fn", "deps", "is_dma", "sig", "sem", "val", "idx")

    def __init__(self, eng, fn, is_dma):
        self.eng = eng
        self.fn = fn
        self.is_dma = is_dma
        self.deps = []
        self.sig = False
        self.sem = None
        self.val = 0
        self.idx = 0


class Sched:
    NPOOL = 24

    def __init__(self, nc):
        self.nc = nc
        self.ops = []
        self.lastw = {}
        self.readers = {}
        self.finals = []
        self.barrier = None
        self.bar_seen = set()
        self.last_eng = {}

    def op(self, eng, fn, R=(), W=(), dma=False, final=False):
        o = Op(eng, fn, dma)
        o.idx = len(self.ops)
        deps = {}
        for k in R:
            w = self.lastw.get(k)
            if w is not None:
                deps[w.idx] = w
        for k in W:
            w = self.lastw.get(k)
            if w is not None:
                deps[w.idx] = w
            for r in self.readers.get(k, ()):
                deps[r.idx] = r
        if self.barrier is not None and eng not in self.bar_seen:
            self.bar_seen.add(eng)
            deps[self.barrier.idx] = self.barrier
        for d in deps.values():
            if d.eng == "pe" and eng == "pe" and not d.is_dma and not dma:
                continue
            o.deps.append(d)
            d.sig = True
        for k in W:
            self.lastw[k] = o
            self.readers[k] = []
        for k in R:
            lst = self.readers.setdefault(k, [])
            if not dma:
                lst[:] = [r for r in lst if r.is_dma or r.eng != eng]
            lst.append(o)
        self.ops.append(o)
        self.last_eng[eng] = o
        if final:
            o.sig = True
            self.finals.append(o)
        return o

    def dma(self, eng, out, in_, R=(), W=(), final=False):
        return self.op(eng, lambda e: e.dma_start(out=out, in_=in_), R=R, W=W, dma=True, final=final)

    def phase_barrier(self, scratch_dst, scratch_src):
        o = Op("sp", lambda e: e.dma_start(out=scratch_dst, in_=scratch_src), True)
        o.idx = len(self.ops)
        seen = {}
        for p in self.ops:
            if p.is_dma:
                seen[p.idx] = p
        for e, p in self.last_eng.items():
            seen[p.idx] = p
        start = self.barrier.idx if self.barrier is not None else -1
        for p in seen.values():
            if p.idx > start:
                o.deps.append(p)
                p.sig = True
        o.sig = True
        self.ops.append(o)
        self.last_eng["sp"] = o
        self.barrier = o
        self.bar_seen = {"sp"}
        self.lastw = {}
        self.readers = {}

    def emit(self):
        nc = self.nc
        with ExitStack() as st:
            esem = {e: st.enter_context(nc.semaphore("s_" + e)) for e in ENGS}
            dsem = {e: [st.enter_context(nc.semaphore("d_%s_%d" % (e, i))) for i in range(self.NPOOL)]
                    for e in DMAQ}
            ecount = {e: 0 for e in ENGS}
            dcount = {e: [0] * self.NPOOL for e in dsem}
            dnext = {e: 0 for e in dsem}
            dprev = {e: [None] * self.NPOOL for e in dsem}
            per_eng = {e: [] for e in ENGS}
            for o in self.ops:
                per_eng[o.eng].append(o)
                if o.is_dma:
                    i = dnext[o.eng]
                    dnext[o.eng] = (i + 1) % self.NPOOL
                    prev = dprev[o.eng][i]
                    if prev is not None:
                        o.deps.append(prev)
                    dprev[o.eng][i] = o
                    dcount[o.eng][i] += 16
                    o.sem = dsem[o.eng][i]
                    o.val = dcount[o.eng][i]
                    o.sig = True
                elif o.sig:
                    ecount[o.eng] += 1
                    o.sem = esem[o.eng]
                    o.val = ecount[o.eng]
            self.nwaits = 0
            blk = st.enter_context(nc.Block())

            def run(engname, e):
                waited = {}
                for o in per_eng[engname]:
                    need = {}
                    for d in o.deps:
                        key = id(d.sem)
                        if waited.get(key, 0) >= d.val:
                            continue
                        if key not in need or need[key][1] < d.val:
                            need[key] = (d.sem, d.val)
                    for key, (sem, val) in need.items():
                        e.wait_ge(sem, val)
                        waited[key] = val
                        self.nwaits += 1
                    ins = o.fn(e)
                    if o.sig:
                        ins.then_inc(o.sem, 16 if o.is_dma else 1)
                if engname == "sp":
                    for o in self.finals:
                        key = id(o.sem)
                        if waited.get(key, 0) < o.val:
                            e.wait_ge(o.sem, o.val)
                            waited[key] = o.val

            @blk.tensor
            def _(e):
                run("pe", e)

            @blk.scalar
            def _(e):
                run("act", e)

            @blk.vector
            def _(e):
                run("dve", e)

            @blk.gpsimd
            def _(e):
                run("pool", e)

            @blk.sync
            def _(e):
                run("sp", e)


class Rot:
    def __init__(self, name, aps, keys=None):
        self.name = name
        self.aps = aps
        self.keys = keys if keys is not None else [(name, i) for i in range(len(aps))]
        self.i = -1

    def next(self):
        self.i = (self.i + 1) % len(self.aps)
        return self.aps[self.i], self.keys[self.i]

    def cur(self):
        return self.aps[self.i], self.keys[self.i]


SB_LO = 16512
SB_HI = 229312


class Builder:
    def __init__(self, nc, SEQS, dbg=False):
        self.nc = nc
        self.S = Sched(nc)
        self.SEQS = list(SEQS)
        self.NT = sum(SEQS)
        self.NS = len(SEQS)
        self.off = [int(v) for v in np.cumsum([0] + list(SEQS))]
        self.dbg = dbg
        self.sbp = SB_LO
        self.uid = 0
        self.dram = {}
        self.outs = []

    def sb(self, shape, dt, name=None):
        nbytes = int(np.prod(shape[1:])) * (4 if dt == F32 else 2)
        nbytes = (nbytes + 63) // 64 * 64
        self.uid += 1
        t = self.nc.alloc_sbuf_tensor_at("%s_%d" % (name or "t", self.uid), list(shape), dt, offset=self.sbp)
        self.sbp += nbytes
        assert self.sbp <= SB_HI, ("SBUF overflow", name, self.sbp)
        return t

    def rot(self, name, n, shape, dt):
        self.uid += 1
        return Rot("%s%d" % (name, self.uid), [self.sb(shape, dt, name) for _ in range(n)])

    def pbrot(self, idxs):
        return Rot("pb", [self.pb[i] for i in idxs], keys=["pb%d" % i for i in idxs])

    def din(self, name, shape, dt=F32):
        t = self.nc.dram_tensor(name, list(shape), dt, kind="ExternalInput").ap()
        self.dram[name] = t
        return t

    def dscr(self, name, shape, dt):
        kind = "ExternalOutput" if self.dbg else "Internal"
        t = self.nc.dram_tensor(name, list(shape), dt, kind=kind).ap()
        self.dram[name] = t
        return t

    def seq_of(self, tok):
        for s in range(self.NS):
            if self.off[s] <= tok < self.off[s + 1]:
                return s
        raise ValueError

    def mm(self, out, lhsT, rhs, start, stop, R, W, skip=False):
        if skip:
            self.S.op("pe", lambda e: e.matmul(out, lhsT, rhs, start=start, stop=stop, skip_group_check=True),
                      R=R, W=W)
        else:
            self.S.op("pe", lambda e: e.matmul(out, lhsT, rhs, start=start, stop=stop), R=R, W=W)

    def act(self, out, in_, func, R, W, bias=None, scale=None):
        kw = {}
        if bias is not None:
            kw["bias"] = bias
        if scale is not None:
            kw["scale"] = scale
        self.S.op("act", lambda e: e.activation(out=out, in_=in_, func=func, **kw), R=R, W=W)

    def tt(self, out, in0, in1, op, R, W, eng="dve"):
        self.S.op(eng, lambda e: e.tensor_tensor(out=out, in0=in0, in1=in1, op=op), R=R, W=W)

    def ts(self, out, in0, s1, op0, R, W, s2=None, op1=None, eng="dve"):
        if op1 is None:
            self.S.op(eng, lambda e: e.tensor_scalar(out=out, in0=in0, scalar1=s1, scalar2=None, op0=op0), R=R, W=W)
        else:
            self.S.op(eng, lambda e: e.tensor_scalar(out=out, in0=in0, scalar1=s1, scalar2=s2, op0=op0, op1=op1),
                      R=R, W=W)

    def stt(self, out, in0, scalar, in1, op0, op1, R, W, eng="dve"):
        self.S.op(eng, lambda e: e.scalar_tensor_tensor(out=out, in0=in0, scalar=scalar, in1=in1, op0=op0, op1=op1),
                  R=R, W=W)

    def cp(self, out, in_, R, W, eng="dve"):
        self.S.op(eng, lambda e: e.tensor_copy(out=out, in_=in_), R=R, W=W)

    def recip(self, out, in_, R, W):
        self.S.op("dve", lambda e: e.reciprocal(out=out, in_=in_), R=R, W=W)

    def memset(self, ap, val, W, eng="dve"):
        self.S.op(eng, lambda e: e.memset(ap, val), W=W)

    def dma(self, out, in_, R=(), W=(), q="sp", final=False):
        self.S.dma(q, out, in_, R=R, W=W, final=final)

    def declare(self):
        NT, NS = self.NT, self.NS
        d = self.din
        self.xT = d("xT", [D, NT])
        self.cT = d("cT", [128, 8, NS])
        self.ada_w = d("ada_w", [NL, D, 9 * D])
        self.adabT = d("adabT", [NL, 128, 72])
        self.nrm = d("nrm", [128, (NL * 3 + 1) * 8])
        self.wg = [d("ffn1_wg", [NL, D, FF]), d("ffn2_wg", [NL, D, FF])]
        self.wu = [d("ffn1_wu", [NL, D, FF]), d("ffn2_wu", [NL, D, FF])]
        self.wd = [d("ffn1_wd", [NL, FF, D]), d("ffn2_wd", [NL, FF, D])]
        self.w_in = d("w_in", [NL, D, PW])
        self.w_out = d("w_out", [NL, D, D])
        self.cst = d("cst", [128, CST_W])
        self.vec = d("vec", [128, NL * VEC_W])
        self.wgk = d("wgk", [NL, 33, 256])
        self.w_uq = d("mla_wuq", [NL, 256, 384])
        self.w_ukv = d("mla_wukv", [NL, 128, 512])
        self.rope = d("rope", [32, 2, max(self.SEQS)])
        self.poolw = d("poolw", [NL, 2, 128, 128])
        self.natab = d("natab", [NL, 4, 21, 128, 128])
        self.icnt = {}
        for Sq in sorted(set(self.SEQS)):
            self.icnt[Sq] = d("icnt%d" % Sq, [256, Sq])
        sc = self.dscr
        self.qTa = sc("qTa", [128, NT], BF16)
        self.kTa = sc("kTa", [128, NT], BF16)
        self.ktok = sc("ktok", [NT, 128], BF16)
        self.vtok = sc("vtok", [NT, 256], BF16)
        self.sptok = sc("sptok", [NT, 256], F32)
        self.goT = sc("goT", [4, 64, NT], BF16)
        self.uT = sc("uT", [256, NT], F32)
        self.qTc = sc("qTc", [4, 64, NT], BF16)
        self.kTc = sc("kTc", [4, 64, NT], BF16)
        self.vtokc = sc("vtokc", [NT, 260], BF16)
        self.qTd = sc("qTd", [4, 96, NT], BF16)
        self.kTd = sc("kTd", [4, 96, NT], BF16)
        self.vtokd = sc("vtokd", [NT, 260], BF16)
        self.yT = self.nc.dram_tensor("yT", [D, NT], F32, kind="ExternalOutput").ap()
        self.xs = self.dscr("xs", [D, NT], F32)
        self.ymix = self.dscr("ymix", [16, 64, NT], BF16)
        self.bar = self.nc.dram_tensor("bar", [2, 64], F32, kind="Internal").ap()
        self.pb = [self.nc.alloc_psum_tensor("pb%d" % i, [128, 512], F32) for i in range(8)]

    def prologue(self):
        S = self.S
        NS = self.NS
        self.cst_sb = self.sb([128, CST_W], F32, "cst")
        self.dma(self.cst_sb[:], self.cst, W=["cst"])
        self.nrm_sb = self.sb([128, (NL * 3 + 1) * 8], F32, "nrm")
        self.dma(self.nrm_sb[:], self.nrm, W=["nrm"])
        self.vec_sb = self.sb([128, NL * VEC_W], F32, "vec")
        self.dma(self.vec_sb[:], self.vec, W=["vec"])
        self.ones_f = self.sb([128, 64], F32, "onesf")
        self.memset(self.ones_f[:], 1.0, W=["onesf"])
        self.ident_bf = self.sb([128, 128], BF16, "ident")
        self.cp(self.ident_bf[:], self.cst_sb[:, C_ID:C_ID + 128], R=["cst"], W=["ident"])
        self.ones_bf = self.sb([128, 128], BF16, "ones")
        self.memset(self.ones_bf[:], 1.0, W=["ones"])
        self.eps_c = self.cst_sb[:, C_EPS:C_EPS + 1]
        self.one_c = self.cst_sb[:, C_ONE:C_ONE + 1]
        self.mod = [self.sb([128, 72, NS], F32, "mod%d" % l) for l in range(NL)]
        self.Gn = [self.sb([128, 3, 8, NS], F32, "Gn%d" % l) for l in range(NL)]
        self.gate = [self.sb([128, 3, 8, NS], F32, "gate%d" % l) for l in range(NL)]
        persist_end = self.sbp
        csil = self.sb([128, 8, NS], F32, "csil")
        adab = self.sb([128, NL, 72], F32, "adab")
        self.dma(csil[:], self.cT, W=["csil"])
        self.dma(adab[:], self.adabT.rearrange("l p j -> p l j"), W=["adab"])
        self.act(csil[:], csil[:], AF.Silu, R=["csil"], W=["csil"])
        wrot = self.rot("adaw", 2, [128, 8, 512], F32)
        ps = self.pb[0]
        for l in range(NL):
            psv = ps[:, 0:72 * NS].rearrange("p (j s) -> p j s", s=NS)
            for pc in range(18):
                wt, wk = wrot.next()
                src = self.ada_w[l].rearrange("(kc p) c -> p kc c", p=128)[:, :, pc * 512:(pc + 1) * 512]
                self.dma(wt[:], src, W=[wk], q=("sp" if pc % 2 == 0 else "act"))
                for jj in range(4):
                    j = pc * 4 + jj
                    for kc in range(8):
                        self.mm(psv[:, j, :], wt[:, kc, jj * 128:(jj + 1) * 128], csil[:, kc, :],
                                kc == 0, kc == 7, R=[wk, "csil"], W=["pb0"])
            self.tt(self.mod[l][:], psv, adab[:, l, :].unsqueeze(2).to_broadcast([128, 72, NS]), ALU.add,
                    R=["pb0", "adab"], W=["mod%d" % l])
            m = self.mod[l][:].rearrange("p (i oc) s -> p i oc s", i=9)
            for k, (isc, igate, gmul) in enumerate(((1, 2, 0.5), (4, 5, 1.0), (7, 8, 0.5))):
                nv = self.nrm_sb[:, (l * 3 + k) * 8:(l * 3 + k + 1) * 8]
                self.ts(self.Gn[l][:, k], m[:, isc], 1.0, ALU.add, R=["mod%d" % l], W=["Gn%d" % l])
                self.tt(self.Gn[l][:, k], self.Gn[l][:, k], nv.unsqueeze(2).to_broadcast([128, 8, NS]), ALU.mult,
                        R=["Gn%d" % l, "nrm"], W=["Gn%d" % l])
                self.ts(self.gate[l][:, k], m[:, igate], 1.0, ALU.add, R=["mod%d" % l], W=["gate%d" % l],
                        s2=gmul, op1=ALU.mult)
        self.sbp = persist_end
        self.persist_end = persist_end
        S.phase_barrier(self.bar[0:1, 0:8], self.cst[0:1, 0:8])

    def shift(self, l, k, kc, s):
        i = (0, 3, 6)[k]
        return self.mod[l][:, i * 8 + kc, s:s + 1]

    def norm_tile(self, x, xk, T, l, k, s, sq, sqk, t, tk, h, hk, ssb, rs, rsk):
        if isinstance(sq, Rot):
            for kc in range(8):
                sq1, sq1k = sq.next()
                self.act(sq1[:], x[:, kc, :], AF.Square, R=[xk], W=[sq1k])
                self.mm(ssb[:, 0:T], self.ones_bf[:], sq1[:], kc == 0, kc == 7, R=[sq1k, "ones"], W=["pb6"])
        else:
            self.act(sq[:], x[:], AF.Square, R=[xk], W=[sqk])
            for kc in range(8):
                self.mm(ssb[:, 0:T], self.ones_bf[:], sq[:, kc, :], kc == 0, kc == 7, R=[sqk, "ones"], W=["pb6"])
        self.act(rs[:], ssb[:, 0:T], AF.Sqrt, R=["pb6", "cst"], W=[rsk], bias=self.eps_c, scale=1.0 / D)
        self.recip(rs[:], rs[:], R=[rsk], W=[rsk])
        for kc in range(8):
            tt_, ttk = t.next()
            self.tt(tt_[:], x[:, kc, :], rs[:], ALU.mult, R=[xk, rsk], W=[ttk])
            self.act(h[:, kc, :], tt_[:], AF.Identity, R=[ttk, "Gn%d" % l, "mod%d" % l], W=[hk],
                     bias=self.shift(l, k, kc, s), scale=self.Gn[l][:, k, kc, s:s + 1])

    def ffn_sweep(self, l, which, src, dst, pre_wout=False, final_norm=False, final=False):
        S = self.S
        T = 256
        NT = self.NT
        k = 0 if which == 0 else 2
        self.sbp = self.persist_end
        wg = self.sb([128, 8, FF], BF16, "wg")
        wu = self.sb([128, 8, FF], BF16, "wu")
        wd = self.sb([128, 22, D], BF16, "wd")
        self.dma(wg[:], self.wg[which][l].rearrange("(kc p) f -> p kc f", p=128), W=["wg"], q="pool")
        self.dma(wu[:], self.wu[which][l].rearrange("(kc p) f -> p kc f", p=128), W=["wu"], q="pool")
        self.dma(wd[:], self.wd[which][l].rearrange("(fc p) d -> p fc d", p=128), W=["wd"], q="pool")
        if pre_wout:
            wo = self.sb([128, 8, D], BF16, "wo")
            self.dma(wo[:], self.w_out[l].rearrange("(kc p) d -> p kc d", p=128), W=["wo"], q="pool")
            yrot = self.rot("yt", 2, [128, 8, T], BF16)
        xrot = self.rot("x", 3, [128, 8, T], F32)
        sq = self.rot("sq", 3, [128, T], BF16)
        t = self.rot("t", 2, [128, T], F32)
        hrot = self.rot("h", 2, [128, 8, T], BF16)
        rsrot = self.rot("rs", 2, [128, T], F32)
        sgrot = self.rot("sg", 2, [128, T], BF16)
        arot = self.rot("a", 3, [128, T], BF16)
        acc = [self.pb[i] for i in range(4)]
        gurot = self.pbrot([4, 5])
        ssb = self.pb[6]

        def accv(oc):
            return acc[oc // 2][:, (oc % 2) * 256:(oc % 2) * 256 + T], "pb%d" % (oc // 2)

        ntiles = NT // T
        xsrc = src.rearrange("(kc p) t -> p kc t", p=128)
        xdst = dst.rearrange("(kc p) t -> p kc t", p=128)
        ysrc = self.ymix.rearrange("(kc two) f t -> (two f) kc t", two=2)
        state = {}

        def stage_load(i):
            x, xk = xrot.next()
            self.dma(x[:], xsrc[:, :, i * T:(i + 1) * T], W=[xk], q="sp")
            state[i] = dict(x=x, xk=xk)
            if pre_wout:
                y, yk = yrot.next()
                self.dma(y[:], ysrc[:, :, i * T:(i + 1) * T], W=[yk], q="sp")
                state[i].update(y=y, yk=yk)

        def norm_steps(i):
            st = state[i]
            s = self.seq_of(i * T)
            x, xk = st["x"], st["xk"]
            steps = []
            if pre_wout:
                y, yk = st["y"], st["yk"]
                slots = [(self.pb[6][:, 256:256 + T], "pb6"), (self.pb[7][:, 0:T], "pb7")]

                def pre(oc):
                    av, ak = slots[oc % 2]
                    for kc in range(8):
                        self.mm(av, wo[:, kc, oc * 128:(oc + 1) * 128], y[:, kc, :], kc == 0, kc == 7,
                                R=[yk, "wo"], W=[ak])
                    self.stt(x[:, oc, :], av, self.gate[l][:, 1, oc, s:s + 1], x[:, oc, :], ALU.mult, ALU.add,
                             R=[ak, xk, "gate%d" % l], W=[xk])
                for o2 in range(4):
                    steps.append(lambda o2=o2: (pre(2 * o2), pre(2 * o2 + 1)))
            h, hk = hrot.next()
            rs, rsk = rsrot.next()
            st.update(h=h, hk=hk)

            def n2():
                for kc in range(8):
                    sq1, sq1k = sq.next()
                    self.tt(sq1[:], x[:, kc, :], x[:, kc, :], ALU.mult, R=[xk], W=[sq1k], eng="pool")
                    self.mm(ssb[:, 0:T], self.ones_bf[:], sq1[:], kc == 0, kc == 7, R=[sq1k, "ones"], W=["pb6"])

            def n3():
                self.act(rs[:], ssb[:, 0:T], AF.Sqrt, R=["pb6", "cst"], W=[rsk], bias=self.eps_c, scale=1.0 / D)
                self.recip(rs[:], rs[:], R=[rsk], W=[rsk])

            def n4(kc):
                tt_, ttk = t.next()
                self.tt(tt_[:], x[:, kc, :], rs[:], ALU.mult, R=[xk, rsk], W=[ttk])
                self.ts(h[:, kc, :], tt_[:], self.Gn[l][:, k, kc, s:s + 1], ALU.mult,
                        R=[ttk, "Gn%d" % l, "mod%d" % l], W=[hk], s2=self.shift(l, k, kc, s), op1=ALU.add)
            steps.append(n2)
            steps.append(None)
            steps.append(n3)
            steps.append(None)
            for kc in range(8):
                steps.append(lambda kc=kc: n4(kc))
            return steps

        def stage_norm(i):
            for f in norm_steps(i):
                if f is not None:
                    f()

        def stage_body(i):
            st = state[i]
            s = self.seq_of(i * T)
            x, xk, h, hk = st["x"], st["xk"], st["h"], st["hk"]

            def gu(fc):
                gub, gk = gurot.next()
                uk = gk
                g = gub[:, 0:T]
                u = gub[:, 256:256 + T]
                for kc in range(8):
                    self.mm(g, wg[:, kc, fc * 128:(fc + 1) * 128], h[:, kc, :], kc == 0, kc == 7, R=[hk, "wg"], W=[gk])
                for kc in range(8):
                    self.mm(u, wu[:, kc, fc * 128:(fc + 1) * 128], h[:, kc, :], kc == 0, kc == 7, R=[hk, "wu"], W=[uk])
                return g, gk, u, uk

            nsteps = norm_steps(i + 1) if i + 1 < ntiles else []
            pend = gu(0)
            for fc in range(22):
                g, gk, u, uk = pend
                if fc + 1 < 22:
                    pend = gu(fc + 1)
                if fc >= 1 and nsteps:
                    f = nsteps.pop(0)
                    if f is not None:
                        f()
                sg, sgk = sgrot.next()
                a, ak = arot.next()
                self.act(sg[:], g, AF.Silu, R=[gk], W=[sgk])
                self.tt(a[:], u, sg[:], ALU.mult, R=[uk, sgk], W=[ak])
                for oc in range(8):
                    av, avk = accv(oc)
                    self.mm(av, wd[:, fc, oc * 128:(oc + 1) * 128], a[:], fc == 0 and oc % 2 == 0, fc == 21,
                            R=[ak, "wd"], W=[avk], skip=True)
            while nsteps:
                f = nsteps.pop(0)
                if f is not None:
                    f()
            for oc in range(8):
                av, avk = accv(oc)
                self.stt(x[:, oc, :], av, self.gate[l][:, k, oc, s:s + 1], x[:, oc, :], ALU.mult, ALU.add,
                         R=[avk, xk, "gate%d" % l], W=[xk])
            if final_norm:
                rs, rsk = rsrot.next()
                for kc in range(8):
                    sq1, sq1k = sq.next()
                    self.act(sq1[:], x[:, kc, :], AF.Square, R=[xk], W=[sq1k])
                    self.mm(ssb[:, 0:T], self.ones_bf[:], sq1[:], kc == 0, kc == 7, R=[sq1k, "ones"], W=["pb6"])
                self.act(rs[:], ssb[:, 0:T], AF.Sqrt, R=["pb6", "cst"], W=[rsk], bias=self.eps_c, scale=1.0 / D)
                self.recip(rs[:], rs[:], R=[rsk], W=[rsk])
                fn = self.nrm_sb[:, NL * 3 * 8:NL * 3 * 8 + 8]
                for kc in range(8):
                    self.stt(x[:, kc, :], x[:, kc, :], fn[:, kc:kc + 1], rs[:], ALU.mult, ALU.mult,
                             R=[xk, rsk, "nrm"], W=[xk])
            self.dma(xdst[:, :, i * T:(i + 1) * T], x[:], R=[xk], q="act", final=final)
            del state[i]

        stage_load(0)
        if ntiles > 1:
            stage_load(1)
        stage_norm(0)
        for i in range(ntiles):
            if i + 2 < ntiles:
                stage_load(i + 2)
            stage_body(i)
        S.phase_barrier(self.bar[0:1, 0:8], self.cst[0:1, 0:8])


    def proj_sweep(self, l):
        S = self.S
        T = 512
        NT = self.NT
        self.sbp = self.persist_end
        cst = self.cst_sb
        vec = self.vec_sb[:, l * VEC_W:(l + 1) * VEC_W]
        win = self.sb([128, 8, PW], BF16, "win")
        self.dma(win[:], self.w_in[l].rearrange("(kc p) c -> p kc c", p=128), W=["win"], q="pool")
        wkpe = self.sb([128, 8, 96], BF16, "wkpe")
        wkrh = self.sb([128, 8, 96], BF16, "wkrh")
        self.memset(wkpe[:], 0.0, W=["wkpe"])
        self.memset(wkrh[:], 0.0, W=["wkrh"])
        wsrc = self.w_in[l].rearrange("(kc p) c -> p kc c", p=128)
        self.dma(wkpe[:, :, 64:96], wsrc[:, :, 2208:2240], W=["wkpe"], q="pool")
        self.dma(wkrh[:, :, 64:80], wsrc[:, :, 2224:2240], W=["wkrh"], q="pool")
        self.dma(wkrh[:, :, 80:96], wsrc[:, :, 2208:2224], W=["wkrh"], q="pool")
        self.ts(wkrh[:, :, 64:80], wkrh[:, :, 64:80], -1.0, ALU.mult, R=["wkrh"], W=["wkrh"])
        wuq_f = self.sb([128, 2, 384], F32, "wuqf")
        self.dma(wuq_f[:], self.w_uq[l].rearrange("(kc p) c -> p kc c", p=128), W=["wuqf"])
        wuq = self.sb([128, 2, 384], BF16, "wuq")
        wuqr = self.sb([128, 2, 384], BF16, "wuqr")
        for kc in range(2):
            self.ts(wuq[:, kc, :], wuq_f[:, kc, :], vec[:, kc:kc + 1], ALU.mult, R=["wuqf", "vec"], W=["wuq"])
            self.cp(wuqr[:, kc, :], wuq[:, kc, :], R=["wuq"], W=["wuqr"])
            v4 = wuq[:, kc, :].rearrange("p (h c) -> p h c", c=96)
            r4 = wuqr[:, kc, :].rearrange("p (h c) -> p h c", c=96)
            self.ts(r4[:, :, 64:80], v4[:, :, 80:96], -1.0, ALU.mult, R=["wuq", "wuqr"], W=["wuqr"])
            self.cp(r4[:, :, 80:96], v4[:, :, 64:80], R=["wuq", "wuqr"], W=["wuqr"])
        wukv_f = self.sb([128, 512], F32, "wukvf")
        self.dma(wukv_f[:], self.w_ukv[l], W=["wukvf"])
        wukv = self.sb([128, 512], BF16, "wukv")
        self.ts(wukv[:], wukv_f[:], vec[:, 2:3], ALU.mult, R=["wukvf", "vec"], W=["wukv"])
        wukvv = self.sb([128, 4, 64], BF16, "wukvv")
        self.cp(wukvv[:], wukv[:].rearrange("p (h c) -> p h c", c=128)[:, :, 64:128], R=["wukv"], W=["wukvv"])
        wgk = self.sb([33, 256], BF16, "wgk")
        self.dma(wgk[:], self.wgk[l], W=["wgk"], q="pool")
        xrot = self.rot("x", 2, [128, 8, T], F32)
        csrot = self.rot("cs", 2, [96, 2, T], F32)
        sq = self.sb([128, 8, T], BF16, "sq")
        trot = self.rot("t", 2, [128, T], F32)
        hrot = self.rot("h", 2, [128, 8, T], BF16)
        rsrot = self.rot("rs", 2, [128, T], F32)
        lrT = self.sb([33, T], BF16, "lrT")
        self.memset(lrT[:], 1.0, W=["lrT"])
        cq_sb = self.sb([128, 2, T], BF16, "cq")
        sqq = self.sb([128, 2, T], BF16, "sqq")
        ckv_sb = self.sb([128, T], BF16, "ckv")
        sqkv = self.sb([128, T], BF16, "sqkv")
        rq = self.sb([128, T], F32, "rq")
        rkv = self.sb([128, T], F32, "rkv")
        rkvc = self.sb([128, 4], F32, "rkvc")
        t1 = self.sb([96, T], F32, "t1")
        t2 = self.sb([96, T], F32, "t2")
        etmp = self.sb([128, 256], F32, "etmp")
        st = {}
        for nm, shp, dt in (("q", [128, T], BF16), ("k", [128, T], BF16), ("go", [128, 2, T], BF16),
                            ("u", [128, 2, T], F32), ("qc", [128, 2, T], BF16), ("kc", [128, 2, T], BF16),
                            ("krot", [96, T], BF16), ("qd", [96, 4, T], BF16), ("kn", [64, 4, T], BF16),
                            ("kv", [128, 4, 384], BF16), ("vc", [128, 4, 4, 65], BF16), ("sp", [128, 4, 256], F32),
                            ("vd", [128, 4, 4, 65], BF16)):
            st[nm] = self.sb(shp, dt, "st_" + nm)
        self.memset(st["vc"][:], 1.0, W=["st_vc"])
        self.memset(st["vd"][:], 1.0, W=["st_vd"])
        prot = self.pbrot(range(6))
        ssb = self.pb[6]
        xsrc = self.xs.rearrange("(kc p) t -> p kc t", p=128)
        ntiles = NT // T
        state = {}

        def stage_load(i):
            x, xk = xrot.next()
            self.dma(x[:], xsrc[:, :, i * T:(i + 1) * T], W=[xk], q="sp")
            cs, csk = csrot.next()
            s = self.seq_of(i * T)
            p0 = i * T - self.off[s]
            self.dma(cs[64:96, :, :], self.rope[:, :, p0:p0 + T], W=[csk], q="sp")
            state[i] = dict(x=x, xk=xk, cs=cs, csk=csk)

        pending = []

        def norm_steps(i):
            sti = state[i]
            s = self.seq_of(i * T)
            x, xk = sti["x"], sti["xk"]
            h, hk = hrot.next()
            rs, rsk = rsrot.next()
            sti.update(h=h, hk=hk)
            steps = []

            def n2():
                self.act(sq[:], x[:], AF.Square, R=[xk], W=["sq"])
                for kc in range(8):
                    self.mm(ssb[:, 0:T], self.ones_bf[:], sq[:, kc, :], kc == 0, kc == 7, R=["sq", "ones"], W=["pb6"])

            def n3():
                self.act(rs[:], ssb[:, 0:T], AF.Sqrt, R=["pb6", "cst"], W=[rsk], bias=self.eps_c, scale=1.0 / D)
                self.recip(rs[:], rs[:], R=[rsk], W=[rsk])

            def n4(kc):
                tt_, ttk = trot.next()
                self.tt(tt_[:], x[:, kc, :], rs[:], ALU.mult, R=[xk, rsk], W=[ttk])
                self.ts(h[:, kc, :], tt_[:], self.Gn[l][:, 1, kc, s:s + 1], ALU.mult,
                        R=[ttk, "Gn%d" % l, "mod%d" % l], W=[hk], s2=self.shift(l, 1, kc, s), op1=ALU.add)
            steps.append(n2)
            steps.append(None)
            steps.append(n3)
            steps.append(None)
            for kc in range(8):
                steps.append(lambda kc=kc: n4(kc))
            return steps

        def stage_norm(i):
            for f in norm_steps(i):
                if f is not None:
                    f()

        def poll():
            if pending:
                f = pending.pop(0)
                if f is not None:
                    f()

        def stage_body(i):
            sti = state[i]
            h, hk, cs, csk = sti["h"], sti["hk"], sti["cs"], sti["csk"]
            t0 = i * T
            cosv = cs[64:96, 0, :]
            sinv = cs[64:96, 1, :]

            def fm(w, wk, c0, M):
                ps, pk = prot.next()
                for kc in range(8):
                    self.mm(ps[0:M, :], w[:, kc, c0:c0 + M], h[:, kc, :], kc == 0, kc == 7, R=[hk, wk], W=[pk])
                poll()
                return ps, pk

            ps, pk = fm(win, "win", 0, 128)
            self.act(st["q"][:], ps[:], AF.Copy, R=[pk], W=["st_q"])
            self.dma(self.qTa[:, t0:t0 + T], st["q"][:], R=["st_q"], q="pool")
            ps, pk = fm(win, "win", 128, 128)
            self.cp(st["k"][:], ps[:], R=[pk], W=["st_k"])
            self.dma(self.kTa[:, t0:t0 + T], st["k"][:], R=["st_k"], q="pool")
            for hp in range(2):
                ps, pk = fm(win, "win", 512 + hp * 128, 128)
                self.act(st["go"][:, hp, :], ps[:], AF.Silu, R=[pk], W=["st_go"])
            self.dma(self.goT[:, :, t0:t0 + T].rearrange("(hp two) p t -> (two p) hp t", two=2), st["go"][:],
                     R=["st_go"], q="pool")
            ps, pk = fm(win, "win", 768, 32)
            self.cp(lrT[0:32, :], ps[0:32, :], R=[pk], W=["lrT"])
            for c in range(2):
                ps, pk = fm(win, "win", 800 + c * 128, 128)
                self.act(st["u"][:, c, :], ps[:], AF.Copy, R=[pk], W=["st_u"])
            self.dma(self.uT.rearrange("(c p) t -> p c t", p=128)[:, :, t0:t0 + T], st["u"][:], R=["st_u"], q="pool")
            for hp in range(2):
                ps, pk = fm(win, "win", 1056 + hp * 128, 128)
                self.ts(st["qc"][:, hp, :], ps[:], 0.125, ALU.mult, R=[pk], W=["st_qc"])
            self.dma(self.qTc[:, :, t0:t0 + T].rearrange("(hp two) p t -> (two p) hp t", two=2), st["qc"][:],
                     R=["st_qc"], q="pool")
            for hp in range(2):
                ps, pk = fm(win, "win", 1312 + hp * 128, 128)
                self.act(st["kc"][:, hp, :], ps[:], AF.Copy, R=[pk], W=["st_kc"])
            self.dma(self.kTc[:, :, t0:t0 + T].rearrange("(hp two) p t -> (two p) hp t", two=2), st["kc"][:],
                     R=["st_kc"], q="pool")
            for c in range(2):
                ps, pk = fm(win, "win", 1824 + c * 128, 128)
                self.act(cq_sb[:, c, :], ps[:], AF.Copy, R=[pk], W=["cq"])
                self.act(sqq[:, c, :], ps[:], AF.Square, R=[pk], W=["sqq"])
            for c in range(2):
                self.mm(ssb[:, 0:T], self.ones_bf[:], sqq[:, c, :], c == 0, c == 1, R=["sqq", "ones"], W=["pb6"])
            self.act(rq[:], ssb[:, 0:T], AF.Ln, R=["pb6", "cst"], W=["rq"], bias=self.eps_c, scale=1.0 / 256)
            self.act(rq[:], rq[:], AF.Exp, R=["rq"], W=["rq"], scale=-0.5)
            ps, pk = fm(win, "win", 2080, 128)
            self.act(ckv_sb[:], ps[:], AF.Copy, R=[pk], W=["ckv"])
            self.act(sqkv[:], ps[:], AF.Square, R=[pk], W=["sqkv"])
            self.mm(ssb[:, 0:T], self.ones_bf[:], sqkv[:], True, True, R=["sqkv", "ones"], W=["pb6"])
            self.act(rkv[:], ssb[:, 0:T], AF.Ln, R=["pb6", "cst"], W=["rkv"], bias=self.eps_c, scale=1.0 / 128)
            self.act(rkv[:], rkv[:], AF.Exp, R=["rkv"], W=["rkv"], scale=-0.5)
            for j in range(4):
                self.mm(ssb[:, j:j + 1], sqkv[:, j * 128:(j + 1) * 128], self.ones_bf[:, 0:1], True, True,
                        R=["sqkv", "ones"], W=["pb6"])
            self.act(rkvc[:], ssb[:, 0:4], AF.Ln, R=["pb6", "cst"], W=["rkvc"], bias=self.eps_c, scale=1.0 / 128)
            self.act(rkvc[:], rkvc[:], AF.Exp, R=["rkvc"], W=["rkvc"], scale=-0.5)
            psA, pkA = fm(wkpe, "wkpe", 0, 96)
            psB, pkB = fm(wkrh, "wkrh", 0, 96)
            self.tt(t1[64:96, :], psA[64:96, :], cosv, ALU.mult, R=[pkA, csk], W=["t1"])
            self.tt(t2[64:96, :], psB[64:96, :], sinv, ALU.mult, R=[pkB, csk], W=["t2"])
            self.tt(st["krot"][64:96, :], t1[64:96, :], t2[64:96, :], ALU.add, R=["t1", "t2"], W=["st_krot"])
            for hh in range(4):
                self.dma(self.kTd[hh, 64:96, t0:t0 + T], st["krot"][64:96, :], R=["st_krot"], q="pool")
            for hh in range(4):
                psQ, pkQ = prot.next()
                for kc in range(2):
                    self.mm(psQ[0:96, :], wuq[:, kc, hh * 96:(hh + 1) * 96], cq_sb[:, kc, :], kc == 0, kc == 1,
                            R=["cq", "wuq"], W=[pkQ])
                psR, pkR = prot.next()
                for kc in range(2):
                    self.mm(psR[0:96, :], wuqr[:, kc, hh * 96:(hh + 1) * 96], cq_sb[:, kc, :], kc == 0, kc == 1,
                            R=["cq", "wuqr"], W=[pkR])
                self.stt(st["qd"][0:64, hh, :], psQ[0:64, :], MLA_SCALE, rq[0:64, :], ALU.mult, ALU.mult,
                         R=[pkQ, "rq"], W=["st_qd"])
                self.tt(t1[64:96, :], psQ[64:96, :], cosv, ALU.mult, R=[pkQ, csk], W=["t1"])
                self.tt(t2[64:96, :], psR[64:96, :], sinv, ALU.mult, R=[pkR, csk], W=["t2"])
                self.tt(t1[64:96, :], t1[64:96, :], t2[64:96, :], ALU.add, R=["t1", "t2"], W=["t1"])
                self.stt(st["qd"][64:96, hh, :], t1[64:96, :], MLA_SCALE, rq[64:96, :], ALU.mult, ALU.mult,
                         R=["t1", "rq"], W=["st_qd"])
            self.dma(self.qTd[:, :, t0:t0 + T].rearrange("h p t -> p h t"), st["qd"][:], R=["st_qd"], q="pool")
            for hh in range(4):
                ps, pk = prot.next()
                self.mm(ps[0:64, :], wukv[:, hh * 128:hh * 128 + 64], ckv_sb[:], True, True, R=["ckv", "wukv"], W=[pk])
                self.tt(st["kn"][:, hh, :], ps[0:64, :], rkv[0:64, :], ALU.mult, R=[pk, "rkv"], W=["st_kn"])
            self.dma(self.kTd[:, 0:64, t0:t0 + T].rearrange("h p t -> p h t"), st["kn"][:], R=["st_kn"], q="pool")
            for j in range(4):
                tok = slice(j * 128, (j + 1) * 128)
                ps, pk = prot.next()
                for kc in range(8):
                    self.mm(ps[:, 0:384], h[:, kc, tok], win[:, kc, 128:512], kc == 0, kc == 7, R=[hk, "win"], W=[pk])
                self.act(st["kv"][:, j, :], ps[:, 0:384], AF.Copy, R=[pk], W=["st_kv"])
                ps, pk = prot.next()
                for kc in range(8):
                    self.mm(ps[:, 0:256], h[:, kc, tok], win[:, kc, 1568:1824], kc == 0, kc == 7, R=[hk, "win"], W=[pk])
                self.cp(st["vc"][:, j, :, 0:64], ps[:, 0:256].rearrange("p (h c) -> p h c", c=64), R=[pk], W=["st_vc"])
                ps, pk = prot.next()
                self.mm(ps[:, 0:256], lrT[0:33, tok], wgk[0:33, :], True, True, R=["lrT", "wgk"], W=[pk])
                self.act(etmp[:], ps[:, 0:256], AF.Exp, R=[pk], W=["etmp"], scale=-1.0)
                self.act(st["sp"][:, j, :], etmp[:], AF.Ln, R=["etmp", "cst"], W=["st_sp"], bias=self.one_c)
                ps, pk = prot.next()
                self.mm(ps[:, 0:256], ckv_sb[:, tok], wukvv[:].rearrange("p h c -> p (h c)"), True, True,
                        R=["ckv", "wukvv"], W=[pk])
                self.ts(st["vd"][:, j, :, 0:64], ps[:, 0:256].rearrange("p (h c) -> p h c", c=64), rkvc[:, j:j + 1],
                        ALU.mult, R=[pk, "rkvc"], W=["st_vd"])
            tmv = lambda d: d[t0:t0 + T, :].rearrange("(j p) c -> p j c", p=128)
            self.dma(tmv(self.ktok), st["kv"][:, :, 0:128], R=["st_kv"], q="pool")
            self.dma(tmv(self.vtok), st["kv"][:, :, 128:384], R=["st_kv"], q="pool")
            self.dma(tmv(self.vtokc), st["vc"][:].rearrange("p j h c -> p j (h c)"), R=["st_vc"], q="pool")
            self.dma(tmv(self.sptok), st["sp"][:], R=["st_sp"], q="pool")
            self.dma(tmv(self.vtokd), st["vd"][:].rearrange("p j h c -> p j (h c)"), R=["st_vd"], q="pool")
            del state[i]

        stage_load(0)
        stage_norm(0)
        for i in range(ntiles):
            if i + 1 < ntiles:
                stage_load(i + 1)
                pending.extend(norm_steps(i + 1))
            stage_body(i)
            while pending:
                poll()
        S.phase_barrier(self.bar[0:1, 0:8], self.cst[0:1, 0:8])


    def pool_mixer(self, l, s):
        Sq = self.SEQS[s]
        o = self.off[s]
        TP = 1024 if Sq >= 1024 else Sq
        L = TP + 16
        self.sbp = self.persist_end
        vec = self.vec_sb[:, l * VEC_W:(l + 1) * VEC_W]
        pw_f = self.sb([128, 2, 128], F32, "pwf")
        pw = self.sb([128, 2, 128], BF16, "pw")
        self.dma(pw_f[:], self.poolw[l].rearrange("c p q -> p c q"), W=["pwf"])
        self.cp(pw[:], pw_f[:], R=["pwf"], W=["pw"])
        urot = self.rot("u", 2, [128, 2, L], F32)
        icrot = self.rot("ic", 2, [128, 2, TP], F32)
        P1 = self.sb([128, 2, L], F32, "P1")
        Q = self.sb([128, 2, L], F32, "Q")
        Rr = self.sb([128, L], F32, "R")
        Tt = self.sb([128, L], F32, "Tt")
        dd = self.sb([128, 2, TP], F32, "dd")
        db = self.sb([128, 2, TP], BF16, "db")
        yo = self.rot("yo", 2, [128, 2, TP], BF16)
        prot = self.pbrot(range(4))
        usrc = self.uT.rearrange("(c p) t -> p c t", p=128)
        icsrc = self.icnt[Sq].rearrange("(c p) t -> p c t", p=128)
        ydst = self.ymix[4:8].rearrange("(c two) f t -> (two f) c t", two=2)
        for it in range(Sq // TP):
            p0 = it * TP
            u, uk = urot.next()
            ic, ick = icrot.next()
            lo = max(p0 - 8, 0)
            hi = min(p0 + TP + 8, Sq)
            if lo > p0 - 8:
                self.memset(u[:, :, 0:8], 0.0, W=[uk])
            if hi < p0 + TP + 8:
                self.memset(u[:, :, L - 8:L], 0.0, W=[uk])
            self.dma(u[:, :, lo - (p0 - 8):hi - (p0 - 8)], usrc[:, :, o + lo:o + hi], W=[uk])
            self.dma(ic[:], icsrc[:, :, p0:p0 + TP], W=[ick])
            self.tt(P1[:, :, 1:L], u[:, :, 0:L - 1], u[:, :, 1:L], ALU.add, R=[uk], W=["P1"])
            self.tt(Q[:, :, 2:L - 2], P1[:, :, 1:L - 3], P1[:, :, 3:L - 1], ALU.add, R=["P1"], W=["Q"])
            self.tt(Rr[:, 4:L - 4], Q[:, 1, 2:L - 6], Q[:, 1, 6:L - 2], ALU.add, R=["Q"], W=["R"])
            self.tt(Tt[64:128, 8:L - 8], Rr[64:128, 4:L - 12], Rr[64:128, 12:L - 4], ALU.add, R=["R"], W=["Tt"])
            c8 = slice(8, 8 + TP)
            self.tt(dd[0:64, 0, :], P1[0:64, 0, c8], ic[0:64, 0, :], ALU.mult, R=["P1", ick], W=["dd"])
            self.tt(dd[64:128, 0, :], Q[64:128, 0, c8], ic[64:128, 0, :], ALU.mult, R=["Q", ick], W=["dd"])
            self.tt(dd[0:64, 1, :], Rr[0:64, c8], ic[0:64, 1, :], ALU.mult, R=["R", ick], W=["dd"])
            self.tt(dd[64:128, 1, :], Tt[64:128, c8], ic[64:128, 1, :], ALU.mult, R=["Tt", ick], W=["dd"])
            self.tt(db[:], dd[:], u[:, :, c8], ALU.subtract, R=["dd", uk], W=["db"])
            y, yk = yo.next()
            for c in range(2):
                for q in range(TP // 512):
                    ps, pk = prot.next()
                    self.mm(ps[:], pw[:, c, :], db[:, c, q * 512:(q + 1) * 512], True, True, R=["db", "pw"], W=[pk])
                    self.act(y[:, c, q * 512:(q + 1) * 512], ps[:], AF.Copy, R=[pk, "vec"], W=[yk],
                             scale=vec[:, 3 + c:4 + c])
            self.dma(ydst[:, :, o + p0:o + p0 + TP], y[:], R=[yk], q="pool")
        self.S.phase_barrier(self.bar[0:1, 0:8], self.cst[0:1, 0:8])

    def mla_mixer(self, l, s):
        Sq = self.SEQS[s]
        o = self.off[s]
        NK = Sq // 128
        self.sbp = self.persist_end
        KT = self.sb([96, 4, Sq], BF16, "KT")
        self.dma(KT[:], self.kTd[:, :, o:o + Sq].rearrange("h p t -> p h t"), W=["KT"])
        Vp = self.sb([128, NK, 4, 65], BF16, "Vp")
        self.dma(Vp[:].rearrange("p n h c -> p n (h c)"), self.vtokd[o:o + Sq, :].rearrange("(n p) c -> p n c", p=128),
                 W=["Vp"])
        qrot = self.rot("QT", 2, [96, 4, 512], BF16)
        prot_p = self.rot("pT", 3, [128, 512], BF16)
        rec = self.sb([65, 512], F32, "rec")
        bcs = self.sb([64, 512], F32, "bcs")
        yrot = self.rot("y", 2, [64, 4, 512], BF16)
        srot = self.pbrot([0, 1, 2])
        orot = self.pbrot([3, 4])
        bcp = self.pb[5]
        NQ = Sq // 512
        qts = {}

        def load_q(qt):
            QT, qk = qrot.next()
            self.dma(QT[:], self.qTd[:, :, o + qt * 512:o + (qt + 1) * 512].rearrange("h p t -> p h t"), W=[qk])
            qts[qt] = (QT, qk)

        items = [(qt, h, kc) for qt in range(NQ) for h in range(4) for kc in range(NK)]
        load_q(0)
        pend = []

        def qk_mm(it):
            qt, h, kc = it
            QT, qk = qts[qt]
            sp_, sk = srot.next()
            self.mm(sp_[:], KT[:, h, kc * 128:(kc + 1) * 128], QT[:, h, :], True, True, R=["KT", qk], W=[sk])
            pend.append((sp_, sk))

        y = yk = None
        cur_o = None
        for n, (qt, h, kc) in enumerate(items):
            if h == 0 and kc == 0:
                y, yk = yrot.next()
                if qt + 1 < NQ:
                    load_q(qt + 1)
            if n == 0:
                qk_mm(items[0])
                if len(items) > 1:
                    qk_mm(items[1])
            if kc == 0:
                cur_o = orot.next()
            ops_, ok = cur_o
            if n + 2 < len(items):
                qk_mm(items[n + 2])
            sp_, sk = pend.pop(0)
            pT, pk = prot_p.next()
            self.act(pT[:], sp_[:], AF.Exp, R=[sk], W=[pk])
            self.mm(ops_[0:65, :], Vp[:, kc, h, :], pT[:], kc == 0, kc == NK - 1, R=["Vp", pk], W=[ok])
            if kc == NK - 1:
                self.act(rec[64:65, :], ops_[64:65, :], AF.Ln, R=[ok], W=["rec"])
                self.act(rec[64:65, :], rec[64:65, :], AF.Exp, R=["rec"], W=["rec"], scale=-1.0)
                self.mm(bcp[0:64, :], self.ones_f[64:65, 0:64], rec[64:65, :], True, True, R=["rec", "onesf"], W=["pb5"])
                self.cp(bcs[:], bcp[0:64, :], R=["pb5"], W=["bcs"])
                self.tt(y[:, h, :], ops_[0:64, :], bcs[:], ALU.mult, R=[ok, "bcs"], W=[yk])
                if h == 3:
                    self.dma(self.ymix[12:16, :, o + qt * 512:o + (qt + 1) * 512].rearrange("h p t -> p h t"), y[:],
                             R=[yk], q="pool")
        self.S.phase_barrier(self.bar[0:1, 0:8], self.cst[0:1, 0:8])

    def na_mixer(self, l, s):
        Sq = self.SEQS[s]
        o = self.off[s]
        P = Sq // 128
        self.sbp = self.persist_end
        KT = self.sb([64, 4, Sq], BF16, "KT")
        self.dma(KT[:], self.kTc[:, :, o:o + Sq].rearrange("h p t -> p h t"), W=["KT"])
        Vp = self.sb([128, P, 4, 65], BF16, "Vp")
        self.dma(Vp[:].rearrange("p n h c -> p n (h c)"), self.vtokc[o:o + Sq, :].rearrange("(n p) c -> p n c", p=128),
                 W=["Vp"])
        tab = self.sb([128, 4, 21, 128], BF16, "tab")
        for h in range(4):
            self.dma(tab[:, h], self.natab[l, h].rearrange("v k q -> k v q"), W=["tab"], q="pool")
        sT_r = self.rot("sT", 4, [128, 640], F32)
        qrot = self.rot("QT", 2, [64, 4, 512], BF16)
        prot_p = self.rot("pT", 3, [128, 640], BF16)
        rec = self.sb([65, 512], F32, "rec")
        bcs = self.sb([64, 512], F32, "bcs")
        yrot = self.rot("y", 2, [64, 4, 512], BF16)
        arot = self.pbrot([0, 1, 2])
        brot = self.pbrot([3, 7])
        orot = self.pbrot([4, 5])
        bcp = self.pb[6]

        def chunks(i):
            if P >= 5 and 2 <= i <= P - 3:
                return [(i - 2 + d, d) for d in range(5)]
            if i == 0:
                return [(j, 5 + j) for j in range(4)]
            if i == 1:
                return [(j, 9 + j) for j in range(4)]
            if i == P - 2:
                return [(P - 4 + j, 13 + j) for j in range(4)]
            assert i == P - 1
            return [(P - 4 + j, 17 + j) for j in range(4)]

        NG = Sq // 512
        qts = {}

        def load_q(g):
            QT, qk = qrot.next()
            self.dma(QT[:], self.qTc[:, :, o + g * 512:o + (g + 1) * 512].rearrange("h p t -> p h t"), W=[qk])
            qts[g] = (QT, qk)

        items = [(g, ip, h) for g in range(NG) for ip in range(4) for h in range(4)]
        pend = []

        def stage1(it):
            g, ip, h = it
            QT, qk = qts[g]
            ch = chunks(g * 4 + ip)
            pa, pak = arot.next()
            pb_, pbk = (None, None)
            if len(ch) > 4:
                pb_, pbk = brot.next()
            for ci, (j, tix) in enumerate(ch):
                dst = pa[:, ci * 128:(ci + 1) * 128] if ci < 4 else pb_[:, 0:128]
                dk = pak if ci < 4 else pbk
                self.mm(dst, KT[:, h, j * 128:(j + 1) * 128], QT[:, h, ip * 128:(ip + 1) * 128], True, True,
                        R=["KT", qk], W=[dk])
            sT, stk = sT_r.next()
            t0x = ch[0][1]
            n4 = min(len(ch), 4)
            self.tt(sT[:, 0:n4 * 128].rearrange("p (c q) -> p c q", c=n4), pa[:, 0:n4 * 128].rearrange("p (c q) -> p c q", c=n4),
                    tab[:, h, t0x:t0x + n4, :], ALU.add, R=[pak, "tab"], W=[stk])
            if len(ch) > 4:
                self.tt(sT[:, 512:640], pb_[:, 0:128], tab[:, h, t0x + 4, :], ALU.add, R=[pbk, "tab"], W=[stk])
            pend.append((ch, sT, stk))

        y = yk = None
        cur_o = None
        for n, (g, ip, h) in enumerate(items):
            if ip == 0 and h == 0:
                y, yk = yrot.next()
                if n == 0:
                    load_q(0)
                    stage1(items[0])
                    if len(items) > 1:
                        stage1(items[1])
                if g + 1 < NG:
                    load_q(g + 1)
            if h == 0:
                cur_o = orot.next()
            ops_, ok = cur_o
            if n + 2 < len(items):
                stage1(items[n + 2])
            ch, sT, stk = pend.pop(0)
            pT, pk = prot_p.next()
            self.act(pT[:, 0:len(ch) * 128], sT[:, 0:len(ch) * 128], AF.Exp, R=[stk], W=[pk])
            for ci, (j, tix) in enumerate(ch):
                self.mm(ops_[0:65, h * 128:(h + 1) * 128], Vp[:, j, h, :], pT[:, ci * 128:(ci + 1) * 128],
                        ci == 0, ci == len(ch) - 1, R=["Vp", pk], W=[ok])
            if h == 3:
                self.act(rec[64:65, :], ops_[64:65, :], AF.Ln, R=[ok], W=["rec"])
                self.act(rec[64:65, :], rec[64:65, :], AF.Exp, R=["rec"], W=["rec"], scale=-1.0)
                self.mm(bcp[0:64, :], self.ones_f[64:65, 0:64], rec[64:65, :], True, True, R=["rec", "onesf"], W=["pb6"])
                self.cp(bcs[:], bcp[0:64, :], R=["pb6"], W=["bcs"])
                self.tt(y[:, :, ip * 128:(ip + 1) * 128], ops_[0:64, :].rearrange("p (h t) -> p h t", h=4),
                        bcs[:].rearrange("p (h t) -> p h t", h=4), ALU.mult, R=[ok, "bcs"], W=[yk])
                if ip == 3:
                    self.dma(self.ymix[8:12, :, o + g * 512:o + (g + 1) * 512].rearrange("h p t -> p h t"), y[:],
                             R=[yk], q="pool")
        self.S.phase_barrier(self.bar[0:1, 0:8], self.cst[0:1, 0:8])


    def gla_mixer(self, l, s):
        Sq = self.SEQS[s]
        o = self.off[s]
        NCH = Sq // 128
        NSC = Sq // 512
        self.sbp = self.persist_end
        cst = self.cst_sb
        vec = self.vec_sb[:, l * VEC_W:(l + 1) * VEC_W]
        triu = cst[:, C_TRIU:C_TRIU + 128]
        tril = cst[:, C_TRIL:C_TRIL + 128]
        nsix = cst[:, C_NSIX:C_NSIX + 1]
        bmask = cst[:, C_BM:C_BM + 256]
        maskF = cst[:, C_MF:C_MF + 128]
        maskB = cst[:, C_MB:C_MB + 128]
        Sf_all = self.sb([128, NCH, 256], BF16, "Sfall")
        Sf = self.sb([128, 256], F32, "Sf")
        Sb = self.sb([128, 256], F32, "Sb")
        Sb_bf = self.sb([128, 256], BF16, "Sbbf")
        tmp = self.sb([128, 256], F32, "tmp")
        ebl = self.sb([128, 1], F32, "ebl")
        einv_tok = self.sb([128, 128], F32, "einvtok")
        ktil = self.sb([128, 128], BF16, "ktil")
        ktok_r = self.rot("ktok", 3, [128, 4, 128], BF16)
        vtok_r = self.rot("vtok", 3, [128, 4, 256], BF16)
        sp_r = self.rot("sp", 3, [128, 4, 256], F32)
        q_r = self.rot("q4", 3, [128, 512], BF16)
        k_r = self.rot("k4", 3, [128, 512], BF16)
        go_r = self.rot("go4", 3, [64, 4, 512], BF16)
        y_r = self.rot("y4", 2, [64, 4, 512], BF16)
        E = self.sb([128, 256], F32, "E")
        EI = self.sb([128, 384], F32, "EI")
        ecf = self.sb([128, 1], F32, "ecf")
        Qf_r = self.rot("Qf", 2, [128, 4, 128], BF16)
        Qb_r = self.rot("Qb", 2, [128, 4, 128], BF16)
        Kf = self.sb([128, 128], BF16, "Kf")
        Kb = self.sb([128, 128], BF16, "Kb")
        ktilb = self.sb([128, 128], BF16, "ktilb")
        Amf = self.sb([128, 4, 128], BF16, "Amf")
        Amb = self.sb([128, 4, 128], BF16, "Amb")
        osq = self.sb([64, 512], BF16, "osq")
        rr = self.sb([64, 512], F32, "rr")
        tt_ = self.sb([64, 512], F32, "tt")
        ktsrc = self.ktok[o:o + Sq, :].rearrange("(n p) c -> p n c", p=128)
        vtsrc = self.vtok[o:o + Sq, :].rearrange("(n p) c -> p n c", p=128)
        spsrc = self.sptok[o:o + Sq, :].rearrange("(n p) c -> p n c", p=128)

        def load_tok(sc, want_qk):
            d = {}
            kt, d["ktk"] = ktok_r.next()
            vt, d["vtk"] = vtok_r.next()
            sp, d["spk"] = sp_r.next()
            self.dma(kt[:], ktsrc[:, sc * 4:(sc + 1) * 4, :], W=[d["ktk"]])
            self.dma(vt[:], vtsrc[:, sc * 4:(sc + 1) * 4, :], W=[d["vtk"]])
            self.dma(sp[:], spsrc[:, sc * 4:(sc + 1) * 4, :], W=[d["spk"]])
            d.update(kt=kt, vt=vt, sp=sp)
            if want_qk:
                q4, d["qk"] = q_r.next()
                k4, d["kk"] = k_r.next()
                go, d["gok"] = go_r.next()
                tsl = slice(o + sc * 512, o + (sc + 1) * 512)
                self.dma(q4[:], self.qTa[:, tsl], W=[d["qk"]])
                self.dma(k4[:], self.kTa[:, tsl], W=[d["kk"]])
                self.dma(go[:], self.goT[:, :, tsl].rearrange("h p t -> p h t"), W=[d["gok"]])
                d.update(q4=q4, k4=k4, go=go)
            return d

        self.memset(Sf[:], 0.0, W=["Sf"])
        self.memset(Sf_all[:, 0, :], 0.0, W=["Sfall"])
        yb = self.pbrot([0, 1])
        kvb = self.pbrot([2, 3])
        ebl_r = self.rot("ebl1", 2, [128, 1], F32)
        einv_r = self.rot("einv1", 2, [128, 128], F32)
        ktil_r = self.rot("ktil1", 2, [128, 128], BF16)
        toks1 = {}

        def get_tok1(sc):
            if sc not in toks1:
                toks1[sc] = load_tok(sc, False)
            return toks1[sc]

        def prep1(n):
            sc, c = n // 4, n % 4
            cur = get_tok1(sc)
            if c == 1 and sc + 1 < NSC:
                get_tok1(sc + 1)
            spf = cur["sp"][:, c, 0:128]
            Y, yk = yb.next()
            self.mm(Y[:, 0:128], triu, spf, True, True, R=["cst", cur["spk"]], W=[yk])
            self.mm(Y[:, 128:129], spf, nsix, True, True, R=["cst", cur["spk"]], W=[yk])
            einv, eik = einv_r.next()
            ebl1, eblk = ebl_r.next()
            ktil1, ktk = ktil_r.next()
            self.act(einv[:], Y[:, 0:128], AF.Exp, R=[yk], W=[eik], scale=-1.0)
            self.act(ebl1[:], Y[:, 128:129], AF.Exp, R=[yk], W=[eblk])
            self.tt(ktil1[:], cur["kt"][:, c, :], einv[:], ALU.mult, R=[cur["ktk"], eik], W=[ktk], eng="pool")
            KV, kvk = kvb.next()
            self.mm(KV[:, 0:256], ktil1[:], cur["vt"][:, c, :], True, True, R=[ktk, cur["vtk"]], W=[kvk])
            return KV, kvk, ebl1, eblk

        if NCH > 1:
            pn = prep1(0)
            for n in range(NCH - 1):
                pnext = prep1(n + 1) if n + 1 < NCH - 1 else None
                KV, kvk, ebl1, eblk = pn
                self.tt(tmp[:], KV[:, 0:256], Sf[:], ALU.add, R=[kvk, "Sf"], W=["tmp"])
                self.stt(Sf[:], tmp[:], ebl1[:, 0:1], bmask, ALU.mult, ALU.mult, R=["tmp", eblk, "cst"], W=["Sf"])
                self.cp(Sf_all[:, n + 1, :], Sf[:], R=["Sf"], W=["Sfall"], eng="pool")
                pn = pnext
        self.memset(Sb[:], 0.0, W=["Sb"])
        self.memset(Sb_bf[:], 0.0, W=["Sbbf"])
        xb = self.pbrot([0, 1])
        ob = self.pbrot([2, 3])
        afp, abp, kvp, ssp = self.pb[4], self.pb[5], self.pb[6], self.pb[7]
        E_r = self.rot("E2", 2, [128, 256], F32)
        EI_r = self.rot("EI2", 2, [128, 384], F32)
        ecf_r = self.rot("ecf2", 2, [128, 1], F32)
        Kf_r = self.rot("Kf2", 2, [128, 128], BF16)
        Kb_r = self.rot("Kb2", 2, [128, 128], BF16)
        ktb_r = self.rot("ktb2", 2, [128, 128], BF16)
        Amf_r = self.rot("Amf2", 2, [128, 4, 128], BF16)
        Amb_r = self.rot("Amb2", 2, [128, 4, 128], BF16)
        toks = {}
        ys = {}

        def get_tok(sc):
            if sc not in toks:
                toks[sc] = load_tok(sc, True)
            return toks[sc]

        def stage_P(n):
            sc, c = n // 4, n % 4
            cur = get_tok(sc)
            csl = slice(c * 128, (c + 1) * 128)
            spf = cur["sp"][:, c, 0:128]
            spb = cur["sp"][:, c, 128:256]
            X, xk = xb.next()
            R0 = ["cst", cur["spk"]]
            self.mm(X[:, 0:128], spf, triu, True, True, R=R0, W=[xk])
            self.mm(X[:, 128:256], spb, tril, True, True, R=R0, W=[xk])
            self.mm(X[:, 256:384], tril, spb, True, True, R=R0, W=[xk])
            self.mm(X[:, 384:385], spb, nsix, True, True, R=R0, W=[xk])
            E, ek = E_r.next()
            EI, eik = EI_r.next()
            ecf, ecfk = ecf_r.next()
            self.act(E[:], X[:, 0:256], AF.Exp, R=[xk], W=[ek])
            self.act(EI[:], X[:, 0:384], AF.Exp, R=[xk], W=[eik], scale=-1.0)
            self.act(ecf[:], X[:, 384:385], AF.Exp, R=[xk], W=[ecfk])
            Qf, qfk = Qf_r.next()
            Qb, qbk = Qb_r.next()
            for h in range(4):
                hm = cst[:, C_HM + h:C_HM + h + 1]
                self.stt(Qf[:, h, :], cur["q4"][:, csl], hm, E[:, 0:128], ALU.mult, ALU.mult,
                         R=[cur["qk"], ek, "cst"], W=[qfk])
                self.stt(Qb[:, h, :], cur["q4"][:, csl], hm, E[:, 128:256], ALU.mult, ALU.mult,
                         R=[cur["qk"], ek, "cst"], W=[qbk])
            Kf, kfk = Kf_r.next()
            Kb, kbk = Kb_r.next()
            ktilb, ktbk = ktb_r.next()
            self.tt(Kf[:], cur["k4"][:, csl], EI[:, 0:128], ALU.mult, R=[cur["kk"], eik], W=[kfk], eng="pool")
            self.tt(Kb[:], cur["k4"][:, csl], EI[:, 128:256], ALU.mult, R=[cur["kk"], eik], W=[kbk], eng="pool")
            self.tt(ktilb[:], cur["kt"][:, c, :], EI[:, 256:384], ALU.mult, R=[cur["ktk"], eik], W=[ktbk], eng="pool")
            for h in range(4):
                self.mm(afp[:, h * 128:(h + 1) * 128], Kf[:], Qf[:, h, :], True, True, R=[kfk, qfk], W=["pb4"])
            for h in range(4):
                self.mm(abp[:, h * 128:(h + 1) * 128], Kb[:], Qb[:, h, :], True, True, R=[kbk, qbk], W=["pb5"])
            Amf, amfk = Amf_r.next()
            Amb, ambk = Amb_r.next()
            self.tt(Amf[:], afp[:].rearrange("p (h t) -> p h t", h=4), maskF.unsqueeze(1).to_broadcast([128, 4, 128]),
                    ALU.mult, R=["pb4", "cst"], W=[amfk])
            self.tt(Amb[:], abp[:].rearrange("p (h t) -> p h t", h=4), maskB.unsqueeze(1).to_broadcast([128, 4, 128]),
                    ALU.mult, R=["pb5", "cst"], W=[ambk])
            return dict(cur=cur, c=c, sc=sc, csl=csl, ecf=ecf, ecfk=ecfk, Qf=Qf, qfk=qfk, Qb=Qb, qbk=qbk,
                        ktilb=ktilb, ktbk=ktbk, Amf=Amf, amfk=amfk, Amb=Amb, ambk=ambk)

        def stage_Q(n, P):
            cur, c, sc, csl = P["cur"], P["c"], P["sc"], P["csl"]
            if sc not in ys:
                ys[sc] = y_r.next()
            y4, y4k = ys[sc]
            O, ok = ob.next()
            for h in range(4):
                od = O[0:64, h * 128:(h + 1) * 128]
                vh = cur["vt"][:, c, h * 64:(h + 1) * 64]
                self.mm(od, vh, P["Amf"][:, h, :], True, False, R=[cur["vtk"], P["amfk"]], W=[ok])
                self.mm(od, vh, P["Amb"][:, h, :], False, False, R=[cur["vtk"], P["ambk"]], W=[ok])
                self.mm(od, Sf_all[:, n, h * 64:(h + 1) * 64], P["Qf"][:, h, :], False, False, R=["Sfall", P["qfk"]], W=[ok])
                self.mm(od, Sb_bf[:, h * 64:(h + 1) * 64], P["Qb"][:, h, :], False, True, R=["Sbbf", P["qbk"]], W=[ok])
            self.mm(kvp[:, 0:256], P["ktilb"][:], cur["vt"][:, c, :], True, True, R=[P["ktbk"], cur["vtk"]], W=["pb6"])
            self.tt(tmp[:], kvp[:, 0:256], Sb[:], ALU.add, R=["pb6", "Sb"], W=["tmp"])
            self.stt(Sb[:], tmp[:], P["ecf"][:, 0:1], bmask, ALU.mult, ALU.mult, R=["tmp", P["ecfk"], "cst"], W=["Sb"])
            self.cp(Sb_bf[:], Sb[:], R=["Sb"], W=["Sbbf"], eng="pool")
            self.act(osq[:], O[0:64, :], AF.Square, R=[ok], W=["osq"])
            self.mm(ssp[0:64, :], self.ones_bf[0:64, 0:64], osq[:], True, True, R=["osq", "ones"], W=["pb7"])
            self.act(rr[:], ssp[0:64, :], AF.Ln, R=["pb7", "cst"], W=["rr"], bias=self.eps_c[0:64, :], scale=1.0 / 64)
            self.act(rr[:], rr[:], AF.Exp, R=["rr"], W=["rr"], scale=-0.5)
            self.tt(tt_[:], O[0:64, :], rr[:], ALU.mult, R=[ok, "rr"], W=["tt"])
            self.stt(y4[:, :, csl], tt_[:].rearrange("p (h t) -> p h t", h=4), vec[0:64, 5:6], cur["go"][:, :, csl],
                     ALU.mult, ALU.mult, R=["tt", "vec", cur["gok"]], W=[y4k])
            if c == 0:
                self.dma(self.ymix[0:4, :, o + sc * 512:o + (sc + 1) * 512].rearrange("h p t -> p h t"), y4[:], R=[y4k],
                         q="pool")
                del toks[sc]

        Pn = stage_P(NCH - 1)
        for n in range(NCH - 1, -1, -1):
            Pnext = None
            if n - 1 >= 0:
                if (n - 1) % 4 == 1 and (n - 1) // 4 - 1 >= 0:
                    get_tok((n - 1) // 4 - 1)
                Pnext = stage_P(n - 1)
            stage_Q(n, Pn)
            Pn = Pnext
        self.S.phase_barrier(self.bar[0:1, 0:8], self.cst[0:1, 0:8])


C_EPS = 0
C_NSIX = 1
C_ONE = 2
C_HM = 4
C_BM = 8
C_TRIU = 264
C_TRIL = 392
C_MF = 520
C_MB = 648
C_ID = 776
CST_W = 904
VEC_W = 8
POOL_WINDOWS = (2, 4, 8, 16)


def host_consts():
    c = np.zeros((128, CST_W), np.float32)
    c[:, C_EPS] = EPS
    c[:, C_NSIX] = -1.0 / 16.0
    c[:, C_ONE] = 1.0
    p = np.arange(128)
    for h in range(4):
        c[:, C_HM + h] = (p // 32 == h) * (32.0 ** -0.5)
    f = np.arange(256)
    c[:, C_BM:C_BM + 256] = (f[None, :] // 64 == p[:, None] // 32)
    j = p[:, None]
    i = p[None, :]
    c[:, C_TRIU:C_TRIU + 128] = (j <= i) * (-1.0 / 16.0)
    c[:, C_TRIL:C_TRIL + 128] = (j >= i) * (-1.0 / 16.0)
    c[:, C_MF:C_MF + 128] = (j <= i)
    c[:, C_MB:C_MB + 128] = (j > i)
    c[:, C_ID:C_ID + 128] = (j == i)
    return c


def host_rope(smax):
    half = 16
    inv = (10000.0 ** (-np.arange(half, dtype=np.float32) / half)).astype(np.float32)
    pos = np.arange(smax, dtype=np.float32)
    ang = (pos[None, :] * np.concatenate([inv, inv])[:, None]).astype(np.float32)
    return np.ascontiguousarray(np.stack([np.cos(ang), np.sin(ang)], axis=1).astype(np.float32))


def host_icnt(S):
    t = np.arange(S)
    out = np.zeros((256, S), np.float32)
    for gi, w in enumerate(POOL_WINDOWS):
        lo = np.clip(t - w // 2, 0, S)
        hi = np.clip(t + w // 2, 0, S)
        out[gi * 64:(gi + 1) * 64, :] = (1.0 / (hi - lo).astype(np.float32))[None, :]
    return out


def build(nc, SEQS, dbg=False, stages="ABCD", nlayers=NL, mixers="pdca"):
    b = Builder(nc, SEQS, dbg)
    b.declare()
    b.prologue()
    for l in range(nlayers):
        last = (l == nlayers - 1)
        if "A" in stages:
            b.ffn_sweep(l, 0, b.xT if l == 0 else b.xs, b.xs if "D" in stages or True else b.yT)
        if "B" in stages:
            b.proj_sweep(l)
        if "C" in stages:
            for sq in range(b.NS):
                if "p" in mixers:
                    b.pool_mixer(l, sq)
                if "d" in mixers:
                    b.mla_mixer(l, sq)
                if "c" in mixers:
                    b.na_mixer(l, sq)
                if "a" in mixers:
                    b.gla_mixer(l, sq)
        if "D" in stages:
            b.ffn_sweep(l, 1, b.xs, b.yT if last else b.xs, pre_wout=("C" in stages), final_norm=last, final=last)
    b.S.emit()
    return b


def host_shared(w, seqs):
    m = {}
    m["ada_w"] = np.ascontiguousarray(w["ada_w"], np.float32)
    m["adabT"] = np.ascontiguousarray(w["ada_b"].reshape(NL, 72, 128).transpose(0, 2, 1), np.float32)
    nrm = np.zeros((128, (NL * 3 + 1) * 8), np.float32)
    for l in range(NL):
        for k, name in enumerate(("norm_ffn1", "norm_mix", "norm_ffn2")):
            nrm[:, (l * 3 + k) * 8:(l * 3 + k + 1) * 8] = w[name][l].reshape(8, 128).T
    nrm[:, NL * 3 * 8:] = w["final_norm"].reshape(8, 128).T
    m["nrm"] = nrm
    for k in ("ffn1_wg", "ffn1_wu", "ffn1_wd", "ffn2_wg", "ffn2_wu", "ffn2_wd", "w_in", "w_out"):
        m[k] = np.ascontiguousarray(w[k], np.float32)
    m["cst"] = host_consts()
    vec = np.zeros((128, NL * VEC_W), np.float32)
    wgk = np.zeros((NL, 33, 256), np.float32)
    poolw = np.zeros((NL, 2, 128, 128), np.float32)
    for l in range(NL):
        vec[:, l * VEC_W + 0:l * VEC_W + 2] = w["mla_qnorm"][l].reshape(2, 128).T
        vec[:, l * VEC_W + 2] = w["mla_kvnorm"][l]
        vec[:, l * VEC_W + 3:l * VEC_W + 5] = w["pool_scale"][l].reshape(2, 128).T
        vec[0:64, l * VEC_W + 5] = w["gla_norm"][l]
        vec[64:128, l * VEC_W + 5] = w["gla_norm"][l]
        wgk[l, 0:16, 0:128] = w["gla_wgk_f"][l]
        wgk[l, 16:32, 128:256] = w["gla_wgk_b"][l]
        wgk[l, 32, 0:128] = w["gla_bgk_f"][l]
        wgk[l, 32, 128:256] = w["gla_bgk_b"][l]
        for c in range(2):
            poolw[l, c, 0:64, 0:64] = w["pool_w"][l, 2 * c]
            poolw[l, c, 64:128, 64:128] = w["pool_w"][l, 2 * c + 1]
    m["vec"] = vec
    m["wgk"] = wgk
    m["poolw"] = poolw
    m["mla_wuq"] = np.ascontiguousarray(w["mla_wuq"], np.float32)
    m["mla_wukv"] = np.ascontiguousarray(w["mla_wukv"], np.float32)
    m["natab"] = host_natab(w["na_rpb"], seqs)
    m["rope"] = host_rope(max(seqs))
    for Sq in sorted(set(seqs)):
        m["icnt%d" % Sq] = host_icnt(Sq)
    return m


def host_natab(rpb, seqs):
    Rr = 16
    P = Rr // 2
    variants = [(3, [1, 2, 3, 4, 5]), (0, [0, 1, 2, 3]), (1, [0, 1, 2, 3]), (P - 2, [P - 4, P - 3, P - 2, P - 1]),
                (P - 1, [P - 4, P - 3, P - 2, P - 1])]
    out = np.full((NL, 4, 21, 128, 128), NEG, np.float32)
    cq = np.arange(64)
    ck = np.arange(64)
    qstart = np.clip(cq - 8, 0, 48)
    col_ok = (ck[None, :] >= qstart[:, None]) & (ck[None, :] < qstart[:, None] + 16)
    dcol = ck[None, :] - cq[:, None] + 15
    t = 0
    for (i, js) in variants:
        for j in js:
            for rq in range(2):
                qrow = 2 * i + rq
                r0 = min(max(qrow - 4, 0), Rr - 8)
                for rk in range(2):
                    krow = 2 * j + rk
                    if not (r0 <= krow < r0 + 8):
                        continue
                    drow = krow - qrow + 7
                    blk = np.where(col_ok[None, None], rpb[:, :, drow, :][:, :, np.clip(dcol, 0, 30)], NEG)
                    out[:, :, t, rk * 64:(rk + 1) * 64, rq * 64:(rq + 1) * 64] = np.swapaxes(blk, -1, -2)
            t += 1
    assert t == 21
    return out


def host_core(xs, cs):
    xT = np.ascontiguousarray(np.concatenate(xs, axis=0).T, np.float32)
    NS = len(xs)
    cT = np.ascontiguousarray(np.asarray(cs, np.float32).T.reshape(8, 128, NS).transpose(1, 0, 2))
    return {"xT": xT, "cT": cT}


N_CORES = 8
SEQS_FULL = [2048, 2048, 2048, 2048, 8192]
_WNAMES = ("ada_w", "ada_b", "norm_ffn1", "ffn1_wg", "ffn1_wu", "ffn1_wd", "norm_mix", "w_in", "gla_wgk_f",
           "gla_bgk_f", "gla_wgk_b", "gla_bgk_b", "gla_norm", "pool_w", "pool_scale", "na_rpb", "mla_qnorm",
           "mla_wuq", "mla_kvnorm", "mla_wukv", "w_out", "norm_ffn2", "ffn2_wg", "ffn2_wu", "ffn2_wd", "final_norm")


def kernel(x_prompt, x_sample, c_prompt, c_sample, **weights):
    w = {k: np.asarray(weights[k], np.float32) for k in _WNAMES}
    x_prompt = np.asarray(x_prompt, np.float32)
    x_sample = np.asarray(x_sample, np.float32)
    c_prompt = np.asarray(c_prompt, np.float32)
    c_sample = np.asarray(c_sample, np.float32)
    nc = bass.Bass("TRN2", target_bir_lowering=False)
    b = build(nc, SEQS_FULL)
    shared = host_shared(w, SEQS_FULL)
    shared = {k: v for k, v in shared.items() if k in b.dram}
    in_maps = []
    for c in range(N_CORES):
        xs = [x_prompt[4 * c + i] for i in range(4)] + [x_sample[c]]
        cs = np.concatenate([c_prompt[4 * c:4 * c + 4], c_sample[c:c + 1]], axis=0)
        m = dict(shared)
        m.update(host_core(xs, cs))
        in_maps.append(m)
    res = run_bass_kernel_spmd(nc, in_maps, core_ids=list(range(N_CORES)))
    y_prompt = np.empty((32, 2048, D), np.float32)
    y_sample = np.empty((8, 8192, D), np.float32)
    for c in range(N_CORES):
        yT = np.asarray(res.results[c]["yT"], np.float32)
        for i in range(4):
            y_prompt[4 * c + i] = yT[:, i * 2048:(i + 1) * 2048].T
        y_sample[c] = yT[:, 8192:].T
    return (y_prompt, y_sample)
```

```python
import numpy as np
from contextlib import ExitStack
import concourse.bass as bass
import concourse.mybir as mybir
from concourse.bass_utils import run_bass_kernel_spmd

F32 = mybir.dt.float32
BF16 = mybir.dt.bfloat16
AF = mybir.ActivationFunctionType
ALU = mybir.AluOpType

D = 1024
FF = 2816
NL = 2
PW = 2240
EPS = 1e-6
MLA_SCALE = 96.0 ** -0.5
NEG = -30000.0
ENGS = ("pe", "act", "dve", "pool", "sp")
DMAQ = ("sp", "pool", "act")
ATTACH_WAITS = True


class Op:
    __slots__ = ("eng", "fn", "deps", "is_dma", "sig", "sem", "val", "idx")

    def __init__(self, eng, fn, is_dma):
        self.eng = eng
        self.fn = fn
        self.is_dma = is_dma
        self.deps = []
        self.sig = False
        self.sem = None
        self.val = 0
        self.idx = 0


class Sched:
    NPOOL = 24

    def __init__(self, nc):
        self.nc = nc
        self.ops = []
        self.lastw = {}
        self.readers = {}
        self.finals = []
        self.barrier = None
        self.bar_seen = set()
        self.last_eng = {}

    def op(self, eng, fn, R=(), W=(), dma=False, final=False):
        o = Op(eng, fn, dma)
        o.idx = len(self.ops)
        deps = {}
        for k in R:
            w = self.lastw.get(k)
            if w is not None:
                deps[w.idx] = w
        for k in W:
            w = self.lastw.get(k)
            if w is not None:
                deps[w.idx] = w
            for r in self.readers.get(k, ()):
                deps[r.idx] = r
        if self.barrier is not None and eng not in self.bar_seen:
            self.bar_seen.add(eng)
            deps[self.barrier.idx] = self.barrier
        for d in deps.values():
            if d.eng == "pe" and eng == "pe" and not d.is_dma and not dma:
                continue
            o.deps.append(d)
            d.sig = True
        for k in W:
            self.lastw[k] = o
            self.readers[k] = []
        for k in R:
            lst = self.readers.setdefault(k, [])
            if not dma:
                lst[:] = [r for r in lst if r.is_dma or r.eng != eng]
            lst.append(o)
        self.ops.append(o)
        self.last_eng[eng] = o
        if final:
            o.sig = True
            self.finals.append(o)
        return o

    def dma(self, eng, out, in_, R=(), W=(), final=False):
        return self.op(eng, lambda e: e.dma_start(out=out, in_=in_), R=R, W=W, dma=True, final=final)

    def phase_barrier(self, scratch_dst, scratch_src):
        o = Op("sp", lambda e: e.dma_start(out=scratch_dst, in_=scratch_src), True)
        o.idx = len(self.ops)
        seen = {}
        for p in self.ops:
            if p.is_dma:
                seen[p.idx] = p
        for e, p in self.last_eng.items():
            seen[p.idx] = p
        start = self.barrier.idx if self.barrier is not None else -1
        for p in seen.values():
            if p.idx > start:
                o.deps.append(p)
                p.sig = True
        o.sig = True
        self.ops.append(o)
        self.last_eng["sp"] = o
        self.barrier = o
        self.bar_seen = {"sp"}
        self.lastw = {}
        self.readers = {}

    def emit(self):
        nc = self.nc
        with ExitStack() as st:
            esem = {e: st.enter_context(nc.semaphore("s_" + e)) for e in ENGS}
            dsem = {e: [st.enter_context(nc.semaphore("d_%s_%d" % (e, i))) for i in range(self.NPOOL)]
                    for e in DMAQ}
            ecount = {e: 0 for e in ENGS}
            dcount = {e: [0] * self.NPOOL for e in dsem}
            dnext = {e: 0 for e in dsem}
            dprev = {e: [None] * self.NPOOL for e in dsem}
            per_eng = {e: [] for e in ENGS}
            for o in self.ops:
                per_eng[o.eng].append(o)
                if o.is_dma:
                    i = dnext[o.eng]
                    dnext[o.eng] = (i + 1) % self.NPOOL
                    prev = dprev[o.eng][i]
                    if prev is not None:
                        o.deps.append(prev)
                    dprev[o.eng][i] = o
                    dcount[o.eng][i] += 16
                    o.sem = dsem[o.eng][i]
                    o.val = dcount[o.eng][i]
                    o.sig = True
                elif o.sig:
                    ecount[o.eng] += 1
                    o.sem = esem[o.eng]
                    o.val = ecount[o.eng]
            self.nwaits = 0
            blk = st.enter_context(nc.Block())

            def run(engname, e):
                waited = {}
                for o in per_eng[engname]:
                    need = {}
                    for d in o.deps:
                        key = id(d.sem)
                        if waited.get(key, 0) >= d.val:
                            continue
                        if key not in need or need[key][1] < d.val:
                            need[key] = (d.sem, d.val)
                    items = list(need.items())
                    attach = None
                    if items and ATTACH_WAITS and not o.is_dma:
                        attach = items.pop()
                    for key, (sem, val) in items:
                        e.wait_ge(sem, val)
                        waited[key] = val
                        self.nwaits += 1
                    ins = o.fn(e)
                    if attach is not None:
                        ins._wait_ge(attach[1][0], attach[1][1])
                        waited[attach[0]] = attach[1][1]
                    if o.sig:
                        ins.then_inc(o.sem, 16 if o.is_dma else 1)
                if engname == "sp":
                    for o in self.finals:
                        key = id(o.sem)
                        if waited.get(key, 0) < o.val:
                            e.wait_ge(o.sem, o.val)
                            waited[key] = o.val

            @blk.tensor
            def _(e):
                run("pe", e)

            @blk.scalar
            def _(e):
                run("act", e)

            @blk.vector
            def _(e):
                run("dve", e)

            @blk.gpsimd
            def _(e):
                run("pool", e)

            @blk.sync
            def _(e):
                run("sp", e)


class Rot:
    def __init__(self, name, aps, keys=None):
        self.name = name
        self.aps = aps
        self.keys = keys if keys is not None else [(name, i) for i in range(len(aps))]
        self.i = -1

    def next(self):
        self.i = (self.i + 1) % len(self.aps)
        return self.aps[self.i], self.keys[self.i]

    def cur(self):
        return self.aps[self.i], self.keys[self.i]


SB_LO = 16512
SB_HI = 229312


class Builder:
    def __init__(self, nc, SEQS, dbg=False):
        self.nc = nc
        self.S = Sched(nc)
        self.SEQS = list(SEQS)
        self.NT = sum(SEQS)
        self.NS = len(SEQS)
        self.off = [int(v) for v in np.cumsum([0] + list(SEQS))]
        self.dbg = dbg
        self.sbp = SB_LO
        self.uid = 0
        self.dram = {}
        self.outs = []

    def sb(self, shape, dt, name=None):
        nbytes = int(np.prod(shape[1:])) * (4 if dt == F32 else 2)
        nbytes = (nbytes + 63) // 64 * 64
        self.uid += 1
        t = self.nc.alloc_sbuf_tensor_at("%s_%d" % (name or "t", self.uid), list(shape), dt, offset=self.sbp)
        self.sbp += nbytes
        assert self.sbp <= SB_HI, ("SBUF overflow", name, self.sbp)
        return t

    def rot(self, name, n, shape, dt):
        self.uid += 1
        return Rot("%s%d" % (name, self.uid), [self.sb(shape, dt, name) for _ in range(n)])

    def pbrot(self, idxs):
        return Rot("pb", [self.pb[i] for i in idxs], keys=["pb%d" % i for i in idxs])

    def din(self, name, shape, dt=F32):
        t = self.nc.dram_tensor(name, list(shape), dt, kind="ExternalInput").ap()
        self.dram[name] = t
        return t

    def dscr(self, name, shape, dt):
        kind = "ExternalOutput" if self.dbg else "Internal"
        t = self.nc.dram_tensor(name, list(shape), dt, kind=kind).ap()
        self.dram[name] = t
        return t

    def seq_of(self, tok):
        for s in range(self.NS):
            if self.off[s] <= tok < self.off[s + 1]:
                return s
        raise ValueError

    def mm(self, out, lhsT, rhs, start, stop, R, W, skip=False):
        if skip:
            self.S.op("pe", lambda e: e.matmul(out, lhsT, rhs, start=start, stop=stop, skip_group_check=True),
                      R=R, W=W)
        else:
            self.S.op("pe", lambda e: e.matmul(out, lhsT, rhs, start=start, stop=stop), R=R, W=W)

    def act(self, out, in_, func, R, W, bias=None, scale=None):
        kw = {}
        if bias is not None:
            kw["bias"] = bias
        if scale is not None:
            kw["scale"] = scale
        self.S.op("act", lambda e: e.activation(out=out, in_=in_, func=func, **kw), R=R, W=W)

    def tt(self, out, in0, in1, op, R, W, eng="dve"):
        self.S.op(eng, lambda e: e.tensor_tensor(out=out, in0=in0, in1=in1, op=op), R=R, W=W)

    def ts(self, out, in0, s1, op0, R, W, s2=None, op1=None, eng="dve"):
        if op1 is None:
            self.S.op(eng, lambda e: e.tensor_scalar(out=out, in0=in0, scalar1=s1, scalar2=None, op0=op0), R=R, W=W)
        else:
            self.S.op(eng, lambda e: e.tensor_scalar(out=out, in0=in0, scalar1=s1, scalar2=s2, op0=op0, op1=op1),
                      R=R, W=W)

    def stt(self, out, in0, scalar, in1, op0, op1, R, W, eng="dve"):
        self.S.op(eng, lambda e: e.scalar_tensor_tensor(out=out, in0=in0, scalar=scalar, in1=in1, op0=op0, op1=op1),
                  R=R, W=W)

    def cp(self, out, in_, R, W, eng="dve"):
        self.S.op(eng, lambda e: e.tensor_copy(out=out, in_=in_), R=R, W=W)

    def recip(self, out, in_, R, W):
        self.S.op("dve", lambda e: e.reciprocal(out=out, in_=in_), R=R, W=W)

    def memset(self, ap, val, W, eng="dve"):
        self.S.op(eng, lambda e: e.memset(ap, val), W=W)

    def dma(self, out, in_, R=(), W=(), q="sp", final=False):
        self.S.dma(q, out, in_, R=R, W=W, final=final)

    def declare(self):
        NT, NS = self.NT, self.NS
        d = self.din
        self.xT = d("xT", [D, NT])
        self.cT = d("cT", [128, 8, NS])
        self.ada_w = d("ada_w", [NL, D, 9 * D])
        self.adabT = d("adabT", [NL, 128, 72])
        self.nrm = d("nrm", [128, (NL * 3 + 1) * 8])
        self.wg = [d("ffn1_wg", [NL, D, FF]), d("ffn2_wg", [NL, D, FF])]
        self.wu = [d("ffn1_wu", [NL, D, FF]), d("ffn2_wu", [NL, D, FF])]
        self.wd = [d("ffn1_wd", [NL, FF, D]), d("ffn2_wd", [NL, FF, D])]
        self.w_in = d("w_in", [NL, D, PW])
        self.w_out = d("w_out", [NL, D, D])
        self.cst = d("cst", [128, CST_W])
        self.vec = d("vec", [128, NL * VEC_W])
        self.wgk = d("wgk", [NL, 33, 256])
        self.w_uq = d("mla_wuq", [NL, 256, 384])
        self.w_ukv = d("mla_wukv", [NL, 128, 512])
        self.rope = d("rope", [32, 2, max(self.SEQS)])
        self.poolw = d("poolw", [NL, 2, 128, 128])
        self.natab = d("natab", [NL, 4, 21, 128, 128])
        self.icnt = {}
        for Sq in sorted(set(self.SEQS)):
            self.icnt[Sq] = d("icnt%d" % Sq, [256, Sq])
        sc = self.dscr
        self.qTa = sc("qTa", [128, NT], BF16)
        self.kTa = sc("kTa", [128, NT], BF16)
        self.ktok = sc("ktok", [NT, 128], BF16)
        self.vtok = sc("vtok", [NT, 256], BF16)
        self.sptok = sc("sptok", [NT, 256], F32)
        self.goT = sc("goT", [4, 64, NT], BF16)
        self.uT = sc("uT", [256, NT], F32)
        self.qTc = sc("qTc", [4, 64, NT], BF16)
        self.kTc = sc("kTc", [4, 64, NT], BF16)
        self.vtokc = sc("vtokc", [NT, 260], BF16)
        self.qTd = sc("qTd", [4, 96, NT], BF16)
        self.kTd = sc("kTd", [4, 96, NT], BF16)
        self.vtokd = sc("vtokd", [NT, 260], BF16)
        self.yT = self.nc.dram_tensor("yT", [D, NT], F32, kind="ExternalOutput").ap()
        self.xs = self.dscr("xs", [D, NT], F32)
        self.ymix = self.dscr("ymix", [16, 64, NT], BF16)
        self.bar = self.nc.dram_tensor("bar", [2, 64], F32, kind="Internal").ap()
        self.pb = [self.nc.alloc_psum_tensor("pb%d" % i, [128, 512], F32) for i in range(8)]

    def prologue(self):
        S = self.S
        NS = self.NS
        self.cst_sb = self.sb([128, CST_W], F32, "cst")
        self.dma(self.cst_sb[:], self.cst, W=["cst"])
        self.nrm_sb = self.sb([128, (NL * 3 + 1) * 8], F32, "nrm")
        self.dma(self.nrm_sb[:], self.nrm, W=["nrm"])
        self.vec_sb = self.sb([128, NL * VEC_W], F32, "vec")
        self.dma(self.vec_sb[:], self.vec, W=["vec"])
        self.ones_f = self.sb([128, 64], F32, "onesf")
        self.memset(self.ones_f[:], 1.0, W=["onesf"])
        self.ident_bf = self.sb([128, 128], BF16, "ident")
        self.cp(self.ident_bf[:], self.cst_sb[:, C_ID:C_ID + 128], R=["cst"], W=["ident"])
        self.ones_bf = self.sb([128, 128], BF16, "ones")
        self.memset(self.ones_bf[:], 1.0, W=["ones"])
        self.eps_c = self.cst_sb[:, C_EPS:C_EPS + 1]
        self.one_c = self.cst_sb[:, C_ONE:C_ONE + 1]
        self.mod = [self.sb([128, 72, NS], F32, "mod%d" % l) for l in range(NL)]
        self.Gn = [self.sb([128, 3, 8, NS], F32, "Gn%d" % l) for l in range(NL)]
        self.gate = [self.sb([128, 3, 8, NS], F32, "gate%d" % l) for l in range(NL)]
        persist_end = self.sbp
        csil = self.sb([128, 8, NS], F32, "csil")
        adab = self.sb([128, NL, 72], F32, "adab")
        self.dma(csil[:], self.cT, W=["csil"])
        self.dma(adab[:], self.adabT.rearrange("l p j -> p l j"), W=["adab"])
        self.act(csil[:], csil[:], AF.Silu, R=["csil"], W=["csil"])
        wrot = self.rot("adaw", 2, [128, 8, 512], F32)
        ps = self.pb[0]
        for l in range(NL):
            psv = ps[:, 0:72 * NS].rearrange("p (j s) -> p j s", s=NS)
            for pc in range(18):
                wt, wk = wrot.next()
                src = self.ada_w[l].rearrange("(kc p) c -> p kc c", p=128)[:, :, pc * 512:(pc + 1) * 512]
                self.dma(wt[:], src, W=[wk], q=("sp" if pc % 2 == 0 else "act"))
                for jj in range(4):
                    j = pc * 4 + jj
                    for kc in range(8):
                        self.mm(psv[:, j, :], wt[:, kc, jj * 128:(jj + 1) * 128], csil[:, kc, :],
                                kc == 0, kc == 7, R=[wk, "csil"], W=["pb0"])
            self.tt(self.mod[l][:], psv, adab[:, l, :].unsqueeze(2).to_broadcast([128, 72, NS]), ALU.add,
                    R=["pb0", "adab"], W=["mod%d" % l])
            m = self.mod[l][:].rearrange("p (i oc) s -> p i oc s", i=9)
            for k, (isc, igate, gmul) in enumerate(((1, 2, 0.5), (4, 5, 1.0), (7, 8, 0.5))):
                nv = self.nrm_sb[:, (l * 3 + k) * 8:(l * 3 + k + 1) * 8]
                self.ts(self.Gn[l][:, k], m[:, isc], 1.0, ALU.add, R=["mod%d" % l], W=["Gn%d" % l])
                self.tt(self.Gn[l][:, k], self.Gn[l][:, k], nv.unsqueeze(2).to_broadcast([128, 8, NS]), ALU.mult,
                        R=["Gn%d" % l, "nrm"], W=["Gn%d" % l])
                self.ts(self.gate[l][:, k], m[:, igate], 1.0, ALU.add, R=["mod%d" % l], W=["gate%d" % l],
                        s2=gmul, op1=ALU.mult)
        self.sbp = persist_end
        self.persist_end = persist_end
        S.phase_barrier(self.bar[0:1, 0:8], self.cst[0:1, 0:8])

    def shift(self, l, k, kc, s):
        i = (0, 3, 6)[k]
        return self.mod[l][:, i * 8 + kc, s:s + 1]

    def norm_tile(self, x, xk, T, l, k, s, sq, sqk, t, tk, h, hk, ssb, rs, rsk):
        if isinstance(sq, Rot):
            for kc in range(8):
                sq1, sq1k = sq.next()
                self.act(sq1[:], x[:, kc, :], AF.Square, R=[xk], W=[sq1k])
                self.mm(ssb[:, 0:T], self.ones_bf[:], sq1[:], kc == 0, kc == 7, R=[sq1k, "ones"], W=["pb6"])
        else:
            self.act(sq[:], x[:], AF.Square, R=[xk], W=[sqk])
            for kc in range(8):
                self.mm(ssb[:, 0:T], self.ones_bf[:], sq[:, kc, :], kc == 0, kc == 7, R=[sqk, "ones"], W=["pb6"])
        self.act(rs[:], ssb[:, 0:T], AF.Sqrt, R=["pb6", "cst"], W=[rsk], bias=self.eps_c, scale=1.0 / D)
        self.recip(rs[:], rs[:], R=[rsk], W=[rsk])
        for kc in range(8):
            tt_, ttk = t.next()
            self.tt(tt_[:], x[:, kc, :], rs[:], ALU.mult, R=[xk, rsk], W=[ttk])
            self.act(h[:, kc, :], tt_[:], AF.Identity, R=[ttk, "Gn%d" % l, "mod%d" % l], W=[hk],
                     bias=self.shift(l, k, kc, s), scale=self.Gn[l][:, k, kc, s:s + 1])

    def ffn_sweep(self, l, which, src, dst, pre_wout=False, final_norm=False, final=False):
        S = self.S
        T = 256
        NT = self.NT
        k = 0 if which == 0 else 2
        self.sbp = self.persist_end
        wg = self.sb([128, 8, FF], BF16, "wg")
        wu = self.sb([128, 8, FF], BF16, "wu")
        wd = self.sb([128, 22, D], BF16, "wd")
        self.dma(wg[:], self.wg[which][l].rearrange("(kc p) f -> p kc f", p=128), W=["wg"], q="pool")
        self.dma(wu[:], self.wu[which][l].rearrange("(kc p) f -> p kc f", p=128), W=["wu"], q="pool")
        self.dma(wd[:], self.wd[which][l].rearrange("(fc p) d -> p fc d", p=128), W=["wd"], q="pool")
        if pre_wout:
            wo = self.sb([128, 8, D], BF16, "wo")
            self.dma(wo[:], self.w_out[l].rearrange("(kc p) d -> p kc d", p=128), W=["wo"], q="pool")
            yrot = self.rot("yt", 2, [128, 8, T], BF16)
        xrot = self.rot("x", 3, [128, 8, T], F32)
        sq = self.rot("sq", 3, [128, T], BF16)
        t = self.rot("t", 2, [128, T], F32)
        hrot = self.rot("h", 2, [128, 8, T], BF16)
        rsrot = self.rot("rs", 2, [128, T], F32)
        sgrot = self.rot("sg", 2, [128, T], BF16)
        arot = self.rot("a", 3, [128, T], BF16)
        acc = [self.pb[i] for i in range(4)]
        gurot = self.pbrot([4, 5])
        ssb = self.pb[6]

        def accv(oc):
            return acc[oc // 2][:, (oc % 2) * 256:(oc % 2) * 256 + T], "pb%d" % (oc // 2)

        ntiles = NT // T
        xsrc = src.rearrange("(kc p) t -> p kc t", p=128)
        xdst = dst.rearrange("(kc p) t -> p kc t", p=128)
        ysrc = self.ymix.rearrange("(kc two) f t -> (two f) kc t", two=2)
        state = {}

        def stage_load(i):
            x, xk = xrot.next()
            self.dma(x[:], xsrc[:, :, i * T:(i + 1) * T], W=[xk], q="sp")
            state[i] = dict(x=x, xk=xk)
            if pre_wout:
                y, yk = yrot.next()
                self.dma(y[:], ysrc[:, :, i * T:(i + 1) * T], W=[yk], q="sp")
                state[i].update(y=y, yk=yk)

        def norm_steps(i):
            st = state[i]
            s = self.seq_of(i * T)
            x, xk = st["x"], st["xk"]
            steps = []
            if pre_wout:
                y, yk = st["y"], st["yk"]
                slots = [(self.pb[6][:, 256:256 + T], "pb6"), (self.pb[7][:, 0:T], "pb7")]

                def pre(oc):
                    av, ak = slots[oc % 2]
                    for kc in range(8):
                        self.mm(av, wo[:, kc, oc * 128:(oc + 1) * 128], y[:, kc, :], kc == 0, kc == 7,
                                R=[yk, "wo"], W=[ak])
                    self.stt(x[:, oc, :], av, self.gate[l][:, 1, oc, s:s + 1], x[:, oc, :], ALU.mult, ALU.add,
                             R=[ak, xk, "gate%d" % l], W=[xk])
                for o2 in range(4):
                    steps.append(lambda o2=o2: (pre(2 * o2), pre(2 * o2 + 1)))
            h, hk = hrot.next()
            rs, rsk = rsrot.next()
            st.update(h=h, hk=hk)

            def n2():
                for kc in range(8):
                    sq1, sq1k = sq.next()
                    self.tt(sq1[:], x[:, kc, :], x[:, kc, :], ALU.mult, R=[xk], W=[sq1k], eng="pool")
                    self.mm(ssb[:, 0:T], self.ones_bf[:], sq1[:], kc == 0, kc == 7, R=[sq1k, "ones"], W=["pb6"])

            def n3():
                self.act(rs[:], ssb[:, 0:T], AF.Sqrt, R=["pb6", "cst"], W=[rsk], bias=self.eps_c, scale=1.0 / D)
                self.recip(rs[:], rs[:], R=[rsk], W=[rsk])

            def n4(kc):
                tt_, ttk = t.next()
                self.tt(tt_[:], x[:, kc, :], rs[:], ALU.mult, R=[xk, rsk], W=[ttk])
                self.ts(h[:, kc, :], tt_[:], self.Gn[l][:, k, kc, s:s + 1], ALU.mult,
                        R=[ttk, "Gn%d" % l, "mod%d" % l], W=[hk], s2=self.shift(l, k, kc, s), op1=ALU.add)
            steps.append(n2)
            steps.append(None)
            steps.append(n3)
            steps.append(None)
            for kc in range(8):
                steps.append(lambda kc=kc: n4(kc))
            return steps

        def stage_norm(i):
            for f in norm_steps(i):
                if f is not None:
                    f()

        def stage_body(i):
            st = state[i]
            s = self.seq_of(i * T)
            x, xk, h, hk = st["x"], st["xk"], st["h"], st["hk"]

            def gu(fc):
                gub, gk = gurot.next()
                uk = gk
                g = gub[:, 0:T]
                u = gub[:, 256:256 + T]
                for kc in range(8):
                    self.mm(g, wg[:, kc, fc * 128:(fc + 1) * 128], h[:, kc, :], kc == 0, kc == 7, R=[hk, "wg"], W=[gk])
                for kc in range(8):
                    self.mm(u, wu[:, kc, fc * 128:(fc + 1) * 128], h[:, kc, :], kc == 0, kc == 7, R=[hk, "wu"], W=[uk])
                return g, gk, u, uk

            nsteps = norm_steps(i + 1) if i + 1 < ntiles else []
            pend = gu(0)
            for fc in range(22):
                g, gk, u, uk = pend
                if fc + 1 < 22:
                    pend = gu(fc + 1)
                if fc >= 1 and nsteps:
                    f = nsteps.pop(0)
                    if f is not None:
                        f()
                sg, sgk = sgrot.next()
                a, ak = arot.next()
                self.act(sg[:], g, AF.Silu, R=[gk], W=[sgk])
                self.tt(a[:], u, sg[:], ALU.mult, R=[uk, sgk], W=[ak])
                for oc in range(8):
                    av, avk = accv(oc)
                    self.mm(av, wd[:, fc, oc * 128:(oc + 1) * 128], a[:], fc == 0 and oc % 2 == 0, fc == 21,
                            R=[ak, "wd"], W=[avk], skip=True)
            while nsteps:
                f = nsteps.pop(0)
                if f is not None:
                    f()
            for oc in range(8):
                av, avk = accv(oc)
                self.stt(x[:, oc, :], av, self.gate[l][:, k, oc, s:s + 1], x[:, oc, :], ALU.mult, ALU.add,
                         R=[avk, xk, "gate%d" % l], W=[xk])
            if final_norm:
                rs, rsk = rsrot.next()
                for kc in range(8):
                    sq1, sq1k = sq.next()
                    self.act(sq1[:], x[:, kc, :], AF.Square, R=[xk], W=[sq1k])
                    self.mm(ssb[:, 0:T], self.ones_bf[:], sq1[:], kc == 0, kc == 7, R=[sq1k, "ones"], W=["pb6"])
                self.act(rs[:], ssb[:, 0:T], AF.Sqrt, R=["pb6", "cst"], W=[rsk], bias=self.eps_c, scale=1.0 / D)
                self.recip(rs[:], rs[:], R=[rsk], W=[rsk])
                fn = self.nrm_sb[:, NL * 3 * 8:NL * 3 * 8 + 8]
                for kc in range(8):
                    self.stt(x[:, kc, :], x[:, kc, :], fn[:, kc:kc + 1], rs[:], ALU.mult, ALU.mult,
                             R=[xk, rsk, "nrm"], W=[xk])
            self.dma(xdst[:, :, i * T:(i + 1) * T], x[:], R=[xk], q="act", final=final)
            del state[i]

        stage_load(0)
        if ntiles > 1:
            stage_load(1)
        stage_norm(0)
        for i in range(ntiles):
            if i + 2 < ntiles:
                stage_load(i + 2)
            stage_body(i)
        S.phase_barrier(self.bar[0:1, 0:8], self.cst[0:1, 0:8])


    def proj_sweep(self, l):
        S = self.S
        T = 512
        NT = self.NT
        self.sbp = self.persist_end
        cst = self.cst_sb
        vec = self.vec_sb[:, l * VEC_W:(l + 1) * VEC_W]
        win = self.sb([128, 8, PW], BF16, "win")
        self.dma(win[:], self.w_in[l].rearrange("(kc p) c -> p kc c", p=128), W=["win"], q="pool")
        wkpe = self.sb([128, 8, 96], BF16, "wkpe")
        wkrh = self.sb([128, 8, 96], BF16, "wkrh")
        self.memset(wkpe[:], 0.0, W=["wkpe"])
        self.memset(wkrh[:], 0.0, W=["wkrh"])
        wsrc = self.w_in[l].rearrange("(kc p) c -> p kc c", p=128)
        self.dma(wkpe[:, :, 64:96], wsrc[:, :, 2208:2240], W=["wkpe"], q="pool")
        self.dma(wkrh[:, :, 64:80], wsrc[:, :, 2224:2240], W=["wkrh"], q="pool")
        self.dma(wkrh[:, :, 80:96], wsrc[:, :, 2208:2224], W=["wkrh"], q="pool")
        self.ts(wkrh[:, :, 64:80], wkrh[:, :, 64:80], -1.0, ALU.mult, R=["wkrh"], W=["wkrh"])
        wuq_f = self.sb([128, 2, 384], F32, "wuqf")
        self.dma(wuq_f[:], self.w_uq[l].rearrange("(kc p) c -> p kc c", p=128), W=["wuqf"])
        wuq = self.sb([128, 2, 384], BF16, "wuq")
        wuqr = self.sb([128, 2, 384], BF16, "wuqr")
        for kc in range(2):
            self.ts(wuq[:, kc, :], wuq_f[:, kc, :], vec[:, kc:kc + 1], ALU.mult, R=["wuqf", "vec"], W=["wuq"])
            self.cp(wuqr[:, kc, :], wuq[:, kc, :], R=["wuq"], W=["wuqr"])
            v4 = wuq[:, kc, :].rearrange("p (h c) -> p h c", c=96)
            r4 = wuqr[:, kc, :].rearrange("p (h c) -> p h c", c=96)
            self.ts(r4[:, :, 64:80], v4[:, :, 80:96], -1.0, ALU.mult, R=["wuq", "wuqr"], W=["wuqr"])
            self.cp(r4[:, :, 80:96], v4[:, :, 64:80], R=["wuq", "wuqr"], W=["wuqr"])
        wukv_f = self.sb([128, 512], F32, "wukvf")
        self.dma(wukv_f[:], self.w_ukv[l], W=["wukvf"])
        wukv = self.sb([128, 512], BF16, "wukv")
        self.ts(wukv[:], wukv_f[:], vec[:, 2:3], ALU.mult, R=["wukvf", "vec"], W=["wukv"])
        wukvv = self.sb([128, 4, 64], BF16, "wukvv")
        self.cp(wukvv[:], wukv[:].rearrange("p (h c) -> p h c", c=128)[:, :, 64:128], R=["wukv"], W=["wukvv"])
        wgk = self.sb([33, 256], BF16, "wgk")
        self.dma(wgk[:], self.wgk[l], W=["wgk"], q="pool")
        xrot = self.rot("x", 2, [128, 8, T], F32)
        csrot = self.rot("cs", 2, [96, 2, T], F32)
        sq = self.sb([128, 8, T], BF16, "sq")
        trot = self.rot("t", 2, [128, T], F32)
        hrot = self.rot("h", 2, [128, 8, T], BF16)
        rsrot = self.rot("rs", 2, [128, T], F32)
        lrT = self.sb([33, T], BF16, "lrT")
        self.memset(lrT[:], 1.0, W=["lrT"])
        cq_sb = self.sb([128, 2, T], BF16, "cq")
        sqq = self.sb([128, 2, T], BF16, "sqq")
        ckv_sb = self.sb([128, T], BF16, "ckv")
        sqkv = self.sb([128, T], BF16, "sqkv")
        rq = self.sb([128, T], F32, "rq")
        rkv = self.sb([128, T], F32, "rkv")
        rkvc = self.sb([128, 4], F32, "rkvc")
        t1 = self.sb([96, T], F32, "t1")
        t2 = self.sb([96, T], F32, "t2")
        etmp = self.sb([128, 256], F32, "etmp")
        st = {}
        for nm, shp, dt in (("q", [128, T], BF16), ("k", [128, T], BF16), ("go", [128, 2, T], BF16),
                            ("u", [128, 2, T], F32), ("qc", [128, 2, T], BF16), ("kc", [128, 2, T], BF16),
                            ("krot", [96, T], BF16), ("qd", [96, 4, T], BF16), ("kn", [64, 4, T], BF16),
                            ("kv", [128, 4, 384], BF16), ("vc", [128, 4, 4, 65], BF16), ("sp", [128, 4, 256], F32),
                            ("vd", [128, 4, 4, 65], BF16)):
            st[nm] = self.sb(shp, dt, "st_" + nm)
        self.memset(st["vc"][:], 1.0, W=["st_vc"])
        self.memset(st["vd"][:], 1.0, W=["st_vd"])
        prot = self.pbrot(range(6))
        ssb = self.pb[6]
        xsrc = self.xs.rearrange("(kc p) t -> p kc t", p=128)
        ntiles = NT // T
        state = {}

        def stage_load(i):
            x, xk = xrot.next()
            self.dma(x[:], xsrc[:, :, i * T:(i + 1) * T], W=[xk], q="sp")
            cs, csk = csrot.next()
            s = self.seq_of(i * T)
            p0 = i * T - self.off[s]
            self.dma(cs[64:96, :, :], self.rope[:, :, p0:p0 + T], W=[csk], q="sp")
            state[i] = dict(x=x, xk=xk, cs=cs, csk=csk)

        pending = []

        def norm_steps(i):
            sti = state[i]
            s = self.seq_of(i * T)
            x, xk = sti["x"], sti["xk"]
            h, hk = hrot.next()
            rs, rsk = rsrot.next()
            sti.update(h=h, hk=hk)
            steps = []

            def n2():
                self.act(sq[:], x[:], AF.Square, R=[xk], W=["sq"])
                for kc in range(8):
                    self.mm(ssb[:, 0:T], self.ones_bf[:], sq[:, kc, :], kc == 0, kc == 7, R=["sq", "ones"], W=["pb6"])

            def n3():
                self.act(rs[:], ssb[:, 0:T], AF.Sqrt, R=["pb6", "cst"], W=[rsk], bias=self.eps_c, scale=1.0 / D)
                self.recip(rs[:], rs[:], R=[rsk], W=[rsk])

            def n4(kc):
                tt_, ttk = trot.next()
                self.tt(tt_[:], x[:, kc, :], rs[:], ALU.mult, R=[xk, rsk], W=[ttk])
                self.ts(h[:, kc, :], tt_[:], self.Gn[l][:, 1, kc, s:s + 1], ALU.mult,
                        R=[ttk, "Gn%d" % l, "mod%d" % l], W=[hk], s2=self.shift(l, 1, kc, s), op1=ALU.add)
            steps.append(n2)
            steps.append(None)
            steps.append(n3)
            steps.append(None)
            for kc in range(8):
                steps.append(lambda kc=kc: n4(kc))
            return steps

        def stage_norm(i):
            for f in norm_steps(i):
                if f is not None:
                    f()

        def poll():
            if pending:
                f = pending.pop(0)
                if f is not None:
                    f()

        def stage_body(i):
            sti = state[i]
            h, hk, cs, csk = sti["h"], sti["hk"], sti["cs"], sti["csk"]
            t0 = i * T
            cosv = cs[64:96, 0, :]
            sinv = cs[64:96, 1, :]

            def fm(w, wk, c0, M):
                ps, pk = prot.next()
                for kc in range(8):
                    self.mm(ps[0:M, :], w[:, kc, c0:c0 + M], h[:, kc, :], kc == 0, kc == 7, R=[hk, wk], W=[pk])
                poll()
                return ps, pk

            ps, pk = fm(win, "win", 0, 128)
            self.act(st["q"][:], ps[:], AF.Copy, R=[pk], W=["st_q"])
            self.dma(self.qTa[:, t0:t0 + T], st["q"][:], R=["st_q"], q="pool")
            ps, pk = fm(win, "win", 128, 128)
            self.cp(st["k"][:], ps[:], R=[pk], W=["st_k"])
            self.dma(self.kTa[:, t0:t0 + T], st["k"][:], R=["st_k"], q="pool")
            for hp in range(2):
                ps, pk = fm(win, "win", 512 + hp * 128, 128)
                self.act(st["go"][:, hp, :], ps[:], AF.Silu, R=[pk], W=["st_go"])
            self.dma(self.goT[:, :, t0:t0 + T].rearrange("(hp two) p t -> (two p) hp t", two=2), st["go"][:],
                     R=["st_go"], q="pool")
            ps, pk = fm(win, "win", 768, 32)
            self.cp(lrT[0:32, :], ps[0:32, :], R=[pk], W=["lrT"])
            for c in range(2):
                ps, pk = fm(win, "win", 800 + c * 128, 128)
                self.act(st["u"][:, c, :], ps[:], AF.Copy, R=[pk], W=["st_u"])
            self.dma(self.uT.rearrange("(c p) t -> p c t", p=128)[:, :, t0:t0 + T], st["u"][:], R=["st_u"], q="pool")
            for hp in range(2):
                ps, pk = fm(win, "win", 1056 + hp * 128, 128)
                self.ts(st["qc"][:, hp, :], ps[:], 0.125, ALU.mult, R=[pk], W=["st_qc"])
            self.dma(self.qTc[:, :, t0:t0 + T].rearrange("(hp two) p t -> (two p) hp t", two=2), st["qc"][:],
                     R=["st_qc"], q="pool")
            for hp in range(2):
                ps, pk = fm(win, "win", 1312 + hp * 128, 128)
                self.act(st["kc"][:, hp, :], ps[:], AF.Copy, R=[pk], W=["st_kc"])
            self.dma(self.kTc[:, :, t0:t0 + T].rearrange("(hp two) p t -> (two p) hp t", two=2), st["kc"][:],
                     R=["st_kc"], q="pool")
            for c in range(2):
                ps, pk = fm(win, "win", 1824 + c * 128, 128)
                self.act(cq_sb[:, c, :], ps[:], AF.Copy, R=[pk], W=["cq"])
                self.act(sqq[:, c, :], ps[:], AF.Square, R=[pk], W=["sqq"])
            for c in range(2):
                self.mm(ssb[:, 0:T], self.ones_bf[:], sqq[:, c, :], c == 0, c == 1, R=["sqq", "ones"], W=["pb6"])
            self.act(rq[:], ssb[:, 0:T], AF.Ln, R=["pb6", "cst"], W=["rq"], bias=self.eps_c, scale=1.0 / 256)
            self.act(rq[:], rq[:], AF.Exp, R=["rq"], W=["rq"], scale=-0.5)
            ps, pk = fm(win, "win", 2080, 128)
            self.act(ckv_sb[:], ps[:], AF.Copy, R=[pk], W=["ckv"])
            self.act(sqkv[:], ps[:], AF.Square, R=[pk], W=["sqkv"])
            self.mm(ssb[:, 0:T], self.ones_bf[:], sqkv[:], True, True, R=["sqkv", "ones"], W=["pb6"])
            self.act(rkv[:], ssb[:, 0:T], AF.Ln, R=["pb6", "cst"], W=["rkv"], bias=self.eps_c, scale=1.0 / 128)
            self.act(rkv[:], rkv[:], AF.Exp, R=["rkv"], W=["rkv"], scale=-0.5)
            for j in range(4):
                self.mm(ssb[:, j:j + 1], sqkv[:, j * 128:(j + 1) * 128], self.ones_bf[:, 0:1], True, True,
                        R=["sqkv", "ones"], W=["pb6"])
            self.act(rkvc[:], ssb[:, 0:4], AF.Ln, R=["pb6", "cst"], W=["rkvc"], bias=self.eps_c, scale=1.0 / 128)
            self.act(rkvc[:], rkvc[:], AF.Exp, R=["rkvc"], W=["rkvc"], scale=-0.5)
            psA, pkA = fm(wkpe, "wkpe", 0, 96)
            psB, pkB = fm(wkrh, "wkrh", 0, 96)
            self.tt(t1[64:96, :], psA[64:96, :], cosv, ALU.mult, R=[pkA, csk], W=["t1"])
            self.tt(t2[64:96, :], psB[64:96, :], sinv, ALU.mult, R=[pkB, csk], W=["t2"])
            self.tt(st["krot"][64:96, :], t1[64:96, :], t2[64:96, :], ALU.add, R=["t1", "t2"], W=["st_krot"])
            for hh in range(4):
                self.dma(self.kTd[hh, 64:96, t0:t0 + T], st["krot"][64:96, :], R=["st_krot"], q="pool")
            for hh in range(4):
                psQ, pkQ = prot.next()
                for kc in range(2):
                    self.mm(psQ[0:96, :], wuq[:, kc, hh * 96:(hh + 1) * 96], cq_sb[:, kc, :], kc == 0, kc == 1,
                            R=["cq", "wuq"], W=[pkQ])
                psR, pkR = prot.next()
                for kc in range(2):
                    self.mm(psR[0:96, :], wuqr[:, kc, hh * 96:(hh + 1) * 96], cq_sb[:, kc, :], kc == 0, kc == 1,
                            R=["cq", "wuqr"], W=[pkR])
                self.stt(st["qd"][0:64, hh, :], psQ[0:64, :], MLA_SCALE, rq[0:64, :], ALU.mult, ALU.mult,
                         R=[pkQ, "rq"], W=["st_qd"])
                self.tt(t1[64:96, :], psQ[64:96, :], cosv, ALU.mult, R=[pkQ, csk], W=["t1"])
                self.tt(t2[64:96, :], psR[64:96, :], sinv, ALU.mult, R=[pkR, csk], W=["t2"])
                self.tt(t1[64:96, :], t1[64:96, :], t2[64:96, :], ALU.add, R=["t1", "t2"], W=["t1"])
                self.stt(st["qd"][64:96, hh, :], t1[64:96, :], MLA_SCALE, rq[64:96, :], ALU.mult, ALU.mult,
                         R=["t1", "rq"], W=["st_qd"])
            self.dma(self.qTd[:, :, t0:t0 + T].rearrange("h p t -> p h t"), st["qd"][:], R=["st_qd"], q="pool")
            for hh in range(4):
                ps, pk = prot.next()
                self.mm(ps[0:64, :], wukv[:, hh * 128:hh * 128 + 64], ckv_sb[:], True, True, R=["ckv", "wukv"], W=[pk])
                self.tt(st["kn"][:, hh, :], ps[0:64, :], rkv[0:64, :], ALU.mult, R=[pk, "rkv"], W=["st_kn"])
            self.dma(self.kTd[:, 0:64, t0:t0 + T].rearrange("h p t -> p h t"), st["kn"][:], R=["st_kn"], q="pool")
            for j in range(4):
                tok = slice(j * 128, (j + 1) * 128)
                ps, pk = prot.next()
                for kc in range(8):
                    self.mm(ps[:, 0:384], h[:, kc, tok], win[:, kc, 128:512], kc == 0, kc == 7, R=[hk, "win"], W=[pk])
                self.act(st["kv"][:, j, :], ps[:, 0:384], AF.Copy, R=[pk], W=["st_kv"])
                ps, pk = prot.next()
                for kc in range(8):
                    self.mm(ps[:, 0:256], h[:, kc, tok], win[:, kc, 1568:1824], kc == 0, kc == 7, R=[hk, "win"], W=[pk])
                self.cp(st["vc"][:, j, :, 0:64], ps[:, 0:256].rearrange("p (h c) -> p h c", c=64), R=[pk], W=["st_vc"])
                ps, pk = prot.next()
                self.mm(ps[:, 0:256], lrT[0:33, tok], wgk[0:33, :], True, True, R=["lrT", "wgk"], W=[pk])
                self.act(etmp[:], ps[:, 0:256], AF.Exp, R=[pk], W=["etmp"], scale=-1.0)
                self.act(st["sp"][:, j, :], etmp[:], AF.Ln, R=["etmp", "cst"], W=["st_sp"], bias=self.one_c)
                ps, pk = prot.next()
                self.mm(ps[:, 0:256], ckv_sb[:, tok], wukvv[:].rearrange("p h c -> p (h c)"), True, True,
                        R=["ckv", "wukvv"], W=[pk])
                self.ts(st["vd"][:, j, :, 0:64], ps[:, 0:256].rearrange("p (h c) -> p h c", c=64), rkvc[:, j:j + 1],
                        ALU.mult, R=[pk, "rkvc"], W=["st_vd"])
            tmv = lambda d: d[t0:t0 + T, :].rearrange("(j p) c -> p j c", p=128)
            self.dma(tmv(self.ktok), st["kv"][:, :, 0:128], R=["st_kv"], q="pool")
            self.dma(tmv(self.vtok), st["kv"][:, :, 128:384], R=["st_kv"], q="pool")
            self.dma(tmv(self.vtokc), st["vc"][:].rearrange("p j h c -> p j (h c)"), R=["st_vc"], q="pool")
            self.dma(tmv(self.sptok), st["sp"][:], R=["st_sp"], q="pool")
            self.dma(tmv(self.vtokd), st["vd"][:].rearrange("p j h c -> p j (h c)"), R=["st_vd"], q="pool")
            del state[i]

        stage_load(0)
        stage_norm(0)
        for i in range(ntiles):
            if i + 1 < ntiles:
                stage_load(i + 1)
                pending.extend(norm_steps(i + 1))
            stage_body(i)
            while pending:
                poll()
        S.phase_barrier(self.bar[0:1, 0:8], self.cst[0:1, 0:8])


    def pool_mixer(self, l, s):
        Sq = self.SEQS[s]
        o = self.off[s]
        TP = 1024 if Sq >= 1024 else Sq
        L = TP + 16
        self.sbp = self.persist_end
        vec = self.vec_sb[:, l * VEC_W:(l + 1) * VEC_W]
        pw_f = self.sb([128, 2, 128], F32, "pwf")
        pw = self.sb([128, 2, 128], BF16, "pw")
        self.dma(pw_f[:], self.poolw[l].rearrange("c p q -> p c q"), W=["pwf"])
        self.cp(pw[:], pw_f[:], R=["pwf"], W=["pw"])
        urot = self.rot("u", 2, [128, 2, L], F32)
        icrot = self.rot("ic", 2, [128, 2, TP], F32)
        P1 = self.sb([128, 2, L], F32, "P1")
        Q = self.sb([128, 2, L], F32, "Q")
        Rr = self.sb([128, L], F32, "R")
        Tt = self.sb([128, L], F32, "Tt")
        dd = self.sb([128, 2, TP], F32, "dd")
        db = self.sb([128, 2, TP], BF16, "db")
        yo = self.rot("yo", 2, [128, 2, TP], BF16)
        prot = self.pbrot(range(4))
        usrc = self.uT.rearrange("(c p) t -> p c t", p=128)
        icsrc = self.icnt[Sq].rearrange("(c p) t -> p c t", p=128)
        ydst = self.ymix[4:8].rearrange("(c two) f t -> (two f) c t", two=2)
        for it in range(Sq // TP):
            p0 = it * TP
            u, uk = urot.next()
            ic, ick = icrot.next()
            lo = max(p0 - 8, 0)
            hi = min(p0 + TP + 8, Sq)
            if lo > p0 - 8:
                self.memset(u[:, :, 0:8], 0.0, W=[uk])
            if hi < p0 + TP + 8:
                self.memset(u[:, :, L - 8:L], 0.0, W=[uk])
            self.dma(u[:, :, lo - (p0 - 8):hi - (p0 - 8)], usrc[:, :, o + lo:o + hi], W=[uk])
            self.dma(ic[:], icsrc[:, :, p0:p0 + TP], W=[ick])
            self.tt(P1[:, :, 1:L], u[:, :, 0:L - 1], u[:, :, 1:L], ALU.add, R=[uk], W=["P1"])
            self.tt(Q[:, :, 2:L - 2], P1[:, :, 1:L - 3], P1[:, :, 3:L - 1], ALU.add, R=["P1"], W=["Q"])
            self.tt(Rr[:, 4:L - 4], Q[:, 1, 2:L - 6], Q[:, 1, 6:L - 2], ALU.add, R=["Q"], W=["R"])
            self.tt(Tt[64:128, 8:L - 8], Rr[64:128, 4:L - 12], Rr[64:128, 12:L - 4], ALU.add, R=["R"], W=["Tt"])
            c8 = slice(8, 8 + TP)
            self.tt(dd[0:64, 0, :], P1[0:64, 0, c8], ic[0:64, 0, :], ALU.mult, R=["P1", ick], W=["dd"])
            self.tt(dd[64:128, 0, :], Q[64:128, 0, c8], ic[64:128, 0, :], ALU.mult, R=["Q", ick], W=["dd"])
            self.tt(dd[0:64, 1, :], Rr[0:64, c8], ic[0:64, 1, :], ALU.mult, R=["R", ick], W=["dd"])
            self.tt(dd[64:128, 1, :], Tt[64:128, c8], ic[64:128, 1, :], ALU.mult, R=["Tt", ick], W=["dd"])
            self.tt(db[:], dd[:], u[:, :, c8], ALU.subtract, R=["dd", uk], W=["db"])
            y, yk = yo.next()
            for c in range(2):
                for q in range(TP // 512):
                    ps, pk = prot.next()
                    self.mm(ps[:], pw[:, c, :], db[:, c, q * 512:(q + 1) * 512], True, True, R=["db", "pw"], W=[pk])
                    self.act(y[:, c, q * 512:(q + 1) * 512], ps[:], AF.Copy, R=[pk, "vec"], W=[yk],
                             scale=vec[:, 3 + c:4 + c])
            self.dma(ydst[:, :, o + p0:o + p0 + TP], y[:], R=[yk], q="pool")
        self.S.phase_barrier(self.bar[0:1, 0:8], self.cst[0:1, 0:8])

    def mla_mixer(self, l, s):
        Sq = self.SEQS[s]
        o = self.off[s]
        NK = Sq // 128
        self.sbp = self.persist_end
        KT = self.sb([96, 4, Sq], BF16, "KT")
        self.dma(KT[:], self.kTd[:, :, o:o + Sq].rearrange("h p t -> p h t"), W=["KT"])
        Vp = self.sb([128, NK, 4, 65], BF16, "Vp")
        self.dma(Vp[:].rearrange("p n h c -> p n (h c)"), self.vtokd[o:o + Sq, :].rearrange("(n p) c -> p n c", p=128),
                 W=["Vp"])
        qrot = self.rot("QT", 2, [96, 4, 512], BF16)
        prot_p = self.rot("pT", 3, [128, 512], BF16)
        rec = self.sb([65, 512], F32, "rec")
        bcs = self.sb([64, 512], F32, "bcs")
        yrot = self.rot("y", 2, [64, 4, 512], BF16)
        srot = self.pbrot([0, 1, 2])
        orot = self.pbrot([3, 4])
        bcp = self.pb[5]
        NQ = Sq // 512
        qts = {}

        def load_q(qt):
            QT, qk = qrot.next()
            self.dma(QT[:], self.qTd[:, :, o + qt * 512:o + (qt + 1) * 512].rearrange("h p t -> p h t"), W=[qk])
            qts[qt] = (QT, qk)

        items = [(qt, h, kc) for qt in range(NQ) for h in range(4) for kc in range(NK)]
        load_q(0)
        pend = []

        def qk_mm(it):
            qt, h, kc = it
            QT, qk = qts[qt]
            sp_, sk = srot.next()
            self.mm(sp_[:], KT[:, h, kc * 128:(kc + 1) * 128], QT[:, h, :], True, True, R=["KT", qk], W=[sk])
            pend.append((sp_, sk))

        y = yk = None
        cur_o = None
        for n, (qt, h, kc) in enumerate(items):
            if h == 0 and kc == 0:
                y, yk = yrot.next()
                if qt + 1 < NQ:
                    load_q(qt + 1)
            if n == 0:
                qk_mm(items[0])
                if len(items) > 1:
                    qk_mm(items[1])
            if kc == 0:
                cur_o = orot.next()
            ops_, ok = cur_o
            if n + 2 < len(items):
                qk_mm(items[n + 2])
            sp_, sk = pend.pop(0)
            pT, pk = prot_p.next()
            self.act(pT[:], sp_[:], AF.Exp, R=[sk], W=[pk])
            self.mm(ops_[0:65, :], Vp[:, kc, h, :], pT[:], kc == 0, kc == NK - 1, R=["Vp", pk], W=[ok])
            if kc == NK - 1:
                self.act(rec[64:65, :], ops_[64:65, :], AF.Ln, R=[ok], W=["rec"])
                self.act(rec[64:65, :], rec[64:65, :], AF.Exp, R=["rec"], W=["rec"], scale=-1.0)
                self.mm(bcp[0:64, :], self.ones_f[64:65, 0:64], rec[64:65, :], True, True, R=["rec", "onesf"], W=["pb5"])
                self.cp(bcs[:], bcp[0:64, :], R=["pb5"], W=["bcs"])
                self.tt(y[:, h, :], ops_[0:64, :], bcs[:], ALU.mult, R=[ok, "bcs"], W=[yk])
                if h == 3:
                    self.dma(self.ymix[12:16, :, o + qt * 512:o + (qt + 1) * 512].rearrange("h p t -> p h t"), y[:],
                             R=[yk], q="pool")
        self.S.phase_barrier(self.bar[0:1, 0:8], self.cst[0:1, 0:8])

    def na_mixer(self, l, s):
        Sq = self.SEQS[s]
        o = self.off[s]
        P = Sq // 128
        self.sbp = self.persist_end
        KT = self.sb([64, 4, Sq], BF16, "KT")
        self.dma(KT[:], self.kTc[:, :, o:o + Sq].rearrange("h p t -> p h t"), W=["KT"])
        Vp = self.sb([128, P, 4, 65], BF16, "Vp")
        self.dma(Vp[:].rearrange("p n h c -> p n (h c)"), self.vtokc[o:o + Sq, :].rearrange("(n p) c -> p n c", p=128),
                 W=["Vp"])
        tab = self.sb([128, 4, 21, 128], BF16, "tab")
        for h in range(4):
            self.dma(tab[:, h], self.natab[l, h].rearrange("v k q -> k v q"), W=["tab"], q="pool")
        sT_r = self.rot("sT", 4, [128, 640], F32)
        qrot = self.rot("QT", 2, [64, 4, 512], BF16)
        prot_p = self.rot("pT", 3, [128, 640], BF16)
        rec = self.sb([65, 512], F32, "rec")
        bcs = self.sb([64, 512], F32, "bcs")
        yrot = self.rot("y", 2, [64, 4, 512], BF16)
        arot = self.pbrot([0, 1, 2])
        brot = self.pbrot([3, 7])
        orot = self.pbrot([4, 5])
        bcp = self.pb[6]

        def chunks(i):
            if P >= 5 and 2 <= i <= P - 3:
                return [(i - 2 + d, d) for d in range(5)]
            if i == 0:
                return [(j, 5 + j) for j in range(4)]
            if i == 1:
                return [(j, 9 + j) for j in range(4)]
            if i == P - 2:
                return [(P - 4 + j, 13 + j) for j in range(4)]
            assert i == P - 1
            return [(P - 4 + j, 17 + j) for j in range(4)]

        NG = Sq // 512
        qts = {}

        def load_q(g):
            QT, qk = qrot.next()
            self.dma(QT[:], self.qTc[:, :, o + g * 512:o + (g + 1) * 512].rearrange("h p t -> p h t"), W=[qk])
            qts[g] = (QT, qk)

        items = [(g, ip, h) for g in range(NG) for ip in range(4) for h in range(4)]
        pend = []

        def stage1(it):
            g, ip, h = it
            QT, qk = qts[g]
            ch = chunks(g * 4 + ip)
            pa, pak = arot.next()
            pb_, pbk = (None, None)
            if len(ch) > 4:
                pb_, pbk = brot.next()
            for ci, (j, tix) in enumerate(ch):
                dst = pa[:, ci * 128:(ci + 1) * 128] if ci < 4 else pb_[:, 0:128]
                dk = pak if ci < 4 else pbk
                self.mm(dst, KT[:, h, j * 128:(j + 1) * 128], QT[:, h, ip * 128:(ip + 1) * 128], True, True,
                        R=["KT", qk], W=[dk])
            sT, stk = sT_r.next()
            t0x = ch[0][1]
            n4 = min(len(ch), 4)
            self.tt(sT[:, 0:n4 * 128].rearrange("p (c q) -> p c q", c=n4), pa[:, 0:n4 * 128].rearrange("p (c q) -> p c q", c=n4),
                    tab[:, h, t0x:t0x + n4, :], ALU.add, R=[pak, "tab"], W=[stk])
            if len(ch) > 4:
                self.tt(sT[:, 512:640], pb_[:, 0:128], tab[:, h, t0x + 4, :], ALU.add, R=[pbk, "tab"], W=[stk])
            pend.append((ch, sT, stk))

        y = yk = None
        cur_o = None
        for n, (g, ip, h) in enumerate(items):
            if ip == 0 and h == 0:
                y, yk = yrot.next()
                if n == 0:
                    load_q(0)
                    stage1(items[0])
                    if len(items) > 1:
                        stage1(items[1])
                if g + 1 < NG:
                    load_q(g + 1)
            if h == 0:
                cur_o = orot.next()
            ops_, ok = cur_o
            if n + 2 < len(items):
                stage1(items[n + 2])
            ch, sT, stk = pend.pop(0)
            pT, pk = prot_p.next()
            self.act(pT[:, 0:len(ch) * 128], sT[:, 0:len(ch) * 128], AF.Exp, R=[stk], W=[pk])
            for ci, (j, tix) in enumerate(ch):
                self.mm(ops_[0:65, h * 128:(h + 1) * 128], Vp[:, j, h, :], pT[:, ci * 128:(ci + 1) * 128],
                        ci == 0, ci == len(ch) - 1, R=["Vp", pk], W=[ok])
            if h == 3:
                self.act(rec[64:65, :], ops_[64:65, :], AF.Ln, R=[ok], W=["rec"])
                self.act(rec[64:65, :], rec[64:65, :], AF.Exp, R=["rec"], W=["rec"], scale=-1.0)
                self.mm(bcp[0:64, :], self.ones_f[64:65, 0:64], rec[64:65, :], True, True, R=["rec", "onesf"], W=["pb6"])
                self.cp(bcs[:], bcp[0:64, :], R=["pb6"], W=["bcs"])
                self.tt(y[:, :, ip * 128:(ip + 1) * 128], ops_[0:64, :].rearrange("p (h t) -> p h t", h=4),
                        bcs[:].rearrange("p (h t) -> p h t", h=4), ALU.mult, R=[ok, "bcs"], W=[yk])
                if ip == 3:
                    self.dma(self.ymix[8:12, :, o + g * 512:o + (g + 1) * 512].rearrange("h p t -> p h t"), y[:],
                             R=[yk], q="pool")
        self.S.phase_barrier(self.bar[0:1, 0:8], self.cst[0:1, 0:8])


    def gla_mixer(self, l, s):
        Sq = self.SEQS[s]
        o = self.off[s]
        NCH = Sq // 128
        NSC = Sq // 512
        self.sbp = self.persist_end
        cst = self.cst_sb
        vec = self.vec_sb[:, l * VEC_W:(l + 1) * VEC_W]
        triu = cst[:, C_TRIU:C_TRIU + 128]
        tril = cst[:, C_TRIL:C_TRIL + 128]
        nsix = cst[:, C_NSIX:C_NSIX + 1]
        bmask = cst[:, C_BM:C_BM + 256]
        maskF = cst[:, C_MF:C_MF + 128]
        maskB = cst[:, C_MB:C_MB + 128]
        Sf_all = self.sb([128, NCH, 256], BF16, "Sfall")
        Sf = self.sb([128, 256], F32, "Sf")
        Sb = self.sb([128, 256], F32, "Sb")
        Sb_bf = self.sb([128, 256], BF16, "Sbbf")
        tmp = self.sb([128, 256], F32, "tmp")
        ebl = self.sb([128, 1], F32, "ebl")
        einv_tok = self.sb([128, 128], F32, "einvtok")
        ktil = self.sb([128, 128], BF16, "ktil")
        ktok_r = self.rot("ktok", 3, [128, 4, 128], BF16)
        vtok_r = self.rot("vtok", 3, [128, 4, 256], BF16)
        sp_r = self.rot("sp", 3, [128, 4, 256], F32)
        q_r = self.rot("q4", 3, [128, 512], BF16)
        k_r = self.rot("k4", 3, [128, 512], BF16)
        go_r = self.rot("go4", 3, [64, 4, 512], BF16)
        y_r = self.rot("y4", 2, [64, 4, 512], BF16)
        E = self.sb([128, 256], F32, "E")
        EI = self.sb([128, 384], F32, "EI")
        ecf = self.sb([128, 1], F32, "ecf")
        Qf_r = self.rot("Qf", 2, [128, 4, 128], BF16)
        Qb_r = self.rot("Qb", 2, [128, 4, 128], BF16)
        Kf = self.sb([128, 128], BF16, "Kf")
        Kb = self.sb([128, 128], BF16, "Kb")
        ktilb = self.sb([128, 128], BF16, "ktilb")
        Amf = self.sb([128, 4, 128], BF16, "Amf")
        Amb = self.sb([128, 4, 128], BF16, "Amb")
        osq = self.sb([64, 512], BF16, "osq")
        rr = self.sb([64, 512], F32, "rr")
        tt_ = self.sb([64, 512], F32, "tt")
        ktsrc = self.ktok[o:o + Sq, :].rearrange("(n p) c -> p n c", p=128)
        vtsrc = self.vtok[o:o + Sq, :].rearrange("(n p) c -> p n c", p=128)
        spsrc = self.sptok[o:o + Sq, :].rearrange("(n p) c -> p n c", p=128)

        def load_tok(sc, want_qk):
            d = {}
            kt, d["ktk"] = ktok_r.next()
            vt, d["vtk"] = vtok_r.next()
            sp, d["spk"] = sp_r.next()
            self.dma(kt[:], ktsrc[:, sc * 4:(sc + 1) * 4, :], W=[d["ktk"]])
            self.dma(vt[:], vtsrc[:, sc * 4:(sc + 1) * 4, :], W=[d["vtk"]])
            self.dma(sp[:], spsrc[:, sc * 4:(sc + 1) * 4, :], W=[d["spk"]])
            d.update(kt=kt, vt=vt, sp=sp)
            if want_qk:
                q4, d["qk"] = q_r.next()
                k4, d["kk"] = k_r.next()
                go, d["gok"] = go_r.next()
                tsl = slice(o + sc * 512, o + (sc + 1) * 512)
                self.dma(q4[:], self.qTa[:, tsl], W=[d["qk"]])
                self.dma(k4[:], self.kTa[:, tsl], W=[d["kk"]])
                self.dma(go[:], self.goT[:, :, tsl].rearrange("h p t -> p h t"), W=[d["gok"]])
                d.update(q4=q4, k4=k4, go=go)
            return d

        self.memset(Sf[:], 0.0, W=["Sf"])
        self.memset(Sf_all[:, 0, :], 0.0, W=["Sfall"])
        yb = self.pbrot([0, 1])
        kvb = self.pbrot([2, 3])
        ebl_r = self.rot("ebl1", 2, [128, 1], F32)
        einv_r = self.rot("einv1", 2, [128, 128], F32)
        ktil_r = self.rot("ktil1", 2, [128, 128], BF16)
        toks1 = {}

        def get_tok1(sc):
            if sc not in toks1:
                toks1[sc] = load_tok(sc, False)
            return toks1[sc]

        def prep1(n):
            sc, c = n // 4, n % 4
            cur = get_tok1(sc)
            if c == 1 and sc + 1 < NSC:
                get_tok1(sc + 1)
            spf = cur["sp"][:, c, 0:128]
            Y, yk = yb.next()
            self.mm(Y[:, 0:128], triu, spf, True, True, R=["cst", cur["spk"]], W=[yk])
            self.mm(Y[:, 128:129], spf, nsix, True, True, R=["cst", cur["spk"]], W=[yk])
            einv, eik = einv_r.next()
            ebl1, eblk = ebl_r.next()
            ktil1, ktk = ktil_r.next()
            self.act(einv[:], Y[:, 0:128], AF.Exp, R=[yk], W=[eik], scale=-1.0)
            self.act(ebl1[:], Y[:, 128:129], AF.Exp, R=[yk], W=[eblk])
            self.tt(ktil1[:], cur["kt"][:, c, :], einv[:], ALU.mult, R=[cur["ktk"], eik], W=[ktk], eng="pool")
            KV, kvk = kvb.next()
            self.mm(KV[:, 0:256], ktil1[:], cur["vt"][:, c, :], True, True, R=[ktk, cur["vtk"]], W=[kvk])
            return KV, kvk, ebl1, eblk

        if NCH > 1:
            pn = prep1(0)
            for n in range(NCH - 1):
                pnext = prep1(n + 1) if n + 1 < NCH - 1 else None
                KV, kvk, ebl1, eblk = pn
                self.tt(tmp[:], KV[:, 0:256], Sf[:], ALU.add, R=[kvk, "Sf"], W=["tmp"])
                self.stt(Sf[:], tmp[:], ebl1[:, 0:1], bmask, ALU.mult, ALU.mult, R=["tmp", eblk, "cst"], W=["Sf"])
                self.cp(Sf_all[:, n + 1, :], Sf[:], R=["Sf"], W=["Sfall"], eng="pool")
                pn = pnext
        self.memset(Sb[:], 0.0, W=["Sb"])
        self.memset(Sb_bf[:], 0.0, W=["Sbbf"])
        xb = self.pbrot([0, 1])
        ob = self.pbrot([2, 3])
        afp, abp, kvp, ssp = self.pb[4], self.pb[5], self.pb[6], self.pb[7]
        E_r = self.rot("E2", 2, [128, 256], F32)
        EI_r = self.rot("EI2", 2, [128, 384], F32)
        ecf_r = self.rot("ecf2", 2, [128, 1], F32)
        Kf_r = self.rot("Kf2", 2, [128, 128], BF16)
        Kb_r = self.rot("Kb2", 2, [128, 128], BF16)
        ktb_r = self.rot("ktb2", 2, [128, 128], BF16)
        Amf_r = self.rot("Amf2", 2, [128, 4, 128], BF16)
        Amb_r = self.rot("Amb2", 2, [128, 4, 128], BF16)
        toks = {}
        ys = {}

        def get_tok(sc):
            if sc not in toks:
                toks[sc] = load_tok(sc, True)
            return toks[sc]

        def stage_P(n):
            sc, c = n // 4, n % 4
            cur = get_tok(sc)
            csl = slice(c * 128, (c + 1) * 128)
            spf = cur["sp"][:, c, 0:128]
            spb = cur["sp"][:, c, 128:256]
            X, xk = xb.next()
            R0 = ["cst", cur["spk"]]
            self.mm(X[:, 0:128], spf, triu, True, True, R=R0, W=[xk])
            self.mm(X[:, 128:256], spb, tril, True, True, R=R0, W=[xk])
            self.mm(X[:, 256:384], tril, spb, True, True, R=R0, W=[xk])
            self.mm(X[:, 384:385], spb, nsix, True, True, R=R0, W=[xk])
            E, ek = E_r.next()
            EI, eik = EI_r.next()
            ecf, ecfk = ecf_r.next()
            self.act(E[:], X[:, 0:256], AF.Exp, R=[xk], W=[ek])
            self.act(EI[:], X[:, 0:384], AF.Exp, R=[xk], W=[eik], scale=-1.0)
            self.act(ecf[:], X[:, 384:385], AF.Exp, R=[xk], W=[ecfk])
            Qf, qfk = Qf_r.next()
            Qb, qbk = Qb_r.next()
            for h in range(4):
                hm = cst[:, C_HM + h:C_HM + h + 1]
                self.stt(Qf[:, h, :], cur["q4"][:, csl], hm, E[:, 0:128], ALU.mult, ALU.mult,
                         R=[cur["qk"], ek, "cst"], W=[qfk])
                self.stt(Qb[:, h, :], cur["q4"][:, csl], hm, E[:, 128:256], ALU.mult, ALU.mult,
                         R=[cur["qk"], ek, "cst"], W=[qbk])
            Kf, kfk = Kf_r.next()
            Kb, kbk = Kb_r.next()
            ktilb, ktbk = ktb_r.next()
            self.tt(Kf[:], cur["k4"][:, csl], EI[:, 0:128], ALU.mult, R=[cur["kk"], eik], W=[kfk], eng="pool")
            self.tt(Kb[:], cur["k4"][:, csl], EI[:, 128:256], ALU.mult, R=[cur["kk"], eik], W=[kbk], eng="pool")
            self.tt(ktilb[:], cur["kt"][:, c, :], EI[:, 256:384], ALU.mult, R=[cur["ktk"], eik], W=[ktbk], eng="pool")
            for h in range(4):
                self.mm(afp[:, h * 128:(h + 1) * 128], Kf[:], Qf[:, h, :], True, True, R=[kfk, qfk], W=["pb4"])
            for h in range(4):
                self.mm(abp[:, h * 128:(h + 1) * 128], Kb[:], Qb[:, h, :], True, True, R=[kbk, qbk], W=["pb5"])
            Amf, amfk = Amf_r.next()
            Amb, ambk = Amb_r.next()
            self.tt(Amf[:], afp[:].rearrange("p (h t) -> p h t", h=4), maskF.unsqueeze(1).to_broadcast([128, 4, 128]),
                    ALU.mult, R=["pb4", "cst"], W=[amfk])
            self.tt(Amb[:], abp[:].rearrange("p (h t) -> p h t", h=4), maskB.unsqueeze(1).to_broadcast([128, 4, 128]),
                    ALU.mult, R=["pb5", "cst"], W=[ambk])
            return dict(cur=cur, c=c, sc=sc, csl=csl, ecf=ecf, ecfk=ecfk, Qf=Qf, qfk=qfk, Qb=Qb, qbk=qbk,
                        ktilb=ktilb, ktbk=ktbk, Amf=Amf, amfk=amfk, Amb=Amb, ambk=ambk)

        def stage_Q(n, P):
            cur, c, sc, csl = P["cur"], P["c"], P["sc"], P["csl"]
            if sc not in ys:
                ys[sc] = y_r.next()
            y4, y4k = ys[sc]
            O, ok = ob.next()
            for h in range(4):
                od = O[0:64, h * 128:(h + 1) * 128]
                vh = cur["vt"][:, c, h * 64:(h + 1) * 64]
                self.mm(od, vh, P["Amf"][:, h, :], True, False, R=[cur["vtk"], P["amfk"]], W=[ok])
                self.mm(od, vh, P["Amb"][:, h, :], False, False, R=[cur["vtk"], P["ambk"]], W=[ok])
                self.mm(od, Sf_all[:, n, h * 64:(h + 1) * 64], P["Qf"][:, h, :], False, False, R=["Sfall", P["qfk"]], W=[ok])
                self.mm(od, Sb_bf[:, h * 64:(h + 1) * 64], P["Qb"][:, h, :], False, True, R=["Sbbf", P["qbk"]], W=[ok])
            self.mm(kvp[:, 0:256], P["ktilb"][:], cur["vt"][:, c, :], True, True, R=[P["ktbk"], cur["vtk"]], W=["pb6"])
            self.tt(tmp[:], kvp[:, 0:256], Sb[:], ALU.add, R=["pb6", "Sb"], W=["tmp"])
            self.stt(Sb[:], tmp[:], P["ecf"][:, 0:1], bmask, ALU.mult, ALU.mult, R=["tmp", P["ecfk"], "cst"], W=["Sb"])
            self.cp(Sb_bf[:], Sb[:], R=["Sb"], W=["Sbbf"], eng="pool")
            self.act(osq[:], O[0:64, :], AF.Square, R=[ok], W=["osq"])
            self.mm(ssp[0:64, :], self.ones_bf[0:64, 0:64], osq[:], True, True, R=["osq", "ones"], W=["pb7"])
            self.act(rr[:], ssp[0:64, :], AF.Ln, R=["pb7", "cst"], W=["rr"], bias=self.eps_c[0:64, :], scale=1.0 / 64)
            self.act(rr[:], rr[:], AF.Exp, R=["rr"], W=["rr"], scale=-0.5)
            self.tt(tt_[:], O[0:64, :], rr[:], ALU.mult, R=[ok, "rr"], W=["tt"])
            self.stt(y4[:, :, csl], tt_[:].rearrange("p (h t) -> p h t", h=4), vec[0:64, 5:6], cur["go"][:, :, csl],
                     ALU.mult, ALU.mult, R=["tt", "vec", cur["gok"]], W=[y4k])
            if c == 0:
                self.dma(self.ymix[0:4, :, o + sc * 512:o + (sc + 1) * 512].rearrange("h p t -> p h t"), y4[:], R=[y4k],
                         q="pool")
                del toks[sc]

        Pn = stage_P(NCH - 1)
        for n in range(NCH - 1, -1, -1):
            Pnext = None
            if n - 1 >= 0:
                if (n - 1) % 4 == 1 and (n - 1) // 4 - 1 >= 0:
                    get_tok((n - 1) // 4 - 1)
                Pnext = stage_P(n - 1)
            stage_Q(n, Pn)
            Pn = Pnext
        self.S.phase_barrier(self.bar[0:1, 0:8], self.cst[0:1, 0:8])


C_EPS = 0
C_NSIX = 1
C_ONE = 2
C_HM = 4
C_BM = 8
C_TRIU = 264
C_TRIL = 392
C_MF = 520
C_MB = 648
C_ID = 776
CST_W = 904
VEC_W = 8
POOL_WINDOWS = (2, 4, 8, 16)


def host_consts():
    c = np.zeros((128, CST_W), np.float32)
    c[:, C_EPS] = EPS
    c[:, C_NSIX] = -1.0 / 16.0
    c[:, C_ONE] = 1.0
    p = np.arange(128)
    for h in range(4):
        c[:, C_HM + h] = (p // 32 == h) * (32.0 ** -0.5)
    f = np.arange(256)
    c[:, C_BM:C_BM + 256] = (f[None, :] // 64 == p[:, None] // 32)
    j = p[:, None]
    i = p[None, :]
    c[:, C_TRIU:C_TRIU + 128] = (j <= i) * (-1.0 / 16.0)
    c[:, C_TRIL:C_TRIL + 128] = (j >= i) * (-1.0 / 16.0)
    c[:, C_MF:C_MF + 128] = (j <= i)
    c[:, C_MB:C_MB + 128] = (j > i)
    c[:, C_ID:C_ID + 128] = (j == i)
    return c


def host_rope(smax):
    half = 16
    inv = (10000.0 ** (-np.arange(half, dtype=np.float32) / half)).astype(np.float32)
    pos = np.arange(smax, dtype=np.float32)
    ang = (pos[None, :] * np.concatenate([inv, inv])[:, None]).astype(np.float32)
    return np.ascontiguousarray(np.stack([np.cos(ang), np.sin(ang)], axis=1).astype(np.float32))


def host_icnt(S):
    t = np.arange(S)
    out = np.zeros((256, S), np.float32)
    for gi, w in enumerate(POOL_WINDOWS):
        lo = np.clip(t - w // 2, 0, S)
        hi = np.clip(t + w // 2, 0, S)
        out[gi * 64:(gi + 1) * 64, :] = (1.0 / (hi - lo).astype(np.float32))[None, :]
    return out


def build(nc, SEQS, dbg=False, stages="ABCD", nlayers=NL, mixers="pdca"):
    b = Builder(nc, SEQS, dbg)
    b.declare()
    b.prologue()
    for l in range(nlayers):
        last = (l == nlayers - 1)
        if "A" in stages:
            b.ffn_sweep(l, 0, b.xT if l == 0 else b.xs, b.xs if "D" in stages or True else b.yT)
        if "B" in stages:
            b.proj_sweep(l)
        if "C" in stages:
            for sq in range(b.NS):
                if "p" in mixers:
                    b.pool_mixer(l, sq)
                if "d" in mixers:
                    b.mla_mixer(l, sq)
                if "c" in mixers:
                    b.na_mixer(l, sq)
                if "a" in mixers:
                    b.gla_mixer(l, sq)
        if "D" in stages:
            b.ffn_sweep(l, 1, b.xs, b.yT if last else b.xs, pre_wout=("C" in stages), final_norm=last, final=last)
    b.S.emit()
    return b


def host_shared(w, seqs):
    m = {}
    m["ada_w"] = np.ascontiguousarray(w["ada_w"], np.float32)
    m["adabT"] = np.ascontiguousarray(w["ada_b"].reshape(NL, 72, 128).transpose(0, 2, 1), np.float32)
    nrm = np.zeros((128, (NL * 3 + 1) * 8), np.float32)
    for l in range(NL):
        for k, name in enumerate(("norm_ffn1", "norm_mix", "norm_ffn2")):
            nrm[:, (l * 3 + k) * 8:(l * 3 + k + 1) * 8] = w[name][l].reshape(8, 128).T
    nrm[:, NL * 3 * 8:] = w["final_norm"].reshape(8, 128).T
    m["nrm"] = nrm
    for k in ("ffn1_wg", "ffn1_wu", "ffn1_wd", "ffn2_wg", "ffn2_wu", "ffn2_wd", "w_in", "w_out"):
        m[k] = np.ascontiguousarray(w[k], np.float32)
    m["cst"] = host_consts()
    vec = np.zeros((128, NL * VEC_W), np.float32)
    wgk = np.zeros((NL, 33, 256), np.float32)
    poolw = np.zeros((NL, 2, 128, 128), np.float32)
    for l in range(NL):
        vec[:, l * VEC_W + 0:l * VEC_W + 2] = w["mla_qnorm"][l].reshape(2, 128).T
        vec[:, l * VEC_W + 2] = w["mla_kvnorm"][l]
        vec[:, l * VEC_W + 3:l * VEC_W + 5] = w["pool_scale"][l].reshape(2, 128).T
        vec[0:64, l * VEC_W + 5] = w["gla_norm"][l]
        vec[64:128, l * VEC_W + 5] = w["gla_norm"][l]
        wgk[l, 0:16, 0:128] = w["gla_wgk_f"][l]
        wgk[l, 16:32, 128:256] = w["gla_wgk_b"][l]
        wgk[l, 32, 0:128] = w["gla_bgk_f"][l]
        wgk[l, 32, 128:256] = w["gla_bgk_b"][l]
        for c in range(2):
            poolw[l, c, 0:64, 0:64] = w["pool_w"][l, 2 * c]
            poolw[l, c, 64:128, 64:128] = w["pool_w"][l, 2 * c + 1]
    m["vec"] = vec
    m["wgk"] = wgk
    m["poolw"] = poolw
    m["mla_wuq"] = np.ascontiguousarray(w["mla_wuq"], np.float32)
    m["mla_wukv"] = np.ascontiguousarray(w["mla_wukv"], np.float32)
    m["natab"] = host_natab(w["na_rpb"], seqs)
    m["rope"] = host_rope(max(seqs))
    for Sq in sorted(set(seqs)):
        m["icnt%d" % Sq] = host_icnt(Sq)
    return m


def host_natab(rpb, seqs):
    Rr = 16
    P = Rr // 2
    variants = [(3, [1, 2, 3, 4, 5]), (0, [0, 1, 2, 3]), (1, [0, 1, 2, 3]), (P - 2, [P - 4, P - 3, P - 2, P - 1]),
                (P - 1, [P - 4, P - 3, P - 2, P - 1])]
    out = np.full((NL, 4, 21, 128, 128), NEG, np.float32)
    cq = np.arange(64)
    ck = np.arange(64)
    qstart = np.clip(cq - 8, 0, 48)
    col_ok = (ck[None, :] >= qstart[:, None]) & (ck[None, :] < qstart[:, None] + 16)
    dcol = ck[None, :] - cq[:, None] + 15
    t = 0
    for (i, js) in variants:
        for j in js:
            for rq in range(2):
                qrow = 2 * i + rq
                r0 = min(max(qrow - 4, 0), Rr - 8)
                for rk in range(2):
                    krow = 2 * j + rk
                    if not (r0 <= krow < r0 + 8):
                        continue
                    drow = krow - qrow + 7
                    blk = np.where(col_ok[None, None], rpb[:, :, drow, :][:, :, np.clip(dcol, 0, 30)], NEG)
                    out[:, :, t, rk * 64:(rk + 1) * 64, rq * 64:(rq + 1) * 64] = np.swapaxes(blk, -1, -2)
            t += 1
    assert t == 21
    return out


def host_core(xs, cs):
    xT = np.ascontiguousarray(np.concatenate(xs, axis=0).T, np.float32)
    NS = len(xs)
    cT = np.ascontiguousarray(np.asarray(cs, np.float32).T.reshape(8, 128, NS).transpose(1, 0, 2))
    return {"xT": xT, "cT": cT}


N_CORES = 8
SEQS_FULL = [2048, 2048, 2048, 2048, 8192]
_WNAMES = ("ada_w", "ada_b", "norm_ffn1", "ffn1_wg", "ffn1_wu", "ffn1_wd", "norm_mix", "w_in", "gla_wgk_f",
           "gla_bgk_f", "gla_wgk_b", "gla_bgk_b", "gla_norm", "pool_w", "pool_scale", "na_rpb", "mla_qnorm",
           "mla_wuq", "mla_kvnorm", "mla_wukv", "w_out", "norm_ffn2", "ffn2_wg", "ffn2_wu", "ffn2_wd", "final_norm")


def kernel(x_prompt, x_sample, c_prompt, c_sample, **weights):
    w = {k: np.asarray(weights[k], np.float32) for k in _WNAMES}
    x_prompt = np.asarray(x_prompt, np.float32)
    x_sample = np.asarray(x_sample, np.float32)
    c_prompt = np.asarray(c_prompt, np.float32)
    c_sample = np.asarray(c_sample, np.float32)
    nc = bass.Bass("TRN2", target_bir_lowering=False)
    b = build(nc, SEQS_FULL)
    shared = host_shared(w, SEQS_FULL)
    shared = {k: v for k, v in shared.items() if k in b.dram}
    in_maps = []
    for c in range(N_CORES):
        xs = [x_prompt[4 * c + i] for i in range(4)] + [x_sample[c]]
        cs = np.concatenate([c_prompt[4 * c:4 * c + 4], c_sample[c:c + 1]], axis=0)
        m = dict(shared)
        m.update(host_core(xs, cs))
        in_maps.append(m)
    res = run_bass_kernel_spmd(nc, in_maps, core_ids=list(range(N_CORES)))
    y_prompt = np.empty((32, 2048, D), np.float32)
    y_sample = np.empty((8, 8192, D), np.float32)
    for c in range(N_CORES):
        yT = np.asarray(res.results[c]["yT"], np.float32)
        for i in range(4):
            y_prompt[4 * c + i] = yT[:, i * 2048:(i + 1) * 2048].T
        y_sample[c] = yT[:, 8192:].T
    return (y_prompt, y_sample)
```

```python
import numpy as np
from contextlib import ExitStack
import concourse.bass as bass
import concourse.mybir as mybir
from concourse.bass_utils import run_bass_kernel_spmd

F32 = mybir.dt.float32
BF16 = mybir.dt.bfloat16
AF = mybir.ActivationFunctionType
ALU = mybir.AluOpType

D = 1024
FF = 2816
NL = 2
PW = 2240
EPS = 1e-6
MLA_SCALE = 96.0 ** -0.5
NEG = -30000.0
ENGS = ("pe", "act", "dve", "pool", "sp")
DMAQ = ("sp", "pool", "act")
ATTACH_WAITS = True


class Op:
    __slots__ = ("eng", "fn", "deps", "is_dma", "sig", "sem", "val", "idx", "depset")

    def __init__(self, eng, fn, is_dma):
        self.eng = eng
        self.fn = fn
        self.is_dma = is_dma
        self.deps = []
        self.sig = False
        self.sem = None
        self.val = 0
        self.idx = 0
        self.depset = frozenset()


class Sched:
    NPOOL = 24

    def __init__(self, nc):
        self.nc = nc
        self.ops = []
        self.lastw = {}
        self.readers = {}
        self.finals = []
        self.barrier = None
        self.bar_seen = set()
        self.last_eng = {}

    def op(self, eng, fn, R=(), W=(), dma=False, final=False):
        o = Op(eng, fn, dma)
        o.idx = len(self.ops)
        deps = {}
        for k in R:
            w = self.lastw.get(k)
            if w is not None:
                deps[w.idx] = w
        for k in W:
            w = self.lastw.get(k)
            if w is not None:
                deps[w.idx] = w
            for r in self.readers.get(k, ()):
                deps[r.idx] = r
        if self.barrier is not None and eng not in self.bar_seen:
            self.bar_seen.add(eng)
            deps[self.barrier.idx] = self.barrier
        implied = set()
        for d2 in deps.values():
            implied.update(d2.depset)
        for d in deps.values():
            if d.eng == "pe" and eng == "pe" and not d.is_dma and not dma:
                continue
            if d.idx in implied:
                continue
            o.deps.append(d)
            d.sig = True
        o.depset = frozenset(x.idx for x in o.deps)
        for k in W:
            self.lastw[k] = o
            self.readers[k] = []
        for k in R:
            lst = self.readers.setdefault(k, [])
            if not dma:
                lst[:] = [r for r in lst if r.is_dma or r.eng != eng]
            lst.append(o)
        self.ops.append(o)
        self.last_eng[eng] = o
        if final:
            o.sig = True
            self.finals.append(o)
        return o

    def dma(self, eng, out, in_, R=(), W=(), final=False):
        return self.op(eng, lambda e: e.dma_start(out=out, in_=in_), R=R, W=W, dma=True, final=final)

    def phase_barrier(self, scratch_dst, scratch_src):
        o = Op("sp", lambda e: e.dma_start(out=scratch_dst, in_=scratch_src), True)
        o.idx = len(self.ops)
        seen = {}
        for p in self.ops:
            if p.is_dma:
                seen[p.idx] = p
        for e, p in self.last_eng.items():
            seen[p.idx] = p
        start = self.barrier.idx if self.barrier is not None else -1
        for p in seen.values():
            if p.idx > start:
                o.deps.append(p)
                p.sig = True
        o.sig = True
        self.ops.append(o)
        self.last_eng["sp"] = o
        self.barrier = o
        self.bar_seen = {"sp"}
        self.lastw = {}
        self.readers = {}

    def emit(self):
        nc = self.nc
        with ExitStack() as st:
            esem = {e: st.enter_context(nc.semaphore("s_" + e)) for e in ENGS}
            dsem = {e: [st.enter_context(nc.semaphore("d_%s_%d" % (e, i))) for i in range(self.NPOOL)]
                    for e in DMAQ}
            ecount = {e: 0 for e in ENGS}
            dcount = {e: [0] * self.NPOOL for e in dsem}
            dnext = {e: 0 for e in dsem}
            dprev = {e: [None] * self.NPOOL for e in dsem}
            per_eng = {e: [] for e in ENGS}
            for o in self.ops:
                per_eng[o.eng].append(o)
                if o.is_dma:
                    i = dnext[o.eng]
                    dnext[o.eng] = (i + 1) % self.NPOOL
                    prev = dprev[o.eng][i]
                    if prev is not None:
                        o.deps.append(prev)
                    dprev[o.eng][i] = o
                    dcount[o.eng][i] += 16
                    o.sem = dsem[o.eng][i]
                    o.val = dcount[o.eng][i]
                    o.sig = True
                elif o.sig:
                    ecount[o.eng] += 1
                    o.sem = esem[o.eng]
                    o.val = ecount[o.eng]
            self.nwaits = 0
            blk = st.enter_context(nc.Block())

            def run(engname, e):
                waited = {}
                for o in per_eng[engname]:
                    need = {}
                    for d in o.deps:
                        key = id(d.sem)
                        if waited.get(key, 0) >= d.val:
                            continue
                        if key not in need or need[key][1] < d.val:
                            need[key] = (d.sem, d.val)
                    items = list(need.items())
                    attach = None
                    if items and ATTACH_WAITS and not o.is_dma:
                        attach = items.pop()
                    for key, (sem, val) in items:
                        e.wait_ge(sem, val)
                        waited[key] = val
                        self.nwaits += 1
                    ins = o.fn(e)
                    if attach is not None:
                        ins._wait_ge(attach[1][0], attach[1][1])
                        waited[attach[0]] = attach[1][1]
                    if o.sig:
                        ins.then_inc(o.sem, 16 if o.is_dma else 1)
                if engname == "sp":
                    for o in self.finals:
                        key = id(o.sem)
                        if waited.get(key, 0) < o.val:
                            e.wait_ge(o.sem, o.val)
                            waited[key] = o.val

            @blk.tensor
            def _(e):
                run("pe", e)

            @blk.scalar
            def _(e):
                run("act", e)

            @blk.vector
            def _(e):
                run("dve", e)

            @blk.gpsimd
            def _(e):
                run("pool", e)

            @blk.sync
            def _(e):
                run("sp", e)


class Rot:
    def __init__(self, name, aps, keys=None):
        self.name = name
        self.aps = aps
        self.keys = keys if keys is not None else [(name, i) for i in range(len(aps))]
        self.i = -1

    def next(self):
        self.i = (self.i + 1) % len(self.aps)
        return self.aps[self.i], self.keys[self.i]

    def cur(self):
        return self.aps[self.i], self.keys[self.i]


SB_LO = 16512
SB_HI = 229312


class Builder:
    def __init__(self, nc, SEQS, dbg=False):
        self.nc = nc
        self.S = Sched(nc)
        self.SEQS = list(SEQS)
        self.NT = sum(SEQS)
        self.NS = len(SEQS)
        self.off = [int(v) for v in np.cumsum([0] + list(SEQS))]
        self.dbg = dbg
        self.sbp = SB_LO
        self.uid = 0
        self.dram = {}
        self.outs = []

    def sb(self, shape, dt, name=None):
        nbytes = int(np.prod(shape[1:])) * (4 if dt == F32 else 2)
        nbytes = (nbytes + 63) // 64 * 64
        self.uid += 1
        t = self.nc.alloc_sbuf_tensor_at("%s_%d" % (name or "t", self.uid), list(shape), dt, offset=self.sbp)
        self.sbp += nbytes
        assert self.sbp <= SB_HI, ("SBUF overflow", name, self.sbp)
        return t

    def rot(self, name, n, shape, dt):
        self.uid += 1
        return Rot("%s%d" % (name, self.uid), [self.sb(shape, dt, name) for _ in range(n)])

    def pbrot(self, idxs):
        return Rot("pb", [self.pb[i] for i in idxs], keys=["pb%d" % i for i in idxs])

    def din(self, name, shape, dt=F32):
        t = self.nc.dram_tensor(name, list(shape), dt, kind="ExternalInput").ap()
        self.dram[name] = t
        return t

    def dscr(self, name, shape, dt):
        kind = "ExternalOutput" if self.dbg else "Internal"
        t = self.nc.dram_tensor(name, list(shape), dt, kind=kind).ap()
        self.dram[name] = t
        return t

    def seq_of(self, tok):
        for s in range(self.NS):
            if self.off[s] <= tok < self.off[s + 1]:
                return s
        raise ValueError

    def mm(self, out, lhsT, rhs, start, stop, R, W, skip=False):
        if skip:
            self.S.op("pe", lambda e: e.matmul(out, lhsT, rhs, start=start, stop=stop, skip_group_check=True),
                      R=R, W=W)
        else:
            self.S.op("pe", lambda e: e.matmul(out, lhsT, rhs, start=start, stop=stop), R=R, W=W)

    def act(self, out, in_, func, R, W, bias=None, scale=None):
        kw = {}
        if bias is not None:
            kw["bias"] = bias
        if scale is not None:
            kw["scale"] = scale
        self.S.op("act", lambda e: e.activation(out=out, in_=in_, func=func, **kw), R=R, W=W)

    def tt(self, out, in0, in1, op, R, W, eng="dve"):
        self.S.op(eng, lambda e: e.tensor_tensor(out=out, in0=in0, in1=in1, op=op), R=R, W=W)

    def ts(self, out, in0, s1, op0, R, W, s2=None, op1=None, eng="dve"):
        if op1 is None:
            self.S.op(eng, lambda e: e.tensor_scalar(out=out, in0=in0, scalar1=s1, scalar2=None, op0=op0), R=R, W=W)
        else:
            self.S.op(eng, lambda e: e.tensor_scalar(out=out, in0=in0, scalar1=s1, scalar2=s2, op0=op0, op1=op1),
                      R=R, W=W)

    def stt(self, out, in0, scalar, in1, op0, op1, R, W, eng="dve"):
        self.S.op(eng, lambda e: e.scalar_tensor_tensor(out=out, in0=in0, scalar=scalar, in1=in1, op0=op0, op1=op1),
                  R=R, W=W)

    def cp(self, out, in_, R, W, eng="dve"):
        self.S.op(eng, lambda e: e.tensor_copy(out=out, in_=in_), R=R, W=W)

    def recip(self, out, in_, R, W):
        self.S.op("dve", lambda e: e.reciprocal(out=out, in_=in_), R=R, W=W)

    def memset(self, ap, val, W, eng="dve"):
        self.S.op(eng, lambda e: e.memset(ap, val), W=W)

    def dma(self, out, in_, R=(), W=(), q="sp", final=False):
        self.S.dma(q, out, in_, R=R, W=W, final=final)

    def declare(self):
        NT, NS = self.NT, self.NS
        d = self.din
        self.xT = d("xT", [D, NT])
        self.cT = d("cT", [128, 8, NS])
        self.ada_w = d("ada_w", [NL, D, 9 * D])
        self.adabT = d("adabT", [NL, 128, 72])
        self.nrm = d("nrm", [128, (NL * 3 + 1) * 8])
        self.wg = [d("ffn1_wg", [NL, D, FF]), d("ffn2_wg", [NL, D, FF])]
        self.wu = [d("ffn1_wu", [NL, D, FF]), d("ffn2_wu", [NL, D, FF])]
        self.wd = [d("ffn1_wd", [NL, FF, D]), d("ffn2_wd", [NL, FF, D])]
        self.w_in = d("w_in", [NL, D, PW])
        self.w_out = d("w_out", [NL, D, D])
        self.cst = d("cst", [128, CST_W])
        self.vec = d("vec", [128, NL * VEC_W])
        self.wgk = d("wgk", [NL, 33, 256])
        self.w_uq = d("mla_wuq", [NL, 256, 384])
        self.w_ukv = d("mla_wukv", [NL, 128, 512])
        self.rope = d("rope", [32, 2, max(self.SEQS)])
        self.poolw = d("poolw", [NL, 2, 128, 128])
        self.natab = d("natab", [NL, 4, 21, 128, 128])
        self.icnt = {}
        for Sq in sorted(set(self.SEQS)):
            self.icnt[Sq] = d("icnt%d" % Sq, [256, Sq])
        sc = self.dscr
        self.qTa = sc("qTa", [128, NT], BF16)
        self.kTa = sc("kTa", [128, NT], BF16)
        self.ktok = sc("ktok", [NT, 128], BF16)
        self.vtok = sc("vtok", [NT, 256], BF16)
        self.sptok = sc("sptok", [NT, 256], F32)
        self.goT = sc("goT", [4, 64, NT], BF16)
        self.uT = sc("uT", [256, NT], F32)
        self.qTc = sc("qTc", [4, 64, NT], BF16)
        self.kTc = sc("kTc", [4, 64, NT], BF16)
        self.vtokc = sc("vtokc", [NT, 260], BF16)
        self.qTd = sc("qTd", [4, 96, NT], BF16)
        self.kTd = sc("kTd", [4, 96, NT], BF16)
        self.vtokd = sc("vtokd", [NT, 260], BF16)
        self.yT = self.nc.dram_tensor("yT", [D, NT], F32, kind="ExternalOutput").ap()
        self.xs = self.dscr("xs", [D, NT], F32)
        self.ymix = self.dscr("ymix", [16, 64, NT], BF16)
        self.bar = self.nc.dram_tensor("bar", [2, 64], F32, kind="Internal").ap()
        self.pb = [self.nc.alloc_psum_tensor("pb%d" % i, [128, 512], F32) for i in range(8)]

    def prologue(self):
        S = self.S
        NS = self.NS
        self.cst_sb = self.sb([128, CST_W], F32, "cst")
        self.dma(self.cst_sb[:], self.cst, W=["cst"])
        self.nrm_sb = self.sb([128, (NL * 3 + 1) * 8], F32, "nrm")
        self.dma(self.nrm_sb[:], self.nrm, W=["nrm"])
        self.vec_sb = self.sb([128, NL * VEC_W], F32, "vec")
        self.dma(self.vec_sb[:], self.vec, W=["vec"])
        self.ones_f = self.sb([128, 64], F32, "onesf")
        self.memset(self.ones_f[:], 1.0, W=["onesf"])
        self.ident_bf = self.sb([128, 128], BF16, "ident")
        self.cp(self.ident_bf[:], self.cst_sb[:, C_ID:C_ID + 128], R=["cst"], W=["ident"])
        self.ones_bf = self.sb([128, 128], BF16, "ones")
        self.memset(self.ones_bf[:], 1.0, W=["ones"])
        self.eps_c = self.cst_sb[:, C_EPS:C_EPS + 1]
        self.one_c = self.cst_sb[:, C_ONE:C_ONE + 1]
        self.mod = [self.sb([128, 72, NS], F32, "mod%d" % l) for l in range(NL)]
        self.Gn = [self.sb([128, 3, 8, NS], F32, "Gn%d" % l) for l in range(NL)]
        self.gate = [self.sb([128, 3, 8, NS], F32, "gate%d" % l) for l in range(NL)]
        persist_end = self.sbp
        csil = self.sb([128, 8, NS], F32, "csil")
        adab = self.sb([128, NL, 72], F32, "adab")
        self.dma(csil[:], self.cT, W=["csil"])
        self.dma(adab[:], self.adabT.rearrange("l p j -> p l j"), W=["adab"])
        self.act(csil[:], csil[:], AF.Silu, R=["csil"], W=["csil"])
        wrot = self.rot("adaw", 2, [128, 8, 512], F32)
        ps = self.pb[0]
        for l in range(NL):
            psv = ps[:, 0:72 * NS].rearrange("p (j s) -> p j s", s=NS)
            for pc in range(18):
                wt, wk = wrot.next()
                src = self.ada_w[l].rearrange("(kc p) c -> p kc c", p=128)[:, :, pc * 512:(pc + 1) * 512]
                self.dma(wt[:], src, W=[wk], q=("sp" if pc % 2 == 0 else "act"))
                for jj in range(4):
                    j = pc * 4 + jj
                    for kc in range(8):
                        self.mm(psv[:, j, :], wt[:, kc, jj * 128:(jj + 1) * 128], csil[:, kc, :],
                                kc == 0, kc == 7, R=[wk, "csil"], W=["pb0"])
            self.tt(self.mod[l][:], psv, adab[:, l, :].unsqueeze(2).to_broadcast([128, 72, NS]), ALU.add,
                    R=["pb0", "adab"], W=["mod%d" % l])
            m = self.mod[l][:].rearrange("p (i oc) s -> p i oc s", i=9)
            for k, (isc, igate, gmul) in enumerate(((1, 2, 0.5), (4, 5, 1.0), (7, 8, 0.5))):
                nv = self.nrm_sb[:, (l * 3 + k) * 8:(l * 3 + k + 1) * 8]
                self.ts(self.Gn[l][:, k], m[:, isc], 1.0, ALU.add, R=["mod%d" % l], W=["Gn%d" % l])
                self.tt(self.Gn[l][:, k], self.Gn[l][:, k], nv.unsqueeze(2).to_broadcast([128, 8, NS]), ALU.mult,
                        R=["Gn%d" % l, "nrm"], W=["Gn%d" % l])
                self.ts(self.gate[l][:, k], m[:, igate], 1.0, ALU.add, R=["mod%d" % l], W=["gate%d" % l],
                        s2=gmul, op1=ALU.mult)
        self.sbp = persist_end
        self.persist_end = persist_end
        S.phase_barrier(self.bar[0:1, 0:8], self.cst[0:1, 0:8])

    def shift(self, l, k, kc, s):
        i = (0, 3, 6)[k]
        return self.mod[l][:, i * 8 + kc, s:s + 1]

    def norm_tile(self, x, xk, T, l, k, s, sq, sqk, t, tk, h, hk, ssb, rs, rsk):
        if isinstance(sq, Rot):
            for kc in range(8):
                sq1, sq1k = sq.next()
                self.act(sq1[:], x[:, kc, :], AF.Square, R=[xk], W=[sq1k])
                self.mm(ssb[:, 0:T], self.ones_bf[:], sq1[:], kc == 0, kc == 7, R=[sq1k, "ones"], W=["pb6"])
        else:
            self.act(sq[:], x[:], AF.Square, R=[xk], W=[sqk])
            for kc in range(8):
                self.mm(ssb[:, 0:T], self.ones_bf[:], sq[:, kc, :], kc == 0, kc == 7, R=[sqk, "ones"], W=["pb6"])
        self.act(rs[:], ssb[:, 0:T], AF.Sqrt, R=["pb6", "cst"], W=[rsk], bias=self.eps_c, scale=1.0 / D)
        self.recip(rs[:], rs[:], R=[rsk], W=[rsk])
        for kc in range(8):
            tt_, ttk = t.next()
            self.tt(tt_[:], x[:, kc, :], rs[:], ALU.mult, R=[xk, rsk], W=[ttk])
            self.act(h[:, kc, :], tt_[:], AF.Identity, R=[ttk, "Gn%d" % l, "mod%d" % l], W=[hk],
                     bias=self.shift(l, k, kc, s), scale=self.Gn[l][:, k, kc, s:s + 1])

    def ffn_sweep(self, l, which, src, dst, pre_wout=False, final_norm=False, final=False):
        S = self.S
        T = 256
        NT = self.NT
        k = 0 if which == 0 else 2
        self.sbp = self.persist_end
        wg = self.sb([128, 8, FF], BF16, "wg")
        wu = self.sb([128, 8, FF], BF16, "wu")
        wd = self.sb([128, 22, D], BF16, "wd")
        self.dma(wg[:], self.wg[which][l].rearrange("(kc p) f -> p kc f", p=128), W=["wg"], q="pool")
        self.dma(wu[:], self.wu[which][l].rearrange("(kc p) f -> p kc f", p=128), W=["wu"], q="pool")
        self.dma(wd[:], self.wd[which][l].rearrange("(fc p) d -> p fc d", p=128), W=["wd"], q="pool")
        if pre_wout:
            wo = self.sb([128, 8, D], BF16, "wo")
            self.dma(wo[:], self.w_out[l].rearrange("(kc p) d -> p kc d", p=128), W=["wo"], q="pool")
            yrot = self.rot("yt", 2, [128, 8, T], BF16)
        xrot = self.rot("x", 3, [128, 8, T], F32)
        sq = self.rot("sq", 3, [128, T], BF16)
        t = self.rot("t", 2, [128, T], F32)
        hrot = self.rot("h", 2, [128, 8, T], BF16)
        rsrot = self.rot("rs", 2, [128, T], F32)
        sgrot = self.rot("sg", 2, [128, T], BF16)
        arot = self.rot("a", 3, [128, T], BF16)
        acc = [self.pb[i] for i in range(4)]
        gurot = self.pbrot([4, 5])
        ssb = self.pb[6]

        def accv(oc):
            return acc[oc // 2][:, (oc % 2) * 256:(oc % 2) * 256 + T], "pb%d" % (oc // 2)

        ntiles = NT // T
        xsrc = src.rearrange("(kc p) t -> p kc t", p=128)
        xdst = dst.rearrange("(kc p) t -> p kc t", p=128)
        ysrc = self.ymix.rearrange("(kc two) f t -> (two f) kc t", two=2)
        state = {}

        def stage_load(i):
            x, xk = xrot.next()
            self.dma(x[:], xsrc[:, :, i * T:(i + 1) * T], W=[xk], q="sp")
            state[i] = dict(x=x, xk=xk)
            if pre_wout:
                y, yk = yrot.next()
                self.dma(y[:], ysrc[:, :, i * T:(i + 1) * T], W=[yk], q="sp")
                state[i].update(y=y, yk=yk)

        def norm_steps(i):
            st = state[i]
            s = self.seq_of(i * T)
            x, xk = st["x"], st["xk"]
            steps = []
            if pre_wout:
                y, yk = st["y"], st["yk"]
                slots = [(self.pb[6][:, 256:256 + T], "pb6"), (self.pb[7][:, 0:T], "pb7")]

                def pre(oc):
                    av, ak = slots[oc % 2]
                    for kc in range(8):
                        self.mm(av, wo[:, kc, oc * 128:(oc + 1) * 128], y[:, kc, :], kc == 0, kc == 7,
                                R=[yk, "wo"], W=[ak])
                    self.stt(x[:, oc, :], av, self.gate[l][:, 1, oc, s:s + 1], x[:, oc, :], ALU.mult, ALU.add,
                             R=[ak, xk, "gate%d" % l], W=[xk])
                for o2 in range(4):
                    steps.append(lambda o2=o2: (pre(2 * o2), pre(2 * o2 + 1)))
            h, hk = hrot.next()
            rs, rsk = rsrot.next()
            st.update(h=h, hk=hk)

            def n2():
                for kc in range(8):
                    sq1, sq1k = sq.next()
                    self.tt(sq1[:], x[:, kc, :], x[:, kc, :], ALU.mult, R=[xk], W=[sq1k], eng="pool")
                    self.mm(ssb[:, 0:T], self.ones_bf[:], sq1[:], kc == 0, kc == 7, R=[sq1k, "ones"], W=["pb6"])

            def n3():
                self.act(rs[:], ssb[:, 0:T], AF.Sqrt, R=["pb6", "cst"], W=[rsk], bias=self.eps_c, scale=1.0 / D)
                self.recip(rs[:], rs[:], R=[rsk], W=[rsk])

            def n4(kc):
                tt_, ttk = t.next()
                self.tt(tt_[:], x[:, kc, :], rs[:], ALU.mult, R=[xk, rsk], W=[ttk])
                self.ts(h[:, kc, :], tt_[:], self.Gn[l][:, k, kc, s:s + 1], ALU.mult,
                        R=[ttk, "Gn%d" % l, "mod%d" % l], W=[hk], s2=self.shift(l, k, kc, s), op1=ALU.add)
            steps.append(n2)
            steps.append(None)
            steps.append(n3)
            steps.append(None)
            for kc in range(8):
                steps.append(lambda kc=kc: n4(kc))
            return steps

        def stage_norm(i):
            for f in norm_steps(i):
                if f is not None:
                    f()

        def stage_body(i):
            st = state[i]
            s = self.seq_of(i * T)
            x, xk, h, hk = st["x"], st["xk"], st["h"], st["hk"]

            def gu(fc):
                gub, gk = gurot.next()
                uk = gk
                g = gub[:, 0:T]
                u = gub[:, 256:256 + T]
                for kc in range(8):
                    self.mm(g, wg[:, kc, fc * 128:(fc + 1) * 128], h[:, kc, :], kc == 0, kc == 7, R=[hk, "wg"], W=[gk])
                for kc in range(8):
                    self.mm(u, wu[:, kc, fc * 128:(fc + 1) * 128], h[:, kc, :], kc == 0, kc == 7, R=[hk, "wu"], W=[uk])
                return g, gk, u, uk

            nsteps = norm_steps(i + 1) if i + 1 < ntiles else []
            pend = gu(0)
            for fc in range(22):
                g, gk, u, uk = pend
                if fc + 1 < 22:
                    pend = gu(fc + 1)
                if fc >= 1 and nsteps:
                    f = nsteps.pop(0)
                    if f is not None:
                        f()
                sg, sgk = sgrot.next()
                a, ak = arot.next()
                self.act(sg[:], g, AF.Silu, R=[gk], W=[sgk])
                self.tt(a[:], u, sg[:], ALU.mult, R=[uk, sgk], W=[ak])
                for oc in range(8):
                    av, avk = accv(oc)
                    self.mm(av, wd[:, fc, oc * 128:(oc + 1) * 128], a[:], fc == 0 and oc % 2 == 0, fc == 21,
                            R=[ak, "wd"], W=[avk], skip=True)
            while nsteps:
                f = nsteps.pop(0)
                if f is not None:
                    f()
            for oc in range(8):
                av, avk = accv(oc)
                self.stt(x[:, oc, :], av, self.gate[l][:, k, oc, s:s + 1], x[:, oc, :], ALU.mult, ALU.add,
                         R=[avk, xk, "gate%d" % l], W=[xk])
            if final_norm:
                rs, rsk = rsrot.next()
                for kc in range(8):
                    sq1, sq1k = sq.next()
                    self.act(sq1[:], x[:, kc, :], AF.Square, R=[xk], W=[sq1k])
                    self.mm(ssb[:, 0:T], self.ones_bf[:], sq1[:], kc == 0, kc == 7, R=[sq1k, "ones"], W=["pb6"])
                self.act(rs[:], ssb[:, 0:T], AF.Sqrt, R=["pb6", "cst"], W=[rsk], bias=self.eps_c, scale=1.0 / D)
                self.recip(rs[:], rs[:], R=[rsk], W=[rsk])
                fn = self.nrm_sb[:, NL * 3 * 8:NL * 3 * 8 + 8]
                for kc in range(8):
                    self.stt(x[:, kc, :], x[:, kc, :], fn[:, kc:kc + 1], rs[:], ALU.mult, ALU.mult,
                             R=[xk, rsk, "nrm"], W=[xk])
            self.dma(xdst[:, :, i * T:(i + 1) * T], x[:], R=[xk], q="act", final=final)
            del state[i]

        stage_load(0)
        if ntiles > 1:
            stage_load(1)
        stage_norm(0)
        for i in range(ntiles):
            if i + 2 < ntiles:
                stage_load(i + 2)
            stage_body(i)
        S.phase_barrier(self.bar[0:1, 0:8], self.cst[0:1, 0:8])


    def proj_sweep(self, l):
        S = self.S
        T = 512
        NT = self.NT
        self.sbp = self.persist_end
        cst = self.cst_sb
        vec = self.vec_sb[:, l * VEC_W:(l + 1) * VEC_W]
        win = self.sb([128, 8, PW], BF16, "win")
        self.dma(win[:], self.w_in[l].rearrange("(kc p) c -> p kc c", p=128), W=["win"], q="pool")
        wkpe = self.sb([128, 8, 96], BF16, "wkpe")
        wkrh = self.sb([128, 8, 96], BF16, "wkrh")
        self.memset(wkpe[:], 0.0, W=["wkpe"])
        self.memset(wkrh[:], 0.0, W=["wkrh"])
        wsrc = self.w_in[l].rearrange("(kc p) c -> p kc c", p=128)
        self.dma(wkpe[:, :, 64:96], wsrc[:, :, 2208:2240], W=["wkpe"], q="pool")
        self.dma(wkrh[:, :, 64:80], wsrc[:, :, 2224:2240], W=["wkrh"], q="pool")
        self.dma(wkrh[:, :, 80:96], wsrc[:, :, 2208:2224], W=["wkrh"], q="pool")
        self.ts(wkrh[:, :, 64:80], wkrh[:, :, 64:80], -1.0, ALU.mult, R=["wkrh"], W=["wkrh"])
        wuq_f = self.sb([128, 2, 384], F32, "wuqf")
        self.dma(wuq_f[:], self.w_uq[l].rearrange("(kc p) c -> p kc c", p=128), W=["wuqf"])
        wuq = self.sb([128, 2, 384], BF16, "wuq")
        wuqr = self.sb([128, 2, 384], BF16, "wuqr")
        for kc in range(2):
            self.ts(wuq[:, kc, :], wuq_f[:, kc, :], vec[:, kc:kc + 1], ALU.mult, R=["wuqf", "vec"], W=["wuq"])
            self.cp(wuqr[:, kc, :], wuq[:, kc, :], R=["wuq"], W=["wuqr"])
            v4 = wuq[:, kc, :].rearrange("p (h c) -> p h c", c=96)
            r4 = wuqr[:, kc, :].rearrange("p (h c) -> p h c", c=96)
            self.ts(r4[:, :, 64:80], v4[:, :, 80:96], -1.0, ALU.mult, R=["wuq", "wuqr"], W=["wuqr"])
            self.cp(r4[:, :, 80:96], v4[:, :, 64:80], R=["wuq", "wuqr"], W=["wuqr"])
        wukv_f = self.sb([128, 512], F32, "wukvf")
        self.dma(wukv_f[:], self.w_ukv[l], W=["wukvf"])
        wukv = self.sb([128, 512], BF16, "wukv")
        self.ts(wukv[:], wukv_f[:], vec[:, 2:3], ALU.mult, R=["wukvf", "vec"], W=["wukv"])
        wukvv = self.sb([128, 4, 64], BF16, "wukvv")
        self.cp(wukvv[:], wukv[:].rearrange("p (h c) -> p h c", c=128)[:, :, 64:128], R=["wukv"], W=["wukvv"])
        wgk = self.sb([33, 256], BF16, "wgk")
        self.dma(wgk[:], self.wgk[l], W=["wgk"], q="pool")
        xrot = self.rot("x", 2, [128, 8, T], F32)
        csrot = self.rot("cs", 2, [96, 2, T], F32)
        sq = self.sb([128, 8, T], BF16, "sq")
        trot = self.rot("t", 2, [128, T], F32)
        hrot = self.rot("h", 2, [128, 8, T], BF16)
        rsrot = self.rot("rs", 2, [128, T], F32)
        lrT = self.sb([33, T], BF16, "lrT")
        self.memset(lrT[:], 1.0, W=["lrT"])
        cq_sb = self.sb([128, 2, T], BF16, "cq")
        sqq = self.sb([128, 2, T], BF16, "sqq")
        ckv_sb = self.sb([128, T], BF16, "ckv")
        sqkv = self.sb([128, T], BF16, "sqkv")
        rq = self.sb([128, T], F32, "rq")
        rkv = self.sb([128, T], F32, "rkv")
        rkvc = self.sb([128, 4], F32, "rkvc")
        t1 = self.sb([96, T], F32, "t1")
        t2 = self.sb([96, T], F32, "t2")
        etmp = self.sb([128, 256], F32, "etmp")
        st = {}
        for nm, shp, dt in (("q", [128, T], BF16), ("k", [128, T], BF16), ("go", [128, 2, T], BF16),
                            ("u", [128, 2, T], F32), ("qc", [128, 2, T], BF16), ("kc", [128, 2, T], BF16),
                            ("krot", [96, T], BF16), ("qd", [96, 4, T], BF16), ("kn", [64, 4, T], BF16),
                            ("kv", [128, 4, 384], BF16), ("vc", [128, 4, 4, 65], BF16), ("sp", [128, 4, 256], F32),
                            ("vd", [128, 4, 4, 65], BF16)):
            st[nm] = self.sb(shp, dt, "st_" + nm)
        self.memset(st["vc"][:], 1.0, W=["st_vc"])
        self.memset(st["vd"][:], 1.0, W=["st_vd"])
        prot = self.pbrot(range(6))
        ssb = self.pb[6]
        xsrc = self.xs.rearrange("(kc p) t -> p kc t", p=128)
        ntiles = NT // T
        state = {}

        def stage_load(i):
            x, xk = xrot.next()
            self.dma(x[:], xsrc[:, :, i * T:(i + 1) * T], W=[xk], q="sp")
            cs, csk = csrot.next()
            s = self.seq_of(i * T)
            p0 = i * T - self.off[s]
            self.dma(cs[64:96, :, :], self.rope[:, :, p0:p0 + T], W=[csk], q="sp")
            state[i] = dict(x=x, xk=xk, cs=cs, csk=csk)

        pending = []

        def norm_steps(i):
            sti = state[i]
            s = self.seq_of(i * T)
            x, xk = sti["x"], sti["xk"]
            h, hk = hrot.next()
            rs, rsk = rsrot.next()
            sti.update(h=h, hk=hk)
            steps = []

            def n2():
                self.act(sq[:], x[:], AF.Square, R=[xk], W=["sq"])
                for kc in range(8):
                    self.mm(ssb[:, 0:T], self.ones_bf[:], sq[:, kc, :], kc == 0, kc == 7, R=["sq", "ones"], W=["pb6"])

            def n3():
                self.act(rs[:], ssb[:, 0:T], AF.Sqrt, R=["pb6", "cst"], W=[rsk], bias=self.eps_c, scale=1.0 / D)
                self.recip(rs[:], rs[:], R=[rsk], W=[rsk])

            def n4(kc):
                tt_, ttk = trot.next()
                self.tt(tt_[:], x[:, kc, :], rs[:], ALU.mult, R=[xk, rsk], W=[ttk])
                self.ts(h[:, kc, :], tt_[:], self.Gn[l][:, 1, kc, s:s + 1], ALU.mult,
                        R=[ttk, "Gn%d" % l, "mod%d" % l], W=[hk], s2=self.shift(l, 1, kc, s), op1=ALU.add)
            steps.append(n2)
            steps.append(None)
            steps.append(n3)
            steps.append(None)
            for kc in range(8):
                steps.append(lambda kc=kc: n4(kc))
            return steps

        def stage_norm(i):
            for f in norm_steps(i):
                if f is not None:
                    f()

        def poll():
            if pending:
                f = pending.pop(0)
                if f is not None:
                    f()

        def stage_body(i):
            sti = state[i]
            h, hk, cs, csk = sti["h"], sti["hk"], sti["cs"], sti["csk"]
            t0 = i * T
            cosv = cs[64:96, 0, :]
            sinv = cs[64:96, 1, :]

            def fm(w, wk, c0, M):
                ps, pk = prot.next()
                for kc in range(8):
                    self.mm(ps[0:M, :], w[:, kc, c0:c0 + M], h[:, kc, :], kc == 0, kc == 7, R=[hk, wk], W=[pk])
                poll()
                return ps, pk

            ps, pk = fm(win, "win", 0, 128)
            self.act(st["q"][:], ps[:], AF.Copy, R=[pk], W=["st_q"])
            self.dma(self.qTa[:, t0:t0 + T], st["q"][:], R=["st_q"], q="pool")
            ps, pk = fm(win, "win", 128, 128)
            self.cp(st["k"][:], ps[:], R=[pk], W=["st_k"])
            self.dma(self.kTa[:, t0:t0 + T], st["k"][:], R=["st_k"], q="pool")
            for hp in range(2):
                ps, pk = fm(win, "win", 512 + hp * 128, 128)
                self.act(st["go"][:, hp, :], ps[:], AF.Silu, R=[pk], W=["st_go"])
            self.dma(self.goT[:, :, t0:t0 + T].rearrange("(hp two) p t -> (two p) hp t", two=2), st["go"][:],
                     R=["st_go"], q="pool")
            ps, pk = fm(win, "win", 768, 32)
            self.cp(lrT[0:32, :], ps[0:32, :], R=[pk], W=["lrT"])
            for c in range(2):
                ps, pk = fm(win, "win", 800 + c * 128, 128)
                self.act(st["u"][:, c, :], ps[:], AF.Copy, R=[pk], W=["st_u"])
            self.dma(self.uT.rearrange("(c p) t -> p c t", p=128)[:, :, t0:t0 + T], st["u"][:], R=["st_u"], q="pool")
            for hp in range(2):
                ps, pk = fm(win, "win", 1056 + hp * 128, 128)
                self.ts(st["qc"][:, hp, :], ps[:], 0.125, ALU.mult, R=[pk], W=["st_qc"])
            self.dma(self.qTc[:, :, t0:t0 + T].rearrange("(hp two) p t -> (two p) hp t", two=2), st["qc"][:],
                     R=["st_qc"], q="pool")
            for hp in range(2):
                ps, pk = fm(win, "win", 1312 + hp * 128, 128)
                self.act(st["kc"][:, hp, :], ps[:], AF.Copy, R=[pk], W=["st_kc"])
            self.dma(self.kTc[:, :, t0:t0 + T].rearrange("(hp two) p t -> (two p) hp t", two=2), st["kc"][:],
                     R=["st_kc"], q="pool")
            for c in range(2):
                ps, pk = fm(win, "win", 1824 + c * 128, 128)
                self.act(cq_sb[:, c, :], ps[:], AF.Copy, R=[pk], W=["cq"])
                self.act(sqq[:, c, :], ps[:], AF.Square, R=[pk], W=["sqq"])
            for c in range(2):
                self.mm(ssb[:, 0:T], self.ones_bf[:], sqq[:, c, :], c == 0, c == 1, R=["sqq", "ones"], W=["pb6"])
            self.act(rq[:], ssb[:, 0:T], AF.Ln, R=["pb6", "cst"], W=["rq"], bias=self.eps_c, scale=1.0 / 256)
            self.act(rq[:], rq[:], AF.Exp, R=["rq"], W=["rq"], scale=-0.5)
            ps, pk = fm(win, "win", 2080, 128)
            self.act(ckv_sb[:], ps[:], AF.Copy, R=[pk], W=["ckv"])
            self.act(sqkv[:], ps[:], AF.Square, R=[pk], W=["sqkv"])
            self.mm(ssb[:, 0:T], self.ones_bf[:], sqkv[:], True, True, R=["sqkv", "ones"], W=["pb6"])
            self.act(rkv[:], ssb[:, 0:T], AF.Ln, R=["pb6", "cst"], W=["rkv"], bias=self.eps_c, scale=1.0 / 128)
            self.act(rkv[:], rkv[:], AF.Exp, R=["rkv"], W=["rkv"], scale=-0.5)
            for j in range(4):
                self.mm(ssb[:, j:j + 1], sqkv[:, j * 128:(j + 1) * 128], self.ones_bf[:, 0:1], True, True,
                        R=["sqkv", "ones"], W=["pb6"])
            self.act(rkvc[:], ssb[:, 0:4], AF.Ln, R=["pb6", "cst"], W=["rkvc"], bias=self.eps_c, scale=1.0 / 128)
            self.act(rkvc[:], rkvc[:], AF.Exp, R=["rkvc"], W=["rkvc"], scale=-0.5)
            psA, pkA = fm(wkpe, "wkpe", 0, 96)
            psB, pkB = fm(wkrh, "wkrh", 0, 96)
            self.tt(t1[64:96, :], psA[64:96, :], cosv, ALU.mult, R=[pkA, csk], W=["t1"])
            self.tt(t2[64:96, :], psB[64:96, :], sinv, ALU.mult, R=[pkB, csk], W=["t2"])
            self.tt(st["krot"][64:96, :], t1[64:96, :], t2[64:96, :], ALU.add, R=["t1", "t2"], W=["st_krot"])
            for hh in range(4):
                self.dma(self.kTd[hh, 64:96, t0:t0 + T], st["krot"][64:96, :], R=["st_krot"], q="pool")
            for hh in range(4):
                psQ, pkQ = prot.next()
                for kc in range(2):
                    self.mm(psQ[0:96, :], wuq[:, kc, hh * 96:(hh + 1) * 96], cq_sb[:, kc, :], kc == 0, kc == 1,
                            R=["cq", "wuq"], W=[pkQ])
                psR, pkR = prot.next()
                for kc in range(2):
                    self.mm(psR[0:96, :], wuqr[:, kc, hh * 96:(hh + 1) * 96], cq_sb[:, kc, :], kc == 0, kc == 1,
                            R=["cq", "wuqr"], W=[pkR])
                self.stt(st["qd"][0:64, hh, :], psQ[0:64, :], MLA_SCALE, rq[0:64, :], ALU.mult, ALU.mult,
                         R=[pkQ, "rq"], W=["st_qd"])
                self.tt(t1[64:96, :], psQ[64:96, :], cosv, ALU.mult, R=[pkQ, csk], W=["t1"])
                self.tt(t2[64:96, :], psR[64:96, :], sinv, ALU.mult, R=[pkR, csk], W=["t2"])
                self.tt(t1[64:96, :], t1[64:96, :], t2[64:96, :], ALU.add, R=["t1", "t2"], W=["t1"])
                self.stt(st["qd"][64:96, hh, :], t1[64:96, :], MLA_SCALE, rq[64:96, :], ALU.mult, ALU.mult,
                         R=["t1", "rq"], W=["st_qd"])
            self.dma(self.qTd[:, :, t0:t0 + T].rearrange("h p t -> p h t"), st["qd"][:], R=["st_qd"], q="pool")
            for hh in range(4):
                ps, pk = prot.next()
                self.mm(ps[0:64, :], wukv[:, hh * 128:hh * 128 + 64], ckv_sb[:], True, True, R=["ckv", "wukv"], W=[pk])
                self.tt(st["kn"][:, hh, :], ps[0:64, :], rkv[0:64, :], ALU.mult, R=[pk, "rkv"], W=["st_kn"])
            self.dma(self.kTd[:, 0:64, t0:t0 + T].rearrange("h p t -> p h t"), st["kn"][:], R=["st_kn"], q="pool")
            for j in range(4):
                tok = slice(j * 128, (j + 1) * 128)
                ps, pk = prot.next()
                for kc in range(8):
                    self.mm(ps[:, 0:384], h[:, kc, tok], win[:, kc, 128:512], kc == 0, kc == 7, R=[hk, "win"], W=[pk])
                self.act(st["kv"][:, j, :], ps[:, 0:384], AF.Copy, R=[pk], W=["st_kv"])
                ps, pk = prot.next()
                for kc in range(8):
                    self.mm(ps[:, 0:256], h[:, kc, tok], win[:, kc, 1568:1824], kc == 0, kc == 7, R=[hk, "win"], W=[pk])
                self.cp(st["vc"][:, j, :, 0:64], ps[:, 0:256].rearrange("p (h c) -> p h c", c=64), R=[pk], W=["st_vc"])
                ps, pk = prot.next()
                self.mm(ps[:, 0:256], lrT[0:33, tok], wgk[0:33, :], True, True, R=["lrT", "wgk"], W=[pk])
                self.act(etmp[:], ps[:, 0:256], AF.Exp, R=[pk], W=["etmp"], scale=-1.0)
                self.act(st["sp"][:, j, :], etmp[:], AF.Ln, R=["etmp", "cst"], W=["st_sp"], bias=self.one_c)
                ps, pk = prot.next()
                self.mm(ps[:, 0:256], ckv_sb[:, tok], wukvv[:].rearrange("p h c -> p (h c)"), True, True,
                        R=["ckv", "wukvv"], W=[pk])
                self.ts(st["vd"][:, j, :, 0:64], ps[:, 0:256].rearrange("p (h c) -> p h c", c=64), rkvc[:, j:j + 1],
                        ALU.mult, R=[pk, "rkvc"], W=["st_vd"])
            tmv = lambda d: d[t0:t0 + T, :].rearrange("(j p) c -> p j c", p=128)
            self.dma(tmv(self.ktok), st["kv"][:, :, 0:128], R=["st_kv"], q="pool")
            self.dma(tmv(self.vtok), st["kv"][:, :, 128:384], R=["st_kv"], q="pool")
            self.dma(tmv(self.vtokc), st["vc"][:].rearrange("p j h c -> p j (h c)"), R=["st_vc"], q="pool")
            self.dma(tmv(self.sptok), st["sp"][:], R=["st_sp"], q="pool")
            self.dma(tmv(self.vtokd), st["vd"][:].rearrange("p j h c -> p j (h c)"), R=["st_vd"], q="pool")
            del state[i]

        stage_load(0)
        stage_norm(0)
        for i in range(ntiles):
            if i + 1 < ntiles:
                stage_load(i + 1)
                pending.extend(norm_steps(i + 1))
            stage_body(i)
            while pending:
                poll()
        S.phase_barrier(self.bar[0:1, 0:8], self.cst[0:1, 0:8])


    def pool_mixer(self, l, s):
        Sq = self.SEQS[s]
        o = self.off[s]
        TP = 1024 if Sq >= 1024 else Sq
        L = TP + 16
        self.sbp = self.persist_end
        vec = self.vec_sb[:, l * VEC_W:(l + 1) * VEC_W]
        pw_f = self.sb([128, 2, 128], F32, "pwf")
        pw = self.sb([128, 2, 128], BF16, "pw")
        self.dma(pw_f[:], self.poolw[l].rearrange("c p q -> p c q"), W=["pwf"])
        self.cp(pw[:], pw_f[:], R=["pwf"], W=["pw"])
        urot = self.rot("u", 2, [128, 2, L], F32)
        icrot = self.rot("ic", 2, [128, 2, TP], F32)
        P1 = self.sb([128, 2, L], F32, "P1")
        Q = self.sb([128, 2, L], F32, "Q")
        Rr = self.sb([128, L], F32, "R")
        Tt = self.sb([128, L], F32, "Tt")
        dd = self.sb([128, 2, TP], F32, "dd")
        db = self.sb([128, 2, TP], BF16, "db")
        yo = self.rot("yo", 2, [128, 2, TP], BF16)
        prot = self.pbrot(range(4))
        usrc = self.uT.rearrange("(c p) t -> p c t", p=128)
        icsrc = self.icnt[Sq].rearrange("(c p) t -> p c t", p=128)
        ydst = self.ymix[4:8].rearrange("(c two) f t -> (two f) c t", two=2)
        for it in range(Sq // TP):
            p0 = it * TP
            u, uk = urot.next()
            ic, ick = icrot.next()
            lo = max(p0 - 8, 0)
            hi = min(p0 + TP + 8, Sq)
            if lo > p0 - 8:
                self.memset(u[:, :, 0:8], 0.0, W=[uk])
            if hi < p0 + TP + 8:
                self.memset(u[:, :, L - 8:L], 0.0, W=[uk])
            self.dma(u[:, :, lo - (p0 - 8):hi - (p0 - 8)], usrc[:, :, o + lo:o + hi], W=[uk])
            self.dma(ic[:], icsrc[:, :, p0:p0 + TP], W=[ick])
            self.tt(P1[:, :, 1:L], u[:, :, 0:L - 1], u[:, :, 1:L], ALU.add, R=[uk], W=["P1"])
            self.tt(Q[:, :, 2:L - 2], P1[:, :, 1:L - 3], P1[:, :, 3:L - 1], ALU.add, R=["P1"], W=["Q"])
            self.tt(Rr[:, 4:L - 4], Q[:, 1, 2:L - 6], Q[:, 1, 6:L - 2], ALU.add, R=["Q"], W=["R"])
            self.tt(Tt[64:128, 8:L - 8], Rr[64:128, 4:L - 12], Rr[64:128, 12:L - 4], ALU.add, R=["R"], W=["Tt"])
            c8 = slice(8, 8 + TP)
            self.tt(dd[0:64, 0, :], P1[0:64, 0, c8], ic[0:64, 0, :], ALU.mult, R=["P1", ick], W=["dd"])
            self.tt(dd[64:128, 0, :], Q[64:128, 0, c8], ic[64:128, 0, :], ALU.mult, R=["Q", ick], W=["dd"])
            self.tt(dd[0:64, 1, :], Rr[0:64, c8], ic[0:64, 1, :], ALU.mult, R=["R", ick], W=["dd"])
            self.tt(dd[64:128, 1, :], Tt[64:128, c8], ic[64:128, 1, :], ALU.mult, R=["Tt", ick], W=["dd"])
            self.tt(db[:], dd[:], u[:, :, c8], ALU.subtract, R=["dd", uk], W=["db"])
            y, yk = yo.next()
            for c in range(2):
                for q in range(TP // 512):
                    ps, pk = prot.next()
                    self.mm(ps[:], pw[:, c, :], db[:, c, q * 512:(q + 1) * 512], True, True, R=["db", "pw"], W=[pk])
                    self.act(y[:, c, q * 512:(q + 1) * 512], ps[:], AF.Copy, R=[pk, "vec"], W=[yk],
                             scale=vec[:, 3 + c:4 + c])
            self.dma(ydst[:, :, o + p0:o + p0 + TP], y[:], R=[yk], q="pool")
        self.S.phase_barrier(self.bar[0:1, 0:8], self.cst[0:1, 0:8])

    def mla_mixer(self, l, s):
        Sq = self.SEQS[s]
        o = self.off[s]
        NK = Sq // 128
        self.sbp = self.persist_end
        KT = self.sb([96, 4, Sq], BF16, "KT")
        self.dma(KT[:], self.kTd[:, :, o:o + Sq].rearrange("h p t -> p h t"), W=["KT"])
        Vp = self.sb([128, NK, 4, 65], BF16, "Vp")
        self.dma(Vp[:].rearrange("p n h c -> p n (h c)"), self.vtokd[o:o + Sq, :].rearrange("(n p) c -> p n c", p=128),
                 W=["Vp"])
        qrot = self.rot("QT", 2, [96, 4, 512], BF16)
        prot_p = self.rot("pT", 3, [128, 512], BF16)
        rec = self.sb([65, 512], F32, "rec")
        bcs = self.sb([64, 512], F32, "bcs")
        yrot = self.rot("y", 2, [64, 4, 512], BF16)
        srot = self.pbrot([0, 1, 2])
        orot = self.pbrot([3, 4])
        bcp = self.pb[5]
        NQ = Sq // 512
        qts = {}

        def load_q(qt):
            QT, qk = qrot.next()
            self.dma(QT[:], self.qTd[:, :, o + qt * 512:o + (qt + 1) * 512].rearrange("h p t -> p h t"), W=[qk])
            qts[qt] = (QT, qk)

        items = [(qt, h, kc) for qt in range(NQ) for h in range(4) for kc in range(NK)]
        load_q(0)
        pend = []

        def qk_mm(it):
            qt, h, kc = it
            QT, qk = qts[qt]
            sp_, sk = srot.next()
            self.mm(sp_[:], KT[:, h, kc * 128:(kc + 1) * 128], QT[:, h, :], True, True, R=["KT", qk], W=[sk])
            pend.append((sp_, sk))

        y = yk = None
        cur_o = None
        for n, (qt, h, kc) in enumerate(items):
            if h == 0 and kc == 0:
                y, yk = yrot.next()
                if qt + 1 < NQ:
                    load_q(qt + 1)
            if n == 0:
                qk_mm(items[0])
                if len(items) > 1:
                    qk_mm(items[1])
            if kc == 0:
                cur_o = orot.next()
            ops_, ok = cur_o
            if n + 2 < len(items):
                qk_mm(items[n + 2])
            sp_, sk = pend.pop(0)
            pT, pk = prot_p.next()
            self.act(pT[:], sp_[:], AF.Exp, R=[sk], W=[pk])
            self.mm(ops_[0:65, :], Vp[:, kc, h, :], pT[:], kc == 0, kc == NK - 1, R=["Vp", pk], W=[ok])
            if kc == NK - 1:
                self.act(rec[64:65, :], ops_[64:65, :], AF.Ln, R=[ok], W=["rec"])
                self.act(rec[64:65, :], rec[64:65, :], AF.Exp, R=["rec"], W=["rec"], scale=-1.0)
                self.mm(bcp[0:64, :], self.ones_f[64:65, 0:64], rec[64:65, :], True, True, R=["rec", "onesf"], W=["pb5"])
                self.cp(bcs[:], bcp[0:64, :], R=["pb5"], W=["bcs"])
                self.tt(y[:, h, :], ops_[0:64, :], bcs[:], ALU.mult, R=[ok, "bcs"], W=[yk])
                if h == 3:
                    self.dma(self.ymix[12:16, :, o + qt * 512:o + (qt + 1) * 512].rearrange("h p t -> p h t"), y[:],
                             R=[yk], q="pool")
        self.S.phase_barrier(self.bar[0:1, 0:8], self.cst[0:1, 0:8])

    def na_mixer(self, l, s):
        Sq = self.SEQS[s]
        o = self.off[s]
        P = Sq // 128
        self.sbp = self.persist_end
        KT = self.sb([64, 4, Sq], BF16, "KT")
        self.dma(KT[:], self.kTc[:, :, o:o + Sq].rearrange("h p t -> p h t"), W=["KT"])
        Vp = self.sb([128, P, 4, 65], BF16, "Vp")
        self.dma(Vp[:].rearrange("p n h c -> p n (h c)"), self.vtokc[o:o + Sq, :].rearrange("(n p) c -> p n c", p=128),
                 W=["Vp"])
        tab = self.sb([128, 4, 21, 128], BF16, "tab")
        for h in range(4):
            self.dma(tab[:, h], self.natab[l, h].rearrange("v k q -> k v q"), W=["tab"], q="pool")
        sT_r = self.rot("sT", 4, [128, 640], F32)
        qrot = self.rot("QT", 2, [64, 4, 512], BF16)
        prot_p = self.rot("pT", 3, [128, 640], BF16)
        rec = self.sb([65, 512], F32, "rec")
        bcs = self.sb([64, 512], F32, "bcs")
        yrot = self.rot("y", 2, [64, 4, 512], BF16)
        arot = self.pbrot([0, 1, 2])
        brot = self.pbrot([3, 7])
        orot = self.pbrot([4, 5])
        bcp = self.pb[6]

        def chunks(i):
            if P >= 5 and 2 <= i <= P - 3:
                return [(i - 2 + d, d) for d in range(5)]
            if i == 0:
                return [(j, 5 + j) for j in range(4)]
            if i == 1:
                return [(j, 9 + j) for j in range(4)]
            if i == P - 2:
                return [(P - 4 + j, 13 + j) for j in range(4)]
            assert i == P - 1
            return [(P - 4 + j, 17 + j) for j in range(4)]

        NG = Sq // 512
        qts = {}

        def load_q(g):
            QT, qk = qrot.next()
            self.dma(QT[:], self.qTc[:, :, o + g * 512:o + (g + 1) * 512].rearrange("h p t -> p h t"), W=[qk])
            qts[g] = (QT, qk)

        items = [(g, ip, h) for g in range(NG) for ip in range(4) for h in range(4)]
        pend = []

        def stage1(it):
            g, ip, h = it
            QT, qk = qts[g]
            ch = chunks(g * 4 + ip)
            pa, pak = arot.next()
            pb_, pbk = (None, None)
            if len(ch) > 4:
                pb_, pbk = brot.next()
            for ci, (j, tix) in enumerate(ch):
                dst = pa[:, ci * 128:(ci + 1) * 128] if ci < 4 else pb_[:, 0:128]
                dk = pak if ci < 4 else pbk
                self.mm(dst, KT[:, h, j * 128:(j + 1) * 128], QT[:, h, ip * 128:(ip + 1) * 128], True, True,
                        R=["KT", qk], W=[dk])
            sT, stk = sT_r.next()
            t0x = ch[0][1]
            n4 = min(len(ch), 4)
            self.tt(sT[:, 0:n4 * 128].rearrange("p (c q) -> p c q", c=n4), pa[:, 0:n4 * 128].rearrange("p (c q) -> p c q", c=n4),
                    tab[:, h, t0x:t0x + n4, :], ALU.add, R=[pak, "tab"], W=[stk])
            if len(ch) > 4:
                self.tt(sT[:, 512:640], pb_[:, 0:128], tab[:, h, t0x + 4, :], ALU.add, R=[pbk, "tab"], W=[stk])
            pend.append((ch, sT, stk))

        y = yk = None
        cur_o = None
        for n, (g, ip, h) in enumerate(items):
            if ip == 0 and h == 0:
                y, yk = yrot.next()
                if n == 0:
                    load_q(0)
                    stage1(items[0])
                    if len(items) > 1:
                        stage1(items[1])
                if g + 1 < NG:
                    load_q(g + 1)
            if h == 0:
                cur_o = orot.next()
            ops_, ok = cur_o
            if n + 2 < len(items):
                stage1(items[n + 2])
            ch, sT, stk = pend.pop(0)
            pT, pk = prot_p.next()
            self.act(pT[:, 0:len(ch) * 128], sT[:, 0:len(ch) * 128], AF.Exp, R=[stk], W=[pk])
            for ci, (j, tix) in enumerate(ch):
                self.mm(ops_[0:65, h * 128:(h + 1) * 128], Vp[:, j, h, :], pT[:, ci * 128:(ci + 1) * 128],
                        ci == 0, ci == len(ch) - 1, R=["Vp", pk], W=[ok])
            if h == 3:
                self.act(rec[64:65, :], ops_[64:65, :], AF.Ln, R=[ok], W=["rec"])
                self.act(rec[64:65, :], rec[64:65, :], AF.Exp, R=["rec"], W=["rec"], scale=-1.0)
                self.mm(bcp[0:64, :], self.ones_f[64:65, 0:64], rec[64:65, :], True, True, R=["rec", "onesf"], W=["pb6"])
                self.cp(bcs[:], bcp[0:64, :], R=["pb6"], W=["bcs"])
                self.tt(y[:, :, ip * 128:(ip + 1) * 128], ops_[0:64, :].rearrange("p (h t) -> p h t", h=4),
                        bcs[:].rearrange("p (h t) -> p h t", h=4), ALU.mult, R=[ok, "bcs"], W=[yk])
                if ip == 3:
                    self.dma(self.ymix[8:12, :, o + g * 512:o + (g + 1) * 512].rearrange("h p t -> p h t"), y[:],
                             R=[yk], q="pool")
        self.S.phase_barrier(self.bar[0:1, 0:8], self.cst[0:1, 0:8])


    def gla_mixer(self, l, s):
        Sq = self.SEQS[s]
        o = self.off[s]
        NCH = Sq // 128
        NSC = Sq // 512
        self.sbp = self.persist_end
        cst = self.cst_sb
        vec = self.vec_sb[:, l * VEC_W:(l + 1) * VEC_W]
        triu = cst[:, C_TRIU:C_TRIU + 128]
        tril = cst[:, C_TRIL:C_TRIL + 128]
        nsix = cst[:, C_NSIX:C_NSIX + 1]
        bmask = cst[:, C_BM:C_BM + 256]
        maskF = cst[:, C_MF:C_MF + 128]
        maskB = cst[:, C_MB:C_MB + 128]
        Sf_all = self.sb([128, NCH, 256], BF16, "Sfall")
        Sf = self.sb([128, 256], F32, "Sf")
        Sb = self.sb([128, 256], F32, "Sb")
        Sb_bf = self.sb([128, 256], BF16, "Sbbf")
        tmp = self.sb([128, 256], F32, "tmp")
        ebl = self.sb([128, 1], F32, "ebl")
        einv_tok = self.sb([128, 128], F32, "einvtok")
        ktil = self.sb([128, 128], BF16, "ktil")
        ktok_r = self.rot("ktok", 3, [128, 4, 128], BF16)
        vtok_r = self.rot("vtok", 3, [128, 4, 256], BF16)
        sp_r = self.rot("sp", 3, [128, 4, 256], F32)
        q_r = self.rot("q4", 3, [128, 512], BF16)
        k_r = self.rot("k4", 3, [128, 512], BF16)
        go_r = self.rot("go4", 3, [64, 4, 512], BF16)
        y_r = self.rot("y4", 2, [64, 4, 512], BF16)
        E = self.sb([128, 256], F32, "E")
        EI = self.sb([128, 384], F32, "EI")
        ecf = self.sb([128, 1], F32, "ecf")
        Qf_r = self.rot("Qf", 2, [128, 4, 128], BF16)
        Qb_r = self.rot("Qb", 2, [128, 4, 128], BF16)
        Kf = self.sb([128, 128], BF16, "Kf")
        Kb = self.sb([128, 128], BF16, "Kb")
        ktilb = self.sb([128, 128], BF16, "ktilb")
        Amf = self.sb([128, 4, 128], BF16, "Amf")
        Amb = self.sb([128, 4, 128], BF16, "Amb")
        osq = self.sb([64, 512], BF16, "osq")
        rr = self.sb([64, 512], F32, "rr")
        tt_ = self.sb([64, 512], F32, "tt")
        ktsrc = self.ktok[o:o + Sq, :].rearrange("(n p) c -> p n c", p=128)
        vtsrc = self.vtok[o:o + Sq, :].rearrange("(n p) c -> p n c", p=128)
        spsrc = self.sptok[o:o + Sq, :].rearrange("(n p) c -> p n c", p=128)

        def load_tok(sc, want_qk):
            d = {}
            kt, d["ktk"] = ktok_r.next()
            vt, d["vtk"] = vtok_r.next()
            sp, d["spk"] = sp_r.next()
            self.dma(kt[:], ktsrc[:, sc * 4:(sc + 1) * 4, :], W=[d["ktk"]])
            self.dma(vt[:], vtsrc[:, sc * 4:(sc + 1) * 4, :], W=[d["vtk"]])
            self.dma(sp[:], spsrc[:, sc * 4:(sc + 1) * 4, :], W=[d["spk"]])
            d.update(kt=kt, vt=vt, sp=sp)
            if want_qk:
                q4, d["qk"] = q_r.next()
                k4, d["kk"] = k_r.next()
                go, d["gok"] = go_r.next()
                tsl = slice(o + sc * 512, o + (sc + 1) * 512)
                self.dma(q4[:], self.qTa[:, tsl], W=[d["qk"]])
                self.dma(k4[:], self.kTa[:, tsl], W=[d["kk"]])
                self.dma(go[:], self.goT[:, :, tsl].rearrange("h p t -> p h t"), W=[d["gok"]])
                d.update(q4=q4, k4=k4, go=go)
            return d

        self.memset(Sf[:], 0.0, W=["Sf"])
        self.memset(Sf_all[:, 0, :], 0.0, W=["Sfall"])
        yb = self.pbrot([0, 1])
        kvb = self.pbrot([2, 3])
        ebl_r = self.rot("ebl1", 2, [128, 1], F32)
        einv_r = self.rot("einv1", 2, [128, 128], F32)
        ktil_r = self.rot("ktil1", 2, [128, 128], BF16)
        toks1 = {}

        def get_tok1(sc):
            if sc not in toks1:
                toks1[sc] = load_tok(sc, False)
            return toks1[sc]

        def prep1(n):
            sc, c = n // 4, n % 4
            cur = get_tok1(sc)
            if c == 1 and sc + 1 < NSC:
                get_tok1(sc + 1)
            spf = cur["sp"][:, c, 0:128]
            Y, yk = yb.next()
            self.mm(Y[:, 0:128], triu, spf, True, True, R=["cst", cur["spk"]], W=[yk])
            self.mm(Y[:, 128:129], spf, nsix, True, True, R=["cst", cur["spk"]], W=[yk])
            einv, eik = einv_r.next()
            ebl1, eblk = ebl_r.next()
            ktil1, ktk = ktil_r.next()
            self.act(einv[:], Y[:, 0:128], AF.Exp, R=[yk], W=[eik], scale=-1.0)
            self.act(ebl1[:], Y[:, 128:129], AF.Exp, R=[yk], W=[eblk])
            self.tt(ktil1[:], cur["kt"][:, c, :], einv[:], ALU.mult, R=[cur["ktk"], eik], W=[ktk], eng="pool")
            KV, kvk = kvb.next()
            self.mm(KV[:, 0:256], ktil1[:], cur["vt"][:, c, :], True, True, R=[ktk, cur["vtk"]], W=[kvk])
            return KV, kvk, ebl1, eblk

        if NCH > 1:
            pn = prep1(0)
            for n in range(NCH - 1):
                pnext = prep1(n + 1) if n + 1 < NCH - 1 else None
                KV, kvk, ebl1, eblk = pn
                self.tt(tmp[:], KV[:, 0:256], Sf[:], ALU.add, R=[kvk, "Sf"], W=["tmp"])
                self.stt(Sf[:], tmp[:], ebl1[:, 0:1], bmask, ALU.mult, ALU.mult, R=["tmp", eblk, "cst"], W=["Sf"])
                self.cp(Sf_all[:, n + 1, :], Sf[:], R=["Sf"], W=["Sfall"], eng="pool")
                pn = pnext
        self.memset(Sb[:], 0.0, W=["Sb"])
        self.memset(Sb_bf[:], 0.0, W=["Sbbf"])
        xb = self.pbrot([0, 1])
        ob = self.pbrot([2, 3])
        afp, abp, kvp, ssp = self.pb[4], self.pb[5], self.pb[6], self.pb[7]
        E_r = self.rot("E2", 2, [128, 256], F32)
        EI_r = self.rot("EI2", 2, [128, 384], F32)
        ecf_r = self.rot("ecf2", 2, [128, 1], F32)
        Kf_r = self.rot("Kf2", 2, [128, 128], BF16)
        Kb_r = self.rot("Kb2", 2, [128, 128], BF16)
        ktb_r = self.rot("ktb2", 2, [128, 128], BF16)
        Amf_r = self.rot("Amf2", 2, [128, 4, 128], BF16)
        Amb_r = self.rot("Amb2", 2, [128, 4, 128], BF16)
        toks = {}
        ys = {}

        def get_tok(sc):
            if sc not in toks:
                toks[sc] = load_tok(sc, True)
            return toks[sc]

        def stage_P(n):
            sc, c = n // 4, n % 4
            cur = get_tok(sc)
            csl = slice(c * 128, (c + 1) * 128)
            spf = cur["sp"][:, c, 0:128]
            spb = cur["sp"][:, c, 128:256]
            X, xk = xb.next()
            R0 = ["cst", cur["spk"]]
            self.mm(X[:, 0:128], spf, triu, True, True, R=R0, W=[xk])
            self.mm(X[:, 128:256], spb, tril, True, True, R=R0, W=[xk])
            self.mm(X[:, 256:384], tril, spb, True, True, R=R0, W=[xk])
            self.mm(X[:, 384:385], spb, nsix, True, True, R=R0, W=[xk])
            E, ek = E_r.next()
            EI, eik = EI_r.next()
            ecf, ecfk = ecf_r.next()
            self.act(E[:], X[:, 0:256], AF.Exp, R=[xk], W=[ek])
            self.act(EI[:], X[:, 0:384], AF.Exp, R=[xk], W=[eik], scale=-1.0)
            self.act(ecf[:], X[:, 384:385], AF.Exp, R=[xk], W=[ecfk])
            Qf, qfk = Qf_r.next()
            Qb, qbk = Qb_r.next()
            for h in range(4):
                hm = cst[:, C_HM + h:C_HM + h + 1]
                self.stt(Qf[:, h, :], cur["q4"][:, csl], hm, E[:, 0:128], ALU.mult, ALU.mult,
                         R=[cur["qk"], ek, "cst"], W=[qfk])
                self.stt(Qb[:, h, :], cur["q4"][:, csl], hm, E[:, 128:256], ALU.mult, ALU.mult,
                         R=[cur["qk"], ek, "cst"], W=[qbk])
            Kf, kfk = Kf_r.next()
            Kb, kbk = Kb_r.next()
            ktilb, ktbk = ktb_r.next()
            self.tt(Kf[:], cur["k4"][:, csl], EI[:, 0:128], ALU.mult, R=[cur["kk"], eik], W=[kfk], eng="pool")
            self.tt(Kb[:], cur["k4"][:, csl], EI[:, 128:256], ALU.mult, R=[cur["kk"], eik], W=[kbk], eng="pool")
            self.tt(ktilb[:], cur["kt"][:, c, :], EI[:, 256:384], ALU.mult, R=[cur["ktk"], eik], W=[ktbk], eng="pool")
            for h in range(4):
                self.mm(afp[:, h * 128:(h + 1) * 128], Kf[:], Qf[:, h, :], True, True, R=[kfk, qfk], W=["pb4"])
            for h in range(4):
                self.mm(abp[:, h * 128:(h + 1) * 128], Kb[:], Qb[:, h, :], True, True, R=[kbk, qbk], W=["pb5"])
            Amf, amfk = Amf_r.next()
            Amb, ambk = Amb_r.next()
            self.tt(Amf[:], afp[:].rearrange("p (h t) -> p h t", h=4), maskF.unsqueeze(1).to_broadcast([128, 4, 128]),
                    ALU.mult, R=["pb4", "cst"], W=[amfk])
            self.tt(Amb[:], abp[:].rearrange("p (h t) -> p h t", h=4), maskB.unsqueeze(1).to_broadcast([128, 4, 128]),
                    ALU.mult, R=["pb5", "cst"], W=[ambk])
            return dict(cur=cur, c=c, sc=sc, csl=csl, ecf=ecf, ecfk=ecfk, Qf=Qf, qfk=qfk, Qb=Qb, qbk=qbk,
                        ktilb=ktilb, ktbk=ktbk, Amf=Amf, amfk=amfk, Amb=Amb, ambk=ambk)

        def stage_Q(n, P):
            cur, c, sc, csl = P["cur"], P["c"], P["sc"], P["csl"]
            if sc not in ys:
                ys[sc] = y_r.next()
            y4, y4k = ys[sc]
            O, ok = ob.next()
            for h in range(4):
                od = O[0:64, h * 128:(h + 1) * 128]
                vh = cur["vt"][:, c, h * 64:(h + 1) * 64]
                self.mm(od, vh, P["Amf"][:, h, :], True, False, R=[cur["vtk"], P["amfk"]], W=[ok])
                self.mm(od, vh, P["Amb"][:, h, :], False, False, R=[cur["vtk"], P["ambk"]], W=[ok])
                self.mm(od, Sf_all[:, n, h * 64:(h + 1) * 64], P["Qf"][:, h, :], False, False, R=["Sfall", P["qfk"]], W=[ok])
                self.mm(od, Sb_bf[:, h * 64:(h + 1) * 64], P["Qb"][:, h, :], False, True, R=["Sbbf", P["qbk"]], W=[ok])
            self.mm(kvp[:, 0:256], P["ktilb"][:], cur["vt"][:, c, :], True, True, R=[P["ktbk"], cur["vtk"]], W=["pb6"])
            self.tt(tmp[:], kvp[:, 0:256], Sb[:], ALU.add, R=["pb6", "Sb"], W=["tmp"])
            self.stt(Sb[:], tmp[:], P["ecf"][:, 0:1], bmask, ALU.mult, ALU.mult, R=["tmp", P["ecfk"], "cst"], W=["Sb"])
            self.cp(Sb_bf[:], Sb[:], R=["Sb"], W=["Sbbf"], eng="pool")
            self.act(osq[:], O[0:64, :], AF.Square, R=[ok], W=["osq"])
            self.mm(ssp[0:64, :], self.ones_bf[0:64, 0:64], osq[:], True, True, R=["osq", "ones"], W=["pb7"])
            self.act(rr[:], ssp[0:64, :], AF.Ln, R=["pb7", "cst"], W=["rr"], bias=self.eps_c[0:64, :], scale=1.0 / 64)
            self.act(rr[:], rr[:], AF.Exp, R=["rr"], W=["rr"], scale=-0.5)
            self.tt(tt_[:], O[0:64, :], rr[:], ALU.mult, R=[ok, "rr"], W=["tt"])
            self.stt(y4[:, :, csl], tt_[:].rearrange("p (h t) -> p h t", h=4), vec[0:64, 5:6], cur["go"][:, :, csl],
                     ALU.mult, ALU.mult, R=["tt", "vec", cur["gok"]], W=[y4k])
            if c == 0:
                self.dma(self.ymix[0:4, :, o + sc * 512:o + (sc + 1) * 512].rearrange("h p t -> p h t"), y4[:], R=[y4k],
                         q="pool")
                del toks[sc]

        Pn = stage_P(NCH - 1)
        for n in range(NCH - 1, -1, -1):
            Pnext = None
            if n - 1 >= 0:
                if (n - 1) % 4 == 1 and (n - 1) // 4 - 1 >= 0:
                    get_tok((n - 1) // 4 - 1)
                Pnext = stage_P(n - 1)
            stage_Q(n, Pn)
            Pn = Pnext
        self.S.phase_barrier(self.bar[0:1, 0:8], self.cst[0:1, 0:8])


C_EPS = 0
C_NSIX = 1
C_ONE = 2
C_HM = 4
C_BM = 8
C_TRIU = 264
C_TRIL = 392
C_MF = 520
C_MB = 648
C_ID = 776
CST_W = 904
VEC_W = 8
POOL_WINDOWS = (2, 4, 8, 16)


def host_consts():
    c = np.zeros((128, CST_W), np.float32)
    c[:, C_EPS] = EPS
    c[:, C_NSIX] = -1.0 / 16.0
    c[:, C_ONE] = 1.0
    p = np.arange(128)
    for h in range(4):
        c[:, C_HM + h] = (p // 32 == h) * (32.0 ** -0.5)
    f = np.arange(256)
    c[:, C_BM:C_BM + 256] = (f[None, :] // 64 == p[:, None] // 32)
    j = p[:, None]
    i = p[None, :]
    c[:, C_TRIU:C_TRIU + 128] = (j <= i) * (-1.0 / 16.0)
    c[:, C_TRIL:C_TRIL + 128] = (j >= i) * (-1.0 / 16.0)
    c[:, C_MF:C_MF + 128] = (j <= i)
    c[:, C_MB:C_MB + 128] = (j > i)
    c[:, C_ID:C_ID + 128] = (j == i)
    return c


def host_rope(smax):
    half = 16
    inv = (10000.0 ** (-np.arange(half, dtype=np.float32) / half)).astype(np.float32)
    pos = np.arange(smax, dtype=np.float32)
    ang = (pos[None, :] * np.concatenate([inv, inv])[:, None]).astype(np.float32)
    return np.ascontiguousarray(np.stack([np.cos(ang), np.sin(ang)], axis=1).astype(np.float32))


def host_icnt(S):
    t = np.arange(S)
    out = np.zeros((256, S), np.float32)
    for gi, w in enumerate(POOL_WINDOWS):
        lo = np.clip(t - w // 2, 0, S)
        hi = np.clip(t + w // 2, 0, S)
        out[gi * 64:(gi + 1) * 64, :] = (1.0 / (hi - lo).astype(np.float32))[None, :]
    return out


def build(nc, SEQS, dbg=False, stages="ABCD", nlayers=NL, mixers="pdca"):
    b = Builder(nc, SEQS, dbg)
    b.declare()
    b.prologue()
    for l in range(nlayers):
        last = (l == nlayers - 1)
        if "A" in stages:
            b.ffn_sweep(l, 0, b.xT if l == 0 else b.xs, b.xs if "D" in stages or True else b.yT)
        if "B" in stages:
            b.proj_sweep(l)
        if "C" in stages:
            for sq in range(b.NS):
                if "p" in mixers:
                    b.pool_mixer(l, sq)
                if "d" in mixers:
                    b.mla_mixer(l, sq)
                if "c" in mixers:
                    b.na_mixer(l, sq)
                if "a" in mixers:
                    b.gla_mixer(l, sq)
        if "D" in stages:
            b.ffn_sweep(l, 1, b.xs, b.yT if last else b.xs, pre_wout=("C" in stages), final_norm=last, final=last)
    b.S.emit()
    return b


def host_shared(w, seqs):
    m = {}
    m["ada_w"] = np.ascontiguousarray(w["ada_w"], np.float32)
    m["adabT"] = np.ascontiguousarray(w["ada_b"].reshape(NL, 72, 128).transpose(0, 2, 1), np.float32)
    nrm = np.zeros((128, (NL * 3 + 1) * 8), np.float32)
    for l in range(NL):
        for k, name in enumerate(("norm_ffn1", "norm_mix", "norm_ffn2")):
            nrm[:, (l * 3 + k) * 8:(l * 3 + k + 1) * 8] = w[name][l].reshape(8, 128).T
    nrm[:, NL * 3 * 8:] = w["final_norm"].reshape(8, 128).T
    m["nrm"] = nrm
    for k in ("ffn1_wg", "ffn1_wu", "ffn1_wd", "ffn2_wg", "ffn2_wu", "ffn2_wd", "w_in", "w_out"):
        m[k] = np.ascontiguousarray(w[k], np.float32)
    m["cst"] = host_consts()
    vec = np.zeros((128, NL * VEC_W), np.float32)
    wgk = np.zeros((NL, 33, 256), np.float32)
    poolw = np.zeros((NL, 2, 128, 128), np.float32)
    for l in range(NL):
        vec[:, l * VEC_W + 0:l * VEC_W + 2] = w["mla_qnorm"][l].reshape(2, 128).T
        vec[:, l * VEC_W + 2] = w["mla_kvnorm"][l]
        vec[:, l * VEC_W + 3:l * VEC_W + 5] = w["pool_scale"][l].reshape(2, 128).T
        vec[0:64, l * VEC_W + 5] = w["gla_norm"][l]
        vec[64:128, l * VEC_W + 5] = w["gla_norm"][l]
        wgk[l, 0:16, 0:128] = w["gla_wgk_f"][l]
        wgk[l, 16:32, 128:256] = w["gla_wgk_b"][l]
        wgk[l, 32, 0:128] = w["gla_bgk_f"][l]
        wgk[l, 32, 128:256] = w["gla_bgk_b"][l]
        for c in range(2):
            poolw[l, c, 0:64, 0:64] = w["pool_w"][l, 2 * c]
            poolw[l, c, 64:128, 64:128] = w["pool_w"][l, 2 * c + 1]
    m["vec"] = vec
    m["wgk"] = wgk
    m["poolw"] = poolw
    m["mla_wuq"] = np.ascontiguousarray(w["mla_wuq"], np.float32)
    m["mla_wukv"] = np.ascontiguousarray(w["mla_wukv"], np.float32)
    m["natab"] = host_natab(w["na_rpb"], seqs)
    m["rope"] = host_rope(max(seqs))
    for Sq in sorted(set(seqs)):
        m["icnt%d" % Sq] = host_icnt(Sq)
    return m


def host_natab(rpb, seqs):
    Rr = 16
    P = Rr // 2
    variants = [(3, [1, 2, 3, 4, 5]), (0, [0, 1, 2, 3]), (1, [0, 1, 2, 3]), (P - 2, [P - 4, P - 3, P - 2, P - 1]),
                (P - 1, [P - 4, P - 3, P - 2, P - 1])]
    out = np.full((NL, 4, 21, 128, 128), NEG, np.float32)
    cq = np.arange(64)
    ck = np.arange(64)
    qstart = np.clip(cq - 8, 0, 48)
    col_ok = (ck[None, :] >= qstart[:, None]) & (ck[None, :] < qstart[:, None] + 16)
    dcol = ck[None, :] - cq[:, None] + 15
    t = 0
    for (i, js) in variants:
        for j in js:
            for rq in range(2):
                qrow = 2 * i + rq
                r0 = min(max(qrow - 4, 0), Rr - 8)
                for rk in range(2):
                    krow = 2 * j + rk
                    if not (r0 <= krow < r0 + 8):
                        continue
                    drow = krow - qrow + 7
                    blk = np.where(col_ok[None, None], rpb[:, :, drow, :][:, :, np.clip(dcol, 0, 30)], NEG)
                    out[:, :, t, rk * 64:(rk + 1) * 64, rq * 64:(rq + 1) * 64] = np.swapaxes(blk, -1, -2)
            t += 1
    assert t == 21
    return out


def host_core(xs, cs):
    xT = np.ascontiguousarray(np.concatenate(xs, axis=0).T, np.float32)
    NS = len(xs)
    cT = np.ascontiguousarray(np.asarray(cs, np.float32).T.reshape(8, 128, NS).transpose(1, 0, 2))
    return {"xT": xT, "cT": cT}


N_CORES = 8
SEQS_FULL = [2048, 2048, 2048, 2048, 8192]
_WNAMES = ("ada_w", "ada_b", "norm_ffn1", "ffn1_wg", "ffn1_wu", "ffn1_wd", "norm_mix", "w_in", "gla_wgk_f",
           "gla_bgk_f", "gla_wgk_b", "gla_bgk_b", "gla_norm", "pool_w", "pool_scale", "na_rpb", "mla_qnorm",
           "mla_wuq", "mla_kvnorm", "mla_wukv", "w_out", "norm_ffn2", "ffn2_wg", "ffn2_wu", "ffn2_wd", "final_norm")


def kernel(x_prompt, x_sample, c_prompt, c_sample, **weights):
    w = {k: np.asarray(weights[k], np.float32) for k in _WNAMES}
    x_prompt = np.asarray(x_prompt, np.float32)
    x_sample = np.asarray(x_sample, np.float32)
    c_prompt = np.asarray(c_prompt, np.float32)
    c_sample = np.asarray(c_sample, np.float32)
    nc = bass.Bass("TRN2", target_bir_lowering=False)
    b = build(nc, SEQS_FULL)
    shared = host_shared(w, SEQS_FULL)
    shared = {k: v for k, v in shared.items() if k in b.dram}
    in_maps = []
    for c in range(N_CORES):
        xs = [x_prompt[4 * c + i] for i in range(4)] + [x_sample[c]]
        cs = np.concatenate([c_prompt[4 * c:4 * c + 4], c_sample[c:c + 1]], axis=0)
        m = dict(shared)
        m.update(host_core(xs, cs))
        in_maps.append(m)
    res = run_bass_kernel_spmd(nc, in_maps, core_ids=list(range(N_CORES)))
    y_prompt = np.empty((32, 2048, D), np.float32)
    y_sample = np.empty((8, 8192, D), np.float32)
    for c in range(N_CORES):
        yT = np.asarray(res.results[c]["yT"], np.float32)
        for i in range(4):
            y_prompt[4 * c + i] = yT[:, i * 2048:(i + 1) * 2048].T
        y_sample[c] = yT[:, 8192:].T
    return (y_prompt, y_sample)
```
